# Optimizing a Trainium2 kernel written in Bass

```python
import math
import jax
import jax.numpy as jnp
from jax import lax
import numpy as np

D_MODEL = 2048
BATCH = 8
SEQ = 2048
DEPTH = 1
DEC_BATCH = 128
DEC_SEQ = 4
PAST_LEN = 2048
PAGE_SIZE = 128

R_HEADS = 16
R_HEAD_DIM = 64
R_WIDTH = R_HEADS * R_HEAD_DIM
DECAY_LORA = 96
ICLR_LORA = 96
RW_IN = 4 * R_WIDTH + DECAY_LORA + ICLR_LORA

A_HEADS = 8
A_HEAD_DIM = 64
A_VDIM = 2 * A_HEAD_DIM
A_WIDTH = A_HEADS * A_VDIM
AT_IN = 4 * A_WIDTH

GATE_IN = 2 * D_MODEL
TOTAL_IN = RW_IN + AT_IN + GATE_IN

Q_BLOCK = 128
NORM_EPS = 1e-6
RWKV_GN_EPS = 64e-5
NEG_INF = -1e30
ATTN_SCALE = A_HEAD_DIM ** -0.5

kernel_name = 'rwkv7_diffattn_gated_hybrid_step'


def rms_norm(x, g):
    xf = x.astype(jnp.float32)
    y = xf * lax.rsqrt(jnp.mean(xf * xf, axis=-1, keepdims=True) + NORM_EPS)
    return (y * g.astype(jnp.float32)).astype(x.dtype)


def rwkv7_mixer(p, prev_row, wkv0, mu, w0, w2, a0, a2, k_k, k_a, r_k, lnx_w, lnx_b):
    B, T, _ = p.shape
    f32 = jnp.float32
    pf = p.astype(f32)
    prev = jnp.concatenate([prev_row.astype(f32)[:, None, :], pf[:, :-1]], axis=1)
    ps = pf + (prev - pf) * mu.astype(f32)
    r, k, v, g, wd, ad = jnp.split(
        ps, [R_WIDTH, 2 * R_WIDTH, 3 * R_WIDTH, 4 * R_WIDTH, 4 * R_WIDTH + DECAY_LORA], axis=-1)
    w = -jax.nn.softplus(-(w0.astype(f32) + jnp.tanh(wd) @ w2.astype(f32))) - 0.5
    decay = jnp.exp(-jnp.exp(w))
    a = jax.nn.sigmoid(a0.astype(f32) + ad @ a2.astype(f32))
    kk = (k * k_k.astype(f32)).reshape(B, T, R_HEADS, R_HEAD_DIM)
    kk = kk / jnp.maximum(jnp.sqrt(jnp.sum(kk * kk, axis=-1, keepdims=True)), 1e-12)
    k = k * (1.0 + (a - 1.0) * k_a.astype(f32))
    heads = lambda t: t.reshape(B, T, R_HEADS, R_HEAD_DIM)
    r, k, v, a, decay = heads(r), heads(k), heads(v), heads(a), heads(decay)

    def step(S, inp):
        r_t, k_t, v_t, kk_t, b_t, d_t = inp
        s_kk = jnp.einsum('bhvk,bhk->bhv', S, kk_t)
        S = (S * d_t[:, :, None, :] - s_kk[..., None] * b_t[:, :, None, :]
             + v_t[..., None] * k_t[:, :, None, :])
        return S, jnp.einsum('bhvk,bhk->bhv', S, r_t)

    xs = tuple(jnp.swapaxes(t, 0, 1) for t in (r, k, v, kk, kk * a, decay))
    S_T, ys = lax.scan(step, wkv0.astype(f32), xs)
    y = jnp.swapaxes(ys, 0, 1)
    mean = jnp.mean(y, axis=-1, keepdims=True)
    var = jnp.mean(jnp.square(y - mean), axis=-1, keepdims=True)
    yn = ((y - mean) * lax.rsqrt(var + RWKV_GN_EPS)).reshape(B, T, R_WIDTH)
    yn = yn * lnx_w.astype(f32) + lnx_b.astype(f32)
    bonus = jnp.sum(r * k * r_k.astype(f32), axis=-1, keepdims=True) * v
    out = (yn + bonus.reshape(B, T, R_WIDTH)) * jax.nn.silu(g)
    return out.astype(p.dtype), S_T.astype(wkv0.dtype), p[:, -1]


def diff_combine(s, lam):
    pr = jax.nn.softmax(s, axis=-1)
    return pr[:, :, 0] - lam * pr[:, :, 1]


def diff_attn_prompt(q, k, v, lam):
    B, S = q.shape[:2]
    nb = S // Q_BLOCK
    qb = jnp.moveaxis(q.reshape(B, nb, Q_BLOCK, A_HEADS, 2, A_HEAD_DIM), 1, 0)
    k_pos = jnp.arange(S)
    vf = v.astype(jnp.float32)

    def block(args):
        q_blk, blk = args
        q_pos = blk * Q_BLOCK + jnp.arange(Q_BLOCK)
        s = jnp.einsum('bqhcd,bkhcd->bhcqk', q_blk, k).astype(jnp.float32) * ATTN_SCALE
        s = jnp.where(k_pos[None, :] <= q_pos[:, None], s, NEG_INF)
        return jnp.einsum('bhqk,bkhe->bqhe', diff_combine(s, lam), vf)

    o = lax.map(block, (qb, jnp.arange(nb)))
    return jnp.moveaxis(o, 0, 1).reshape(B, S, A_HEADS, A_VDIM)


def diff_attn_sample(q, k_new, v_new, cache_k, cache_v, layer, page_table, lam):
    DB, T = q.shape[:2]
    k_past = cache_k[layer, page_table].reshape(DB, -1, A_HEADS, 2, A_HEAD_DIM)
    v_past = cache_v[layer, page_table].reshape(DB, -1, A_HEADS, A_VDIM)
    P = k_past.shape[1]
    s_past = jnp.einsum('bqhcd,bkhcd->bhcqk', q, k_past).astype(jnp.float32) * ATTN_SCALE
    s_new = jnp.einsum('bqhcd,bkhcd->bhcqk', q, k_new).astype(jnp.float32) * ATTN_SCALE
    t = jnp.arange(T)
    s_new = jnp.where(t[None, :] <= t[:, None], s_new, NEG_INF)
    attn = diff_combine(jnp.concatenate([s_past, s_new], axis=-1), lam)
    return (jnp.einsum('bhqk,bkhe->bqhe', attn[..., :P], v_past.astype(jnp.float32))
            + jnp.einsum('bhqk,bkhe->bqhe', attn[..., P:], v_new.astype(jnp.float32)))


def diff_head_norm(o, subln_w, lam_init, dtype):
    B, T = o.shape[:2]
    o = o * lax.rsqrt(jnp.mean(o * o, axis=-1, keepdims=True) + NORM_EPS)
    o = o * subln_w.astype(jnp.float32) * (1.0 - lam_init)
    return o.reshape(B, T, A_WIDTH).astype(dtype)


def mixer_layer(x, shift_prev, wkv0, attend, norm_in, w_in, mu_shift, w0, w2, a0, a2, k_k, k_a,
                r_k, lnx_w, lnx_b, subln_w, lam_init, w_br_rwkv, w_br_attn, w_out):
    B, T, _ = x.shape
    h = rms_norm(x, norm_in)
    proj = h @ w_in
    p_rw, p_at, p_gate = jnp.split(proj, [RW_IN, RW_IN + AT_IN], axis=-1)
    rw_out, wkv_T, shift_T = rwkv7_mixer(p_rw, shift_prev, wkv0, mu_shift, w0, w2, a0, a2,
                                         k_k, k_a, r_k, lnx_w, lnx_b)
    q, k, v, ag = jnp.split(p_at, 4, axis=-1)
    q = q.reshape(B, T, A_HEADS, 2, A_HEAD_DIM)
    k = k.reshape(B, T, A_HEADS, 2, A_HEAD_DIM)
    v = v.reshape(B, T, A_HEADS, A_VDIM)
    at_out = diff_head_norm(attend(q, k, v), subln_w, lam_init, x.dtype) * jax.nn.silu(ag)
    g_rw, g_at = jnp.split(p_gate, 2, axis=-1)
    merged = (jax.nn.sigmoid(g_rw) * (rw_out @ w_br_rwkv)
              + jax.nn.sigmoid(g_at) * (at_out @ w_br_attn))
    return x + merged @ w_out, k, v, wkv_T, shift_T


def setup_inputs(seed: int = 0) -> dict:
    key = jax.random.key(seed)
    ks = jax.random.split(key, 32)
    n_pages = PAST_LEN // PAGE_SIZE
    n_used = DEC_BATCH * n_pages
    n_pool = n_used + n_used // 4
    nrm = lambda kk, shape, s: jax.random.normal(kk, shape, jnp.float32) * s
    page_table = jax.random.permutation(ks[6], n_pool)[:n_used].reshape(DEC_BATCH, n_pages).astype(jnp.int32)
    return {
        'x_prompt': nrm(ks[0], (BATCH, SEQ, D_MODEL), 1.0),
        'x_sample': nrm(ks[1], (DEC_BATCH, DEC_SEQ, D_MODEL), 1.0),
        'cache_k': nrm(ks[2], (DEPTH, n_pool, PAGE_SIZE, A_HEADS, 2, A_HEAD_DIM), 1.0),
        'cache_v': nrm(ks[3], (DEPTH, n_pool, PAGE_SIZE, A_HEADS, A_VDIM), 1.0),
        'state_wkv': nrm(ks[4], (DEPTH, DEC_BATCH, R_HEADS, R_HEAD_DIM, R_HEAD_DIM), 0.3),
        'state_shift': nrm(ks[5], (DEPTH, DEC_BATCH, RW_IN), 1.0),
        'page_table': page_table,
        'norm_in': 1.0 + nrm(ks[7], (DEPTH, D_MODEL), 0.02),
        'w_in': nrm(ks[8], (DEPTH, D_MODEL, TOTAL_IN), D_MODEL ** -0.5),
        'mu_shift': jax.random.uniform(ks[9], (DEPTH, RW_IN), jnp.float32),
        'w0': jax.random.uniform(ks[10], (DEPTH, R_WIDTH), jnp.float32, -6.0, 1.0),
        'w2': nrm(ks[11], (DEPTH, DECAY_LORA, R_WIDTH), 0.1),
        'a0': nrm(ks[12], (DEPTH, R_WIDTH), 0.1),
        'a2': nrm(ks[13], (DEPTH, ICLR_LORA, R_WIDTH), 0.1),
        'k_k': 0.85 + nrm(ks[14], (DEPTH, R_WIDTH), 0.02),
        'k_a': 1.0 + nrm(ks[15], (DEPTH, R_WIDTH), 0.02),
        'r_k': nrm(ks[16], (DEPTH, R_HEADS, R_HEAD_DIM), 0.1),
        'lnx_w': 1.0 + nrm(ks[17], (DEPTH, R_WIDTH), 0.02),
        'lnx_b': nrm(ks[18], (DEPTH, R_WIDTH), 0.02),
        'lambda_q1': nrm(ks[19], (DEPTH, A_HEAD_DIM), 0.1),
        'lambda_k1': nrm(ks[20], (DEPTH, A_HEAD_DIM), 0.1),
        'lambda_q2': nrm(ks[21], (DEPTH, A_HEAD_DIM), 0.1),
        'lambda_k2': nrm(ks[22], (DEPTH, A_HEAD_DIM), 0.1),
        'subln_w': 1.0 + nrm(ks[23], (DEPTH, A_VDIM), 0.02),
        'w_br_rwkv': nrm(ks[24], (DEPTH, R_WIDTH, D_MODEL), R_WIDTH ** -0.5),
        'w_br_attn': nrm(ks[25], (DEPTH, A_WIDTH, D_MODEL), A_WIDTH ** -0.5),
        'w_out': nrm(ks[26], (DEPTH, D_MODEL, D_MODEL), D_MODEL ** -0.5),
        'norm_f': 1.0 + nrm(ks[27], (D_MODEL,), 0.02),
    }


def reference(x_prompt, x_sample, cache_k, cache_v, state_wkv, state_shift, page_table,
              norm_in, w_in, mu_shift, w0, w2, a0, a2, k_k, k_a, r_k, lnx_w, lnx_b,
              lambda_q1, lambda_k1, lambda_q2, lambda_k2, subln_w, w_br_rwkv, w_br_attn,
              w_out, norm_f):
    xp, xs = x_prompt, x_sample
    Bp = xp.shape[0]
    kp_l, vp_l, ks_l, vs_l, wp_l, ws_l, sp_l, ss_l = [], [], [], [], [], [], [], []
    for l in range(DEPTH):
        lam_init = 0.8 - 0.6 * math.exp(-0.3 * l)
        lam = (jnp.exp(jnp.sum(lambda_q1[l].astype(jnp.float32) * lambda_k1[l].astype(jnp.float32)))
               - jnp.exp(jnp.sum(lambda_q2[l].astype(jnp.float32) * lambda_k2[l].astype(jnp.float32)))
               + lam_init)
        lp = (norm_in[l], w_in[l], mu_shift[l], w0[l], w2[l], a0[l], a2[l], k_k[l], k_a[l],
              r_k[l], lnx_w[l], lnx_b[l], subln_w[l], lam_init, w_br_rwkv[l], w_br_attn[l], w_out[l])
        xp, kp, vp, wkvp, shp = mixer_layer(
            xp, jnp.zeros((Bp, RW_IN), xp.dtype),
            jnp.zeros((Bp, R_HEADS, R_HEAD_DIM, R_HEAD_DIM), xp.dtype),
            lambda q, k, v, lam=lam: diff_attn_prompt(q, k, v, lam), *lp)
        xs, kn, vn, wkvs, shs = mixer_layer(
            xs, state_shift[l], state_wkv[l],
            lambda q, k, v, lam=lam, l=l: diff_attn_sample(q, k, v, cache_k, cache_v, l, page_table, lam),
            *lp)
        kp_l.append(kp); vp_l.append(vp); ks_l.append(kn); vs_l.append(vn)
        wp_l.append(wkvp); ws_l.append(wkvs); sp_l.append(shp); ss_l.append(shs)
    y_prompt = rms_norm(xp, norm_f)
    y_sample = rms_norm(xs, norm_f)
    return (y_prompt, y_sample, jnp.stack(kp_l), jnp.stack(vp_l), jnp.stack(ks_l), jnp.stack(vs_l),
            jnp.stack(wp_l), jnp.stack(ws_l), jnp.stack(sp_l), jnp.stack(ss_l))
```

```python
import contextlib
import numpy as np
import concourse.bass as bass
import concourse.mybir as mybir
from concourse.bass_utils import run_bass_kernel_spmd

F32 = mybir.dt.float32
BF16 = mybir.dt.bfloat16
I32 = mybir.dt.int32
U32 = mybir.dt.uint32
ALU = mybir.AluOpType
AF = mybir.ActivationFunctionType
AX = mybir.AxisListType

NCORES = 8
D = 2048
KC = 16
T = 2048
NT = 16
RW_IN = 4288
AT0 = RW_IN
G0 = RW_IN + 4096
TOTAL_IN = 12480
NS_ALL = 512
NS_OWN = 64
ATTN_SCALE = 0.125
NORM_EPS = 1e-6
GN_EPS = 64e-5
LAM_INIT = 0.2


class Dep:
    __slots__ = ("w", "r")

    def __init__(self):
        self.w = {}
        self.r = {}


class Eng:
    def __init__(self, K, name, handle, is_pe=False):
        self.K = K
        self.name = name
        self.h = handle
        self.is_pe = is_pe
        self.sem = K.new_sem("e_" + name)
        self.cnt = 0
        self.seen = {}
        self.dsems = []
        self.dvals = []
        self.dnext = 0


class Kern:
    def __init__(self, nc, stack):
        self.nc = nc
        self.stack = stack
        self.sems = []
        self.pe = Eng(self, "pe", nc.tensor, is_pe=True)
        self.dve = Eng(self, "dve", nc.vector)
        self.act = Eng(self, "act", nc.scalar)
        self.pool = Eng(self, "pool", nc.gpsimd)
        self.sp = Eng(self, "sp", nc.sync)
        for e, n in ((self.sp, 8), (self.pool, 12), (self.act, 4)):
            for i in range(n):
                e.dsems.append(self.new_sem("d_%s%d" % (e.name, i)))
                e.dvals.append(0)
        self.n_ins = 0
        self.cur = None

    def new_sem(self, name):
        s = self.stack.enter_context(self.nc.semaphore(name))
        self.sems.append(s)
        return len(self.sems) - 1

    def sb(self, name, shape, dtype):
        st = self.cur if self.cur is not None else self.stack
        return st.enter_context(self.nc.sbuf_tensor(name, list(shape), dtype))

    def barrier(self):
        engs = [self.pe, self.dve, self.act, self.pool, self.sp]
        for e in engs:
            for o in engs:
                if o is e or o.cnt == 0:
                    continue
                if e.seen.get(o.sem, 0) < o.cnt:
                    e.h.wait_ge(self.sems[o.sem], o.cnt)
                    e.seen[o.sem] = o.cnt
                for s_, v_ in zip(o.dsems, o.dvals):
                    if v_ and e.seen.get(s_, 0) < v_:
                        e.h.wait_ge(self.sems[s_], v_)
                        e.seen[s_] = v_
            for s_, v_ in zip(e.dsems, e.dvals):
                if v_ and e.seen.get(s_, 0) < v_:
                    e.h.wait_ge(self.sems[s_], v_)
                    e.seen[s_] = v_

    def _wait(self, eng, reads, writes):
        need = {}
        for d in reads:
            for s, v in d.w.items():
                if need.get(s, 0) < v:
                    need[s] = v
        for d in writes:
            for s, v in d.w.items():
                if need.get(s, 0) < v:
                    need[s] = v
            for s, v in d.r.items():
                if need.get(s, 0) < v:
                    need[s] = v
        for s, v in need.items():
            if s == eng.sem and eng.is_pe:
                continue
            if eng.seen.get(s, 0) >= v:
                continue
            eng.h.wait_ge(self.sems[s], v)
            eng.seen[s] = v

    def op(self, eng, fn, reads=(), writes=()):
        self._wait(eng, reads, writes)
        ins = fn(eng.h)
        eng.cnt += 1
        ins.then_inc(self.sems[eng.sem], 1)
        ev = (eng.sem, eng.cnt)
        for d in reads:
            if d.r.get(ev[0], 0) < ev[1]:
                d.r[ev[0]] = ev[1]
        for d in writes:
            d.w[ev[0]] = ev[1]
        self.n_ins += 1
        return ins

    def dma(self, eng, fn, reads=(), writes=()):
        i = eng.dnext
        eng.dnext = (i + 1) % len(eng.dsems)
        s = eng.dsems[i]
        if eng.seen.get(s, 0) < eng.dvals[i]:
            eng.h.wait_ge(self.sems[s], eng.dvals[i])
            eng.seen[s] = eng.dvals[i]
        self._wait(eng, reads, writes)
        ins = fn(eng.h)
        eng.dvals[i] += 16
        ins.then_inc(self.sems[s], 16)
        ev = (s, eng.dvals[i])
        for d in reads:
            if d.r.get(ev[0], 0) < ev[1]:
                d.r[ev[0]] = ev[1]
        for d in writes:
            d.w[ev[0]] = ev[1]
        self.n_ins += 1
        return ins

    def finish(self, deps):
        self._wait(self.sp, deps, ())


def build(dbg=False):
    nc = bass.Bass("TRN2", target_bir_lowering=False)
    stack = contextlib.ExitStack()
    with stack:
        K = Kern(nc, stack)
        _build_body(nc, K, dbg)
    return nc


def _dram_in(nc, name, shape, dtype=F32):
    return nc.dram_tensor(name, list(shape), dtype, kind="ExternalInput").ap()


def _dram_out(nc, name, shape, dtype=F32):
    return nc.dram_tensor(name, list(shape), dtype, kind="ExternalOutput").ap()


def _dram_tmp(nc, name, shape, dtype):
    return nc.dram_tensor(name, list(shape), dtype, kind="Internal").ap()


def _build_body(nc, K, dbg):
    pe, dve, act, pool, sp = K.pe, K.dve, K.act, K.pool, K.sp
    xp = _dram_in(nc, "xp", [T, D])
    xs_own = _dram_in(nc, "xs_own", [NS_OWN, D])
    w_in = _dram_in(nc, "w_in", [D, TOTAL_IN])
    norm_in = _dram_in(nc, "norm_in", [1, D])
    ident_d = _dram_in(nc, "ident", [128, 128])
    trimask_d = _dram_in(nc, "trimask", [128, 128])
    lam4_d = _dram_in(nc, "lam4", [1, 256])
    subln_d = _dram_in(nc, "subln", [1, 128])
    s_atT = _dram_tmp(nc, "s_atT", [1024, T], BF16)
    s_rwT = _dram_tmp(nc, "s_rwT", [1024, T], BF16)
    s_atTs = _dram_tmp(nc, "s_atTs", [1024, NS_OWN], BF16)
    s_qTs = _dram_tmp(nc, "s_qTs", [1024, NS_OWN], BF16)
    s_kTs = _dram_tmp(nc, "s_kTs", [1024, NS_OWN], BF16)
    s_vs = _dram_tmp(nc, "s_vs", [NS_OWN, 1024], BF16)
    s_ags = _dram_tmp(nc, "s_ags", [NS_OWN, 1024], BF16)
    d_qTs, d_kTs, d_vs, d_ags = Dep(), Dep(), Dep(), Dep()
    s_rwTs = _dram_tmp(nc, "s_rwTs", [1024, NS_OWN], BF16)
    d_atTs, d_rwTs = Dep(), Dep()
    normf_d = _dram_in(nc, "normf", [1, D])
    w_out_d = _dram_in(nc, "w_out", [D, D])
    w_brr_d = _dram_in(nc, "w_brr", [1024, D])
    w_bra_d = _dram_in(nc, "w_bra", [1024, D])
    d_rwT = Dep()
    mu_d = _dram_in(nc, "mu", [1, RW_IN])
    ck_d = _dram_in(nc, "ck", [2560 * 128, 1024])
    cv_d = _dram_in(nc, "cv", [2560 * 128, 1024])
    pt_d = _dram_in(nc, "pt_own", [1, 256], I32)
    iota_d = _dram_in(nc, "iota", [128, 1])
    smask_d = _dram_in(nc, "smask", [4, 8])
    sel_d = _dram_in(nc, "sel", [8, 2, 16, 64])
    shift_d = _dram_in(nc, "shift_own", [16, RW_IN])
    wkv_d = _dram_in(nc, "wkv_own", [16, 16, 64, 64])
    s_rs = _dram_tmp(nc, "s_rs", [NS_OWN, 6, 1024], F32)
    s_ysm = _dram_tmp(nc, "s_ysm", [NS_OWN, 1024], F32)
    rwp_d = _dram_in(nc, "rwp", [8, 1024])
    w2_d = _dram_in(nc, "w2", [96, 1024])
    a2_d = _dram_in(nc, "a2", [96, 1024])
    mask4_d = _dram_in(nc, "mask4", [128, 512])
    maskT_d = _dram_in(nc, "maskT", [128, 128])
    umat_d = _dram_in(nc, "umat", [128, 128])
    d_atT = Dep()

    o_kp = _dram_out(nc, "o_kp", [T, 1024])
    o_vp = _dram_out(nc, "o_vp", [T, 1024])
    o_ks = _dram_out(nc, "o_ks", [NS_OWN, 1024])
    o_vs = _dram_out(nc, "o_vs", [NS_OWN, 1024])
    o_shp = _dram_out(nc, "o_shp", [1, RW_IN])
    o_shs = _dram_out(nc, "o_shs", [16, RW_IN])
    o_yp = _dram_out(nc, "o_yp", [T, D])
    o_ys = _dram_out(nc, "o_ys", [NS_OWN, D])
    o_wkvp = _dram_out(nc, "o_wkvp", [16, 64, 64])
    o_wkvs = _dram_out(nc, "o_wkvs", [16, 16, 64, 64])
    d_oyp, d_oys, d_owkvp, d_owkvs = (Dep() for _ in range(4))

    s_prw = _dram_tmp(nc, "s_prw", [T, RW_IN], F32)
    s_prws = _dram_tmp(nc, "s_prws", [NS_OWN, RW_IN], F32)
    s_qT = _dram_tmp(nc, "s_qT", [1024, T], BF16)
    s_kT = _dram_tmp(nc, "s_kT", [1024, T], BF16)
    s_v = _dram_tmp(nc, "s_v", [T, 1024], BF16)
    s_ag = _dram_tmp(nc, "s_ag", [T, 1024], BF16)
    s_sg = _dram_tmp(nc, "s_sg", [4096, T], BF16)
    s_sgs = _dram_tmp(nc, "s_sgs", [4096, NS_OWN], BF16)
    d_prw, d_prws, d_qT, d_kT, d_v, d_ag, d_sg, d_sgs = (Dep() for _ in range(8))
    d_okp, d_ovp, d_oks, d_ovs, d_oshp, d_oshs = (Dep() for _ in range(6))
    out_deps = [d_okp, d_ovp, d_oks, d_ovs, d_oshp, d_oshs]

    ps = [K.stack.enter_context(nc.psum_tensor("ps%d" % i, [128, 512], F32)) for i in range(8)]
    dps = [Dep() for _ in range(8)]
    ident_f = K.sb("ident_f", [128, 128], F32)
    ident_b = K.sb("ident_b", [128, 128], BF16)
    d_ident = Dep()
    eps_t = K.sb("eps_t", [128, 2], F32)
    d_eps = Dep()
    K.op(dve, lambda e: e.memset(eps_t[:, 0:1], NORM_EPS), writes=[d_eps])
    K.op(dve, lambda e: e.memset(eps_t[:, 1:2], GN_EPS), writes=[d_eps])
    phA = contextlib.ExitStack()
    K.cur = phA
    nin_b = K.sb("nin_b", [128, D], F32)
    d_nin = Dep()
    hT = K.sb("hT", [128, KC, T], BF16)
    d_hT = [Dep() for _ in range(NT)]
    hTa, d_hTa = None, None
    hTo = K.sb("hTo", [128, KC, NS_OWN], BF16)
    d_hTo = Dep()

    K.dma(sp, lambda e: e.dma_start(out=ident_f[:], in_=ident_d[:, :]), writes=[d_ident])
    K.op(dve, lambda e: e.tensor_copy(out=ident_b[:], in_=ident_f[:]), reads=[d_ident], writes=[d_ident])
    K.dma(sp, lambda e: e.dma_start(out=nin_b[:], in_=norm_in[0:1, :].partition_broadcast(128)),
          writes=[d_nin])

    xt = [K.sb("xt%d" % i, [128, D], F32) for i in range(2)]
    d_xt = [Dep(), Dep()]
    hb = [K.sb("hb%d" % i, [128, D], BF16) for i in range(2)]
    d_hb = [Dep(), Dep()]
    sq = K.sb("sq", [128, D], BF16)
    d_sq = Dep()
    stat = [K.sb("stat%d" % i, [128, 2], F32) for i in range(2)]
    d_stat = [Dep(), Dep()]

    tiles = [("p", i, 128) for i in range(NT)] + [("o", 0, NS_OWN)]
    for ti, (kind, i, n) in enumerate(tiles):
        b = ti % 2
        if kind == "p":
            src = xp[i * 128:(i + 1) * 128, :]
            dstT, ddst, toff = hT, d_hT[i], i * 128
        elif kind == "a":
            src = xs_all[i * 128:(i + 1) * 128, :]
            dstT, ddst, toff = hTa, d_hTa[i], i * 128
        else:
            src = xs_own[:, :]
            dstT, ddst, toff = hTo, d_hTo, 0
        K.dma(sp, lambda e: e.dma_start(out=xt[b][:n, :], in_=src), writes=[d_xt[b]])
        K.op(act, lambda e: e.activation(out=sq[:n, :], in_=xt[b][:n, :], func=AF.Square,
                                         accum_out=stat[b][:n, 0:1]),
             reads=[d_xt[b]], writes=[d_sq, d_stat[b]])
        K.op(act, lambda e: e.activation(out=stat[b][:n, 1:2], in_=stat[b][:n, 0:1], func=AF.Sqrt,
                                         scale=1.0 / D, bias=eps_t[:n, 0:1]),
             reads=[d_stat[b], d_eps], writes=[d_stat[b]])
        K.op(dve, lambda e: e.reciprocal(out=stat[b][:n, 1:2], in_=stat[b][:n, 1:2]),
             reads=[d_stat[b]], writes=[d_stat[b]])
        K.op(dve, lambda e: e.scalar_tensor_tensor(out=hb[b][:n, :], in0=xt[b][:n, :], scalar=stat[b][:n, 1:2],
                                                   in1=nin_b[:n, :], op0=ALU.mult, op1=ALU.mult),
             reads=[d_xt[b], d_stat[b], d_nin], writes=[d_hb[b]])
        for g in range(4):
            pb = (ti * 4 + g) % 2
            ptv = ps[pb][:].bitcast(BF16)
            for j in range(4):
                kc = g * 4 + j
                K.op(pe, lambda e: e.transpose(out=ptv[:, j * 128:j * 128 + n],
                                               in_=hb[b][:n, kc * 128:(kc + 1) * 128],
                                               identity=ident_b[:n, :n]),
                     reads=[d_hb[b], d_ident], writes=[dps[pb]])
            eng = act if g % 2 == 0 else dve
            src_v = ptv[:, 0:512].rearrange("p (j t) -> p j t", j=4)[:, :, :n]
            if eng is act:
                K.op(act, lambda e: e.copy(out=dstT[:, g * 4:(g + 1) * 4, toff:toff + n], in_=src_v),
                     reads=[dps[pb]], writes=[ddst])
            else:
                K.op(dve, lambda e: e.tensor_copy(out=dstT[:, g * 4:(g + 1) * 4, toff:toff + n], in_=src_v),
                     reads=[dps[pb]], writes=[ddst])

    wst = [K.sb("wst%d" % i, [128, KC, 256], F32) for i in range(2)]
    d_wst = [Dep(), Dep()]
    wb = [K.sb("wb%d" % i, [128, KC, 256], BF16) for i in range(2)]
    d_wb = [Dep(), Dep()]
    ob = [K.sb("ob%d" % i, [128, 512], F32) for i in range(3)]
    d_ob = [Dep() for _ in range(3)]
    obb = [K.sb("obb%d" % i, [128, 512], BF16) for i in range(3)]
    d_obb = [Dep() for _ in range(3)]
    cnt = {"ps": 0, "ob": 0, "obb": 0, "ev": 0}

    def next_ps():
        i = 2 + cnt["ps"] % 6
        cnt["ps"] += 1
        return i

    def next_ob():
        i = cnt["ob"] % 3
        cnt["ob"] += 1
        return i

    def next_obb():
        i = cnt["obb"] % 3
        cnt["obb"] += 1
        return i

    def evac_eng():
        cnt["ev"] += 1
        return act if cnt["ev"] % 2 == 0 else dve

    def copy_op(eng, out, in_, reads, writes):
        if eng is act:
            K.op(act, lambda e: e.copy(out=out, in_=in_), reads=reads, writes=writes)
        else:
            K.op(eng, lambda e: e.tensor_copy(out=out, in_=in_), reads=reads, writes=writes)

    w_in_v = w_in.rearrange("(kc p) n -> p kc n", p=128)

    def load_block(bi, src_v, c0, ncols):
        b = bi % 2
        for half in range(2):
            K.dma(sp, lambda e: e.dma_start(out=wst[b][:, half * 8:(half + 1) * 8, :ncols],
                                            in_=src_v[:, half * 8:(half + 1) * 8, c0:c0 + ncols]),
                  writes=[d_wst[b]])
        K.op(pool, lambda e: e.tensor_copy(out=wb[b][:, :, :ncols], in_=wst[b][:, :, :ncols]),
             reads=[d_wst[b]], writes=[d_wb[b]])
        return b

    def tok_major(b, j0, ncols, lhs, dl, toff, n, handler):
        pi = next_ps()
        for kc in range(KC):
            K.op(pe, lambda e: e.matmul(ps[pi][:n, :ncols], lhsT=lhs[:, kc, toff:toff + n],
                                        rhs=wb[b][:, kc, j0:j0 + ncols], start=(kc == 0), stop=(kc == KC - 1)),
                 reads=[dl, d_wb[b]], writes=[dps[pi]])
        handler(ps[pi][:n, :ncols], dps[pi])

    def feat_major(b, j0, rhs, dr, toff, n, handler):
        pi = next_ps()
        for kc in range(KC):
            K.op(pe, lambda e: e.matmul(ps[pi][:, :n], lhsT=wb[b][:, kc, j0:j0 + 128],
                                        rhs=rhs[:, kc, toff:toff + n], start=(kc == 0), stop=(kc == KC - 1)),
                 reads=[dr, d_wb[b]], writes=[dps[pi]])
        handler(ps[pi][:, :n], dps[pi])

    blocks = []
    for (sa, sb_) in ((0, RW_IN), (RW_IN, G0), (G0, TOTAL_IN)):
        c0 = sa
        while c0 < sb_:
            ncols = min(256, sb_ - c0)
            blocks.append((w_in_v, c0, ncols, "main"))
            c0 += ncols

    def to_dram_f32(psum_ap, dp, n, ncols, dsts):
        oi = next_ob()
        copy_op(evac_eng(), ob[oi][:n, :ncols], psum_ap, [dp], [d_ob[oi]])
        for dst, dd in dsts:
            K.dma(sp, lambda e: e.dma_start(out=dst, in_=ob[oi][:n, :ncols]), reads=[d_ob[oi]], writes=[dd])
        return oi

    import os
    nblk = int(os.environ.get("NBLK", "999"))
    blocks = blocks[:nblk] if nblk < 900 else blocks
    skipb = int(os.environ.get("SKIPB", "0"))
    blocks = blocks[skipb:]
    if blocks:
        load_block(0, blocks[0][0], blocks[0][1], blocks[0][2])
    for bi, (src_v, c0, ncols, kind) in enumerate(blocks):
        b = bi % 2
        if bi + 1 < len(blocks):
            nb = blocks[bi + 1]
            load_block(bi + 1, nb[0], nb[1], nb[2])
        if kind == "main" and c0 < RW_IN:
            for t in range(NT):
                def h(pa, dp, t=t):
                    dsts = [(s_prw[t * 128:(t + 1) * 128, c0:c0 + ncols], d_prw)]
                    oi = to_dram_f32(pa, dp, 128, ncols, dsts)
                    if t == NT - 1:
                        K.dma(sp, lambda e: e.dma_start(out=o_shp[0:1, c0:c0 + ncols],
                                                        in_=ob[oi][127:128, :ncols]),
                              reads=[d_ob[oi]], writes=[d_oshp])
                tok_major(b, 0, ncols, hT, d_hT[t], t * 128, 128, h)

            def hs(pa, dp):
                to_dram_f32(pa, dp, NS_OWN, ncols, [(s_prws[:, c0:c0 + ncols], d_prws)])
            tok_major(b, 0, ncols, hTo, d_hTo, 0, NS_OWN, hs)
        elif kind == "main" and c0 < G0:
            a0 = c0 - AT0
            which = a0 // 1024
            r0 = a0 % 1024
            if which in (0, 1):
                dstT, dd = (s_qT, d_qT) if which == 0 else (s_kT, d_kT)
                for j in range(2):
                    for tg in range(4):
                        def h(pa, dp, j=j, tg=tg):
                            oi = next_obb()
                            if which == 0:
                                K.op(act, lambda e: e.mul(out=obb[oi][:, :], in_=pa, mul=ATTN_SCALE),
                                     reads=[dp], writes=[d_obb[oi]])
                            else:
                                copy_op(evac_eng(), obb[oi][:, :], pa, [dp], [d_obb[oi]])
                            K.dma(sp, lambda e: e.dma_start(
                                out=dstT[r0 + j * 128:r0 + (j + 1) * 128, tg * 512:(tg + 1) * 512],
                                in_=obb[oi][:, :]), reads=[d_obb[oi]], writes=[dd])
                        feat_major(b, j * 128, hT, d_hT[tg * 4 + 3], tg * 512, 512, h)
            if which in (0, 1):
                dstTs, dds = (s_qTs, d_qTs) if which == 0 else (s_kTs, d_kTs)
                for j in range(2):
                    def hsq(pa, dp, j=j):
                        oi = next_obb()
                        if which == 0:
                            K.op(act, lambda e: e.mul(out=obb[oi][:, :NS_OWN], in_=pa, mul=ATTN_SCALE),
                                 reads=[dp], writes=[d_obb[oi]])
                        else:
                            copy_op(evac_eng(), obb[oi][:, :NS_OWN], pa, [dp], [d_obb[oi]])
                        K.dma(sp, lambda e: e.dma_start(out=dstTs[r0 + j * 128:r0 + (j + 1) * 128, :],
                                                        in_=obb[oi][:, :NS_OWN]), reads=[d_obb[oi]], writes=[dds])
                    feat_major(b, j * 128, hTo, d_hTo, 0, NS_OWN, hsq)
            if which in (1, 2, 3):
                def hso(pa, dp):
                    if which == 1:
                        to_dram_f32(pa, dp, NS_OWN, ncols, [(o_ks[:, r0:r0 + ncols], d_oks)])
                    elif which == 2:
                        oi = to_dram_f32(pa, dp, NS_OWN, ncols, [(o_vs[:, r0:r0 + ncols], d_ovs)])
                        bi2 = next_obb()
                        K.op(pool, lambda e: e.tensor_copy(out=obb[bi2][:NS_OWN, :ncols], in_=ob[oi][:NS_OWN, :ncols]),
                             reads=[d_ob[oi]], writes=[d_obb[bi2]])
                        K.dma(sp, lambda e: e.dma_start(out=s_vs[:, r0:r0 + ncols], in_=obb[bi2][:NS_OWN, :ncols]),
                              reads=[d_obb[bi2]], writes=[d_vs])
                    else:
                        bi2 = next_obb()
                        K.op(act, lambda e: e.activation(out=obb[bi2][:NS_OWN, :ncols], in_=pa, func=AF.Silu),
                             reads=[dp], writes=[d_obb[bi2]])
                        K.dma(sp, lambda e: e.dma_start(out=s_ags[:, r0:r0 + ncols], in_=obb[bi2][:NS_OWN, :ncols]),
                              reads=[d_obb[bi2]], writes=[d_ags])
                tok_major(b, 0, ncols, hTo, d_hTo, 0, NS_OWN, hso)
                for t in range(NT):
                    def h(pa, dp, t=t):
                        rows = slice(t * 128, (t + 1) * 128)
                        if which == 1:
                            to_dram_f32(pa, dp, 128, ncols, [(o_kp[rows, r0:r0 + ncols], d_okp)])
                        elif which == 2:
                            oi = to_dram_f32(pa, dp, 128, ncols, [(o_vp[rows, r0:r0 + ncols], d_ovp)])
                            bi2 = next_obb()
                            K.op(pool, lambda e: e.tensor_copy(out=obb[bi2][:, :ncols], in_=ob[oi][:, :ncols]),
                                 reads=[d_ob[oi]], writes=[d_obb[bi2]])
                            K.dma(sp, lambda e: e.dma_start(out=s_v[rows, r0:r0 + ncols],
                                                            in_=obb[bi2][:, :ncols]),
                                  reads=[d_obb[bi2]], writes=[d_v])
                        else:
                            bi2 = next_obb()
                            K.op(act, lambda e: e.activation(out=obb[bi2][:, :ncols], in_=pa, func=AF.Silu),
                                 reads=[dp], writes=[d_obb[bi2]])
                            K.dma(sp, lambda e: e.dma_start(out=s_ag[rows, r0:r0 + ncols],
                                                            in_=obb[bi2][:, :ncols]),
                                  reads=[d_obb[bi2]], writes=[d_ag])
                    tok_major(b, 0, ncols, hT, d_hT[t], t * 128, 128, h)
        elif kind == "main":
            g0 = c0 - G0
            for j in range(ncols // 128):
                for tg in range(4):
                    def h(pa, dp, j=j, tg=tg):
                        oi = next_obb()
                        K.op(act, lambda e: e.activation(out=obb[oi][:, :], in_=pa, func=AF.Sigmoid),
                             reads=[dp], writes=[d_obb[oi]])
                        K.dma(sp, lambda e: e.dma_start(
                            out=s_sg[g0 + j * 128:g0 + (j + 1) * 128, tg * 512:(tg + 1) * 512],
                            in_=obb[oi][:, :]), reads=[d_obb[oi]], writes=[d_sg])
                    feat_major(b, j * 128, hT, d_hT[tg * 4 + 3], tg * 512, 512, h)

                def hs(pa, dp, j=j):
                    oi = next_obb()
                    K.op(act, lambda e: e.activation(out=obb[oi][:, :NS_OWN], in_=pa, func=AF.Sigmoid),
                         reads=[dp], writes=[d_obb[oi]])
                    K.dma(sp, lambda e: e.dma_start(out=s_sgs[g0 + j * 128:g0 + (j + 1) * 128, :],
                                                    in_=obb[oi][:, :NS_OWN]),
                          reads=[d_obb[oi]], writes=[d_sgs])
                feat_major(b, j * 128, hTo, d_hTo, 0, NS_OWN, hs)

    K.dma(sp, lambda e: e.dma_start(out=o_shs[:, :],
                                    in_=s_prws.rearrange("(s t) n -> s t n", t=4)[:, 3, :]),
          reads=[d_prws], writes=[d_oshs])

    K.barrier()
    phA.close()
    K.cur = None
    LL = dict(locals())
    g1 = _phase_attn(nc, K, LL)
    g2 = _phase_sattn(nc, K, LL)
    next(g1)
    next(g2)
    for _i in range(16):
        next(g1)
        next(g2)
    for _g in (g1, g2):
        for _ in _g:
            pass
    K.barrier()
    LL["_ph_sattn"].close()
    LL["_ph_attn"].close()
    K.cur = None
    _phase_rwkv(nc, K, LL)
    _phase_final(nc, K, LL)
    if dbg:
        d_dbg = Dep()
        out_deps.append(d_dbg)
        for nm, src, dd, shp in (("dbg_atTs", s_atTs, d_atTs, [1024, NS_OWN]), ("dbg_rwTs", s_rwTs, d_rwTs, [1024, NS_OWN]),
                                 ("dbg_qTs", s_qTs, d_qTs, [1024, NS_OWN]), ("dbg_ags", s_ags, d_ags, [NS_OWN, 1024])):
            o_ = _dram_out(nc, nm, shp, BF16)
            K.dma(sp, lambda e: e.dma_start(out=o_[:, :], in_=src[:, :]), reads=[dd], writes=[d_dbg])
    K.finish(out_deps + [d_oyp, d_oys, d_owkvp, d_owkvs, d_prw, d_prws, d_qT, d_kT, d_v, d_ag, d_sg, d_sgs])
    print("instructions:", K.n_ins)


def _phase_attn(nc, K, L):
    pe, dve, act, pool, sp = K.pe, K.dve, K.act, K.pool, K.sp
    ps, dps = L["ps"], L["dps"]
    ident_b, d_ident = L["ident_b"], L["d_ident"]
    eps_t, d_eps = L["eps_t"], L["d_eps"]
    s_qT, s_kT, s_v, s_ag, s_atT = L["s_qT"], L["s_kT"], L["s_v"], L["s_ag"], L["s_atT"]
    d_qT, d_kT, d_v, d_ag, d_atT = L["d_qT"], L["d_kT"], L["d_v"], L["d_ag"], L["d_atT"]
    trif = K.sb("trif", [128, 128], F32)
    trib = K.sb("trib", [128, 128], BF16)
    d_tri = Dep()
    K.dma(sp, lambda e: e.dma_start(out=trif[:], in_=L["trimask_d"][:, :]), writes=[d_tri])
    K.op(dve, lambda e: e.tensor_copy(out=trib[:], in_=trif[:]), reads=[d_tri], writes=[d_tri])
    lamt = K.sb("lamt", [128, 256], F32)
    lamw = K.sb("lamw", [128, 136], F32)
    d_lam = Dep()
    K.dma(sp, lambda e: e.dma_start(out=lamt[:], in_=L["lam4_d"][0:1, :].partition_broadcast(128)), writes=[d_lam])
    lv = lamt[:].rearrange("p (a b n) -> p a b n", a=2, b=2)
    K.op(dve, lambda e: e.tensor_tensor(out=lamw[:, 0:128].rearrange("p (a n) -> p a n", a=2),
                                        in0=lv[:, :, 0, :], in1=lv[:, :, 1, :], op=ALU.mult),
         reads=[d_lam], writes=[d_lam])
    K.op(dve, lambda e: e.reduce_sum(out=lamw[:, 128:130], in_=lamw[:, 0:128].rearrange("p (a n) -> p a n", a=2),
                                     axis=AX.X), reads=[d_lam], writes=[d_lam])
    K.op(act, lambda e: e.activation(out=lamw[:, 130:132], in_=lamw[:, 128:130], func=AF.Exp),
         reads=[d_lam], writes=[d_lam])
    K.op(dve, lambda e: e.tensor_tensor(out=lamw[:, 132:133], in0=lamw[:, 130:131], in1=lamw[:, 131:132],
                                        op=ALU.subtract), reads=[d_lam], writes=[d_lam])
    K.op(dve, lambda e: e.tensor_scalar(out=lamw[:, 133:134], in0=lamw[:, 132:133], scalar1=LAM_INIT, scalar2=-1.0,
                                        op0=ALU.add, op1=ALU.mult), reads=[d_lam], writes=[d_lam])
    neglam = lamw[:, 133:134]
    subw = K.sb("subw", [128, 128], F32)
    d_subw = Dep()
    K.dma(sp, lambda e: e.dma_start(out=subw[:], in_=L["subln_d"][0:1, :].partition_broadcast(128)), writes=[d_subw])
    K.op(dve, lambda e: e.tensor_scalar(out=subw[:], in0=subw[:], scalar1=1.0 - LAM_INIT, scalar2=None,
                                        op0=ALU.mult), reads=[d_subw], writes=[d_subw])
    L["neglam"], L["d_lam"], L["subw"], L["d_subw"], L["trib"], L["d_tri"] = neglam, d_lam, subw, d_subw, trib, d_tri

    ph = contextlib.ExitStack()
    K.cur = ph
    QT = [K.sb("QT%d" % i, [128, T], BF16) for i in range(2)]
    KT = [K.sb("KT%d" % i, [128, T], BF16) for i in range(2)]
    VA = [K.sb("VA%d" % i, [128, NT, 130], BF16) for i in range(2)]
    SAG = [K.sb("SAG%d" % i, [128, NT, 128], BF16) for i in range(2)]
    ATT = [K.sb("ATT%d" % i, [128, T], BF16) for i in range(2)]
    d_in = [Dep(), Dep()]
    d_att = [Dep(), Dep()]
    fin = [K.sb("fin%d" % i, [128, 264], F32) for i in range(2)]
    d_fin = [Dep(), Dep()]
    fb = [K.sb("fb%d" % i, [128, 128], BF16) for i in range(2)]
    d_fb = [Dep(), Dep()]
    junk = K.sb("junk", [128, 128], BF16)
    d_junk = Dep()
    for i in range(2):
        K.op(pool, lambda e: e.memset(VA[i][:, :, 128:130], 1.0), writes=[d_in[i]])

    def load_head(h):
        b = h % 2
        rows = slice(h * 128, (h + 1) * 128)
        K.dma(sp, lambda e: e.dma_start(out=QT[b][:], in_=s_qT[rows, :]), reads=[d_qT], writes=[d_in[b]])
        K.dma(sp, lambda e: e.dma_start(out=KT[b][:], in_=s_kT[rows, :]), reads=[d_kT], writes=[d_in[b]])
        K.dma(sp, lambda e: e.dma_start(out=VA[b][:, :, 0:128],
                                        in_=s_v.rearrange("(t p) n -> p t n", p=128)[:, :, rows]),
              reads=[d_v], writes=[d_in[b]])
        K.dma(sp, lambda e: e.dma_start(out=SAG[b][:],
                                        in_=s_ag.rearrange("(t p) n -> p t n", p=128)[:, :, rows]),
              reads=[d_ag], writes=[d_in[b]])

    pT4 = [K.sb("pTq%d" % i, [128, 512], BF16) for i in range(4)]
    d_pT4 = [Dep() for _ in range(4)]
    accs = [K.sb("accs%d" % i, [128, 2, 130], F32) for i in range(2)]
    d_accs = [Dep(), Dep()]
    K.cur = None
    L["_ph_attn"] = ph
    yield
    cnt = {"g": 0, "q": 0}
    load_head(0)
    for h in range(8):
        b = h % 2
        if h + 1 < 8:
            load_head(h + 1)
        groups = []
        for qt in range(NT):
            for g0 in range(0, qt + 1, 4):
                groups.append((qt, g0, min(4, qt + 1 - g0)))

        def emit_S(gidx, qt, g0, nk):
            for c in range(2):
                prt = slice(c * 64, (c + 1) * 64)
                sb_ = 2 + 2 * c + gidx % 2
                for j in range(nk):
                    kt = g0 + j
                    K.op(pe, lambda e: e.matmul(ps[sb_][:, j * 128:(j + 1) * 128], lhsT=KT[b][prt, kt * 128:(kt + 1) * 128],
                                                rhs=QT[b][prt, qt * 128:(qt + 1) * 128], start=True, stop=True),
                         reads=[d_in[b]], writes=[dps[sb_]])

        def emit_rest(gidx, qt, g0, nk):
            for c in range(2):
                sb_ = 2 + 2 * c + gidx % 2
                pi = 2 * c + gidx % 2
                K.op(act, lambda e: e.activation(out=pT4[pi][:, :nk * 128], in_=ps[sb_][:, :nk * 128], func=AF.Exp),
                     reads=[dps[sb_]], writes=[d_pT4[pi]])
            if g0 + nk - 1 == qt:
                for c in range(2):
                    pi = 2 * c + gidx % 2
                    j = nk - 1
                    K.op(dve, lambda e: e.tensor_tensor(out=pT4[pi][:, j * 128:(j + 1) * 128],
                                                        in0=pT4[pi][:, j * 128:(j + 1) * 128], in1=trib[:], op=ALU.mult),
                         reads=[d_pT4[pi], d_tri], writes=[d_pT4[pi]])
            for c in range(2):
                pi = 2 * c + gidx % 2
                pa = 6 + c
                for j in range(nk):
                    kt = g0 + j
                    K.op(pe, lambda e: e.matmul(ps[pa][:, 0:129], lhsT=pT4[pi][:, j * 128:(j + 1) * 128],
                                                rhs=VA[b][:, kt, 0:129], start=(kt == 0), stop=(kt == qt)),
                         reads=[d_pT4[pi], d_in[b]], writes=[dps[pa]])

        def finalize(qt):
            qi = cnt["q"] % 2
            cnt["q"] += 1
            A_ = accs[qi]
            dA = d_accs[qi]
            K.op(act, lambda e: e.copy(out=A_[:, 0, 0:129], in_=ps[6][:, 0:129]), reads=[dps[6]], writes=[dA])
            K.op(dve, lambda e: e.tensor_copy(out=A_[:, 1, 0:129], in_=ps[7][:, 0:129]), reads=[dps[7]], writes=[dA])
            f = fin[qi]
            df = d_fin[qi]
            K.op(dve, lambda e: e.reciprocal(out=f[:, 256:258], in_=A_[:, :, 128]), reads=[dA], writes=[df])
            K.op(dve, lambda e: e.tensor_tensor(out=f[:, 257:258], in0=f[:, 257:258], in1=neglam, op=ALU.mult),
                 reads=[df, d_lam], writes=[df])
            K.op(dve, lambda e: e.tensor_scalar(out=f[:, 0:128], in0=A_[:, 0, 0:128], scalar1=f[:, 256:257], scalar2=None,
                                                op0=ALU.mult), reads=[dA, df], writes=[df])
            K.op(dve, lambda e: e.scalar_tensor_tensor(out=f[:, 128:256], in0=A_[:, 1, 0:128], scalar=f[:, 257:258],
                                                       in1=f[:, 0:128], op0=ALU.mult, op1=ALU.add),
                 reads=[dA, df], writes=[df])
            K.op(act, lambda e: e.activation(out=junk[:], in_=f[:, 128:256], func=AF.Square, accum_out=f[:, 258:259]),
                 reads=[df], writes=[df, d_junk])
            K.op(act, lambda e: e.activation(out=f[:, 259:260], in_=f[:, 258:259], func=AF.Sqrt, scale=1.0 / 128,
                                             bias=eps_t[:, 0:1]), reads=[df, d_eps], writes=[df])
            K.op(dve, lambda e: e.reciprocal(out=f[:, 259:260], in_=f[:, 259:260]), reads=[df], writes=[df])
            K.op(dve, lambda e: e.scalar_tensor_tensor(out=f[:, 0:128], in0=f[:, 128:256], scalar=f[:, 259:260],
                                                       in1=L["subw"][:], op0=ALU.mult, op1=ALU.mult),
                 reads=[df, d_subw], writes=[df])
            K.op(dve, lambda e: e.tensor_tensor(out=fb[qi][:], in0=f[:, 0:128], in1=SAG[b][:, qt, :], op=ALU.mult),
                 reads=[df, d_in[b]], writes=[d_fb[qi]])
            ptv = ps[0][:].bitcast(BF16)
            K.op(pe, lambda e: e.transpose(out=ptv[:, 0:128], in_=fb[qi][:], identity=ident_b[:]),
                 reads=[d_fb[qi], d_ident], writes=[dps[0]])
            K.op(act, lambda e: e.copy(out=ATT[b][:, qt * 128:(qt + 1) * 128], in_=ptv[:, 0:128]),
                 reads=[dps[0]], writes=[d_att[b]])

        base = cnt["g"]
        s_done = [False] * (len(groups) + 1)
        for i, (qt, g0, nk) in enumerate(groups):
            if not s_done[i]:
                emit_S(base + i, qt, g0, nk)
                s_done[i] = True
            last_of_qt = (g0 + nk - 1 == qt)
            will_yield = last_of_qt and qt == 10
            if i + 1 < len(groups) and not will_yield:
                emit_S(base + i + 1, *groups[i + 1])
                s_done[i + 1] = True
            emit_rest(base + i, qt, g0, nk)
            if last_of_qt:
                finalize(qt)
                if will_yield:
                    yield
        cnt["g"] += len(groups)
        K.dma(sp, lambda e: e.dma_start(out=s_atT[h * 128:(h + 1) * 128, :], in_=ATT[b][:]),
              reads=[d_att[b]], writes=[d_atT])
        yield


def _phase_rwkv(nc, K, L):
    pe, dve, act, pool, sp = K.pe, K.dve, K.act, K.pool, K.sp
    ps, dps = L["ps"], L["dps"]
    ident_b, ident_f, d_ident = L["ident_b"], L["ident_f"], L["d_ident"]
    eps_t, d_eps = L["eps_t"], L["d_eps"]
    s_prw, d_prw, s_rwT, d_rwT = L["s_prw"], L["d_prw"], L["s_rwT"], L["d_rwT"]
    o_wkvp, d_owkvp = L["o_wkvp"], L["d_owkvp"]
    ph = contextlib.ExitStack()
    K.cur = ph
    H3 = lambda ap: ap.rearrange("p (h c) -> p h c", c=64)

    def TT(eng, out, in0, in1, op, reads, writes):
        K.op(eng, lambda e: e.tensor_tensor(out=out, in0=in0, in1=in1, op=op), reads=reads, writes=writes)

    cst = K.sb("rw_cst", [128, 8, 1024], F32)
    d_cst = Dep()
    for i in range(8):
        K.dma(sp, lambda e: e.dma_start(out=cst[:, i, :], in_=L["rwp_d"][i:i + 1, :].partition_broadcast(128)),
              writes=[d_cst])
    w0b, a0b, kkb, kab, omka, lnxw, lnxb, rkb = (cst[:, i, :] for i in range(8))
    K.op(dve, lambda e: e.tensor_scalar(out=omka, in0=omka, scalar1=-1.0, scalar2=1.0, op0=ALU.mult, op1=ALU.add),
         reads=[d_cst], writes=[d_cst])
    mub = K.sb("rw_mub", [128, RW_IN], F32)
    K.dma(sp, lambda e: e.dma_start(out=mub[:], in_=L["mu_d"][0:1, :].partition_broadcast(128)), writes=[d_cst])
    w2a2 = K.sb("rw_w2a2", [96, 2, 1024], F32)
    K.dma(sp, lambda e: e.dma_start(out=w2a2[:, 0, :], in_=L["w2_d"][:, :]), writes=[d_cst])
    K.dma(sp, lambda e: e.dma_start(out=w2a2[:, 1, :], in_=L["a2_d"][:, :]), writes=[d_cst])
    mask4 = K.sb("rw_mask4", [128, 512], BF16)
    maskT = K.sb("rw_maskT", [128, 128], BF16)
    umat = K.sb("rw_umat", [128, 128], F32)
    ones2 = K.sb("rw_ones2", [128, 2], F32)
    K.op(dve, lambda e: e.memset(ones2[:], 1.0), writes=[d_cst])

    pc = K.sb("rw_pc", [128, RW_IN], F32); d_pc = Dep()
    pp = K.sb("rw_pp", [128, RW_IN], F32); d_pp = Dep(); d_ppb = Dep()
    lw = K.sb("rw_lw", [128, 192], F32); d_lw = Dep()
    lwT = K.sb("rw_lwT", [96, 256], F32); d_lwT = Dep()
    logd = K.sb("rw_logd", [128, 1024], F32); d_logd = Dep()
    aa = K.sb("rw_aa", [128, 1024], F32); d_aa = Dep()
    kk = K.sb("rw_kk", [128, 1024], F32); d_kk = Dep()
    kp = K.sb("rw_kp", [128, 1024], F32); d_kp = Dep()
    bb = K.sb("rw_bb", [128, 1024], F32); d_bb = Dep()
    bon = K.sb("rw_bon", [128, 1024], F32); d_bon = Dep()
    st16 = K.sb("rw_st16", [128, 64], F32); d_st16 = Dep()
    pcv = K.sb("rw_pcv", [128, 16], F32); d_pcv = Dep()
    t1, t2, eneg, epos = (pc[:, i * 1024:(i + 1) * 1024] for i in range(4))
    ys = K.sb("rw_ys", [128, 1024], F32); d_ys = Dep()
    rwo = K.sb("rw_rwo", [128, 1024], BF16); d_rwo = Dep()
    rwF = K.sb("rw_rwF", [128, 8, 128], BF16); d_rwF = Dep()
    ph2 = contextlib.ExitStack()
    K.cur = ph2
    mk_f = K.sb("rw_mkf", [128, 768], F32)
    K.dma(sp, lambda e: e.dma_start(out=mk_f[:, 0:512], in_=L["mask4_d"][:, :]), writes=[d_cst])
    K.dma(sp, lambda e: e.dma_start(out=mk_f[:, 512:640], in_=L["maskT_d"][:, :]), writes=[d_cst])
    K.dma(sp, lambda e: e.dma_start(out=umat[:], in_=L["umat_d"][:, :]), writes=[d_cst])
    K.op(dve, lambda e: e.tensor_copy(out=mask4[:], in_=mk_f[:, 0:512]), reads=[d_cst], writes=[d_cst])
    K.op(dve, lambda e: e.tensor_copy(out=maskT[:], in_=mk_f[:, 512:640]), reads=[d_cst], writes=[d_cst])
    khT = K.sb("rw_khT", [128, 1024], BF16)
    nbhT = K.sb("rw_nbhT", [128, 1024], BF16)
    kktT = K.sb("rw_kktT", [128, 1024], BF16)
    rtT = K.sb("rw_rtT", [128, 1024], BF16)
    vT = K.sb("rw_vT", [128, 1024], BF16)
    d_tm = Dep()
    khF = K.sb("rw_khF", [128, 8, 128], BF16)
    nbhF = K.sb("rw_nbhF", [128, 8, 128], BF16)
    qrF = K.sb("rw_qrF", [128, 8, 2, 128], BF16)
    d_fm = Dep()
    MM = [K.sb("rw_MM%d" % h, [128, 512], BF16) for h in range(16)]
    NN = [K.sb("rw_NN%d" % h, [128, 256], BF16) for h in range(16)]
    XX = [K.sb("rw_XX%d" % h, [128, 128], BF16) for h in range(16)]
    d_MM = [Dep() for _ in range(16)]
    d_NN = [Dep() for _ in range(16)]
    d_XX = [Dep() for _ in range(16)]
    RT = K.sb("rw_RT", [128, 1024], BF16); d_RT = [Dep(), Dep()]
    WT = K.sb("rw_WT", [128, 1024], BF16); d_WT = [Dep(), Dep()]
    ST = K.sb("rw_ST", [128, 8, 64], F32)
    STs = K.sb("rw_STs", [128, 8, 64], F32)
    STb = K.sb("rw_STb", [128, 8, 64], BF16)
    d_ST = Dep(); d_STs = Dep(); d_STb = Dep()
    sto = K.sb("rw_sto", [64, 8, 128], F32); d_sto = Dep()
    K.cur = ph
    K.op(dve, lambda e: e.memset(ST[:], 0.0), writes=[d_ST])
    K.op(dve, lambda e: e.memset(STb[:], 0.0), writes=[d_STb])

    cnt = {"ps": 0, "ev": 0}

    def next_ps():
        i = 2 + cnt["ps"] % 6
        cnt["ps"] += 1
        return i

    def ev_eng():
        cnt["ev"] += 1
        return act if cnt["ev"] % 2 == 0 else dve

    def copy_op(eng, out, in_, reads, writes):
        if eng is act:
            K.op(act, lambda e: e.copy(out=out, in_=in_), reads=reads, writes=writes)
        else:
            K.op(eng, lambda e: e.tensor_copy(out=out, in_=in_), reads=reads, writes=writes)

    r_, k_, v_, g_ = (pp[:, i * 1024:(i + 1) * 1024] for i in range(4))
    def stepA(ci, is_s):
        r0 = 0 if is_s else ci * 128
        if is_s:
            K.op(pool, lambda e: e.memset(pc[:], 0.0), writes=[d_pc])
            K.op(pool, lambda e: e.memset(pp[:], 0.0), writes=[d_pp, d_ppb])
            K.dma(sp, lambda e: e.dma_start(out=pc[:NS_OWN, :], in_=L["s_prws"][:, :]), reads=[L["d_prws"]], writes=[d_pc])
            K.dma(sp, lambda e: e.dma_start(out=pp[1:NS_OWN, :], in_=L["s_prws"][0:NS_OWN - 1, :]), reads=[L["d_prws"]],
                  writes=[d_pp, d_ppb])
            for sq_ in range(16):
                K.dma(sp, lambda e: e.dma_start(out=pp[4 * sq_:4 * sq_ + 1, :], in_=L["shift_d"][sq_:sq_ + 1, :]),
                      writes=[d_pp, d_ppb])
        elif ci == 0:
            K.dma(sp, lambda e: e.dma_start(out=pc[:], in_=s_prw[r0:r0 + 128, :]), reads=[d_prw], writes=[d_pc])
            K.op(pool, lambda e: e.memset(pp[0:1, :], 0.0), writes=[d_pp, d_ppb])
            K.dma(sp, lambda e: e.dma_start(out=pp[1:128, :], in_=s_prw[0:127, :]), reads=[d_prw], writes=[d_pp, d_ppb])
        else:
            K.dma(sp, lambda e: e.dma_start(out=pc[:], in_=s_prw[r0:r0 + 128, :]), reads=[d_prw], writes=[d_pc])
            K.dma(sp, lambda e: e.dma_start(out=pp[:], in_=s_prw[r0 - 1:r0 + 127, :]), reads=[d_prw], writes=[d_pp, d_ppb])
        ca, cb_ = slice(0, 3072), slice(3072, RW_IN)
        for eng_, cs_, dd_ in ((dve, ca, d_pp), (pool, cb_, d_ppb)):
            TT(eng_, pp[:, cs_], pp[:, cs_], pc[:, cs_], ALU.subtract, [dd_, d_pc], [dd_])
            TT(eng_, pp[:, cs_], pp[:, cs_], mub[:, cs_], ALU.mult, [dd_, d_cst], [dd_])
            TT(eng_, pp[:, cs_], pp[:, cs_], pc[:, cs_], ALU.add, [dd_, d_pc], [dd_])
        K.op(act, lambda e: e.activation(out=lw[:, 0:96], in_=pp[:, 4096:4192], func=AF.Tanh), reads=[d_ppb], writes=[d_lw])
        K.op(dve, lambda e: e.tensor_copy(out=lw[:, 96:192], in_=pp[:, 4192:4288]), reads=[d_ppb], writes=[d_lw])
        for j in range(2):
            K.op(pe, lambda e: e.transpose(out=ps[0][:96, j * 128:(j + 1) * 128], in_=lw[:, j * 96:(j + 1) * 96],
                                           identity=ident_f[:]), reads=[d_lw, d_ident], writes=[dps[0]])
        K.op(dve, lambda e: e.tensor_copy(out=lwT[:, :], in_=ps[0][:96, 0:256]), reads=[dps[0]], writes=[d_lwT])
        for which, dst, dd, cb in ((0, logd, d_logd, w0b), (1, aa, d_aa, a0b)):
            for hf in range(2):
                pi = next_ps()
                K.op(pe, lambda e: e.matmul(ps[pi][:, :], lhsT=lwT[:, which * 128:(which + 1) * 128],
                                            rhs=w2a2[:, which, hf * 512:(hf + 1) * 512], start=True, stop=True),
                     reads=[d_lwT, d_cst], writes=[dps[pi]])
                TT(dve, dst[:, hf * 512:(hf + 1) * 512], ps[pi][:, :], cb[:, hf * 512:(hf + 1) * 512], ALU.add,
                   [dps[pi], d_cst], [dd])
            K.op(act, lambda e: e.activation(out=dst[:], in_=dst[:], func=AF.Sigmoid), reads=[dd], writes=[dd])
        K.op(pool, lambda e: e.tensor_scalar(out=logd[:], in0=logd[:], scalar1=-0.6065306597126334, scalar2=None,
                                             op0=ALU.mult), reads=[d_logd], writes=[d_logd])
        TT(pool, kk[:], k_, kkb, ALU.mult, [d_pp, d_cst], [d_kk])
        TT(pool, t1, kk[:], kk[:], ALU.mult, [d_kk], [d_pc])
        K.op(dve, lambda e: e.reduce_sum(out=st16[:, 0:16], in_=H3(t1), axis=AX.X), reads=[d_pc], writes=[d_st16])
        K.op(act, lambda e: e.activation(out=st16[:, 0:16], in_=st16[:, 0:16], func=AF.Sqrt), reads=[d_st16], writes=[d_st16])
        K.op(dve, lambda e: e.tensor_scalar(out=st16[:, 0:16], in0=st16[:, 0:16], scalar1=1e-12, scalar2=None,
                                            op0=ALU.max), reads=[d_st16], writes=[d_st16])
        K.op(dve, lambda e: e.reciprocal(out=st16[:, 0:16], in_=st16[:, 0:16]), reads=[d_st16], writes=[d_st16])
        TT(dve, H3(kk[:]), H3(kk[:]), st16[:, 0:16].unsqueeze(2).to_broadcast([128, 16, 64]), ALU.mult,
           [d_kk, d_st16], [d_kk])
        TT(pool, t1, aa[:], kab, ALU.mult, [d_aa, d_cst], [d_pc])
        TT(pool, t1, t1, omka, ALU.add, [d_pc, d_cst], [d_pc])
        TT(dve, kp[:], k_, t1, ALU.mult, [d_pp, d_pc], [d_kp])
        TT(pool, bb[:], kk[:], aa[:], ALU.mult, [d_kk, d_aa], [d_bb])
        TT(dve, t2, r_, kp[:], ALU.mult, [d_pp, d_kp], [d_pc])
        TT(pool, t2, t2, rkb, ALU.mult, [d_pc, d_cst], [d_pc])
        K.op(dve, lambda e: e.reduce_sum(out=st16[:, 16:32], in_=H3(t2), axis=AX.X), reads=[d_pc], writes=[d_st16])
        TT(dve, H3(bon[:]), H3(v_), st16[:, 16:32].unsqueeze(2).to_broadcast([128, 16, 64]), ALU.mult,
           [d_pp, d_st16], [d_bon])
    def scan(ci):
        r0 = ci * 128
        cps = []
        for hf in range(2):
            pi = next_ps()
            cps.append(pi)
            K.op(pe, lambda e: e.matmul(ps[pi][:, :], lhsT=umat[:], rhs=logd[:, hf * 512:(hf + 1) * 512],
                                        start=True, stop=True), reads=[d_logd, d_cst], writes=[dps[pi]])
        for hf in range(2):
            pi = cps[hf]
            sl = slice(hf * 512, (hf + 1) * 512)
            K.op(act, lambda e: e.activation(out=eneg[:, sl], in_=ps[pi][:, :], func=AF.Exp, scale=-1.0),
                 reads=[dps[pi]], writes=[d_pc])
            K.op(act, lambda e: e.activation(out=epos[:, sl], in_=ps[pi][:, :], func=AF.Exp), reads=[dps[pi]], writes=[d_pc])
            TT(dve, t1[:, sl], ps[pi][:, :], logd[:, sl], ALU.subtract, [dps[pi], d_logd], [d_pc])
        K.op(act, lambda e: e.activation(out=t1, in_=t1, func=AF.Exp), reads=[d_pc], writes=[d_pc])
        pi = next_ps()
        for g in range(8):
            K.op(pe, lambda e: e.matmul(ps[pi][:, g * 2:g * 2 + 2], lhsT=logd[:, g * 128:(g + 1) * 128], rhs=ones2[:],
                                        start=True, stop=True), reads=[d_logd, d_cst], writes=[dps[pi]])
        K.op(act, lambda e: e.activation(out=pcv[:, 0:16], in_=ps[pi][:, 0:16], func=AF.Exp), reads=[dps[pi]], writes=[d_pcv])
        TT(dve, khT[:], kp[:], eneg, ALU.mult, [d_kp, d_pc], [d_tm])
        K.op(dve, lambda e: e.scalar_tensor_tensor(out=nbhT[:], in0=bb[:], scalar=-1.0, in1=eneg, op0=ALU.mult,
                                                   op1=ALU.mult), reads=[d_bb, d_pc], writes=[d_tm])
        TT(pool, kktT[:], kk[:], t1, ALU.mult, [d_kk, d_pc], [d_tm])
        TT(pool, rtT[:], r_, epos, ALU.mult, [d_pp, d_pc], [d_tm])
        K.op(act, lambda e: e.copy(out=vT[:], in_=v_), reads=[d_pp], writes=[d_tm])
        for src, dstv in ((khT, khF[:, :, :]), (nbhT, nbhF[:, :, :]), (kktT, qrF[:, :, 0, :]), (rtT, qrF[:, :, 1, :])):
            tb = cnt["ps"] % 2
            cnt["ps"] += 1
            ptv = ps[tb][:].bitcast(BF16)
            for g in range(8):
                K.op(pe, lambda e: e.transpose(out=ptv[:, g * 128:(g + 1) * 128], in_=src[:, g * 128:(g + 1) * 128],
                                               identity=ident_b[:]), reads=[d_tm, d_ident], writes=[dps[tb]])
            copy_op(ev_eng(), dstv, ptv[:, :].rearrange("p (g t) -> p g t", g=8), [dps[tb]], [d_fm])
        for h in range(16):
            g = h // 2
            prt = slice((h % 2) * 64, (h % 2) * 64 + 64)
            pi = next_ps()
            K.op(pe, lambda e: e.matmul(ps[pi][:, 0:256], lhsT=khF[prt, g, :], rhs=qrF[prt, g, :, :],
                                        start=True, stop=True), reads=[d_fm], writes=[dps[pi]])
            K.op(pe, lambda e: e.matmul(ps[pi][:, 256:512], lhsT=nbhF[prt, g, :], rhs=qrF[prt, g, :, :],
                                        start=True, stop=True), reads=[d_fm], writes=[dps[pi]])
            TT(dve, MM[h][:], ps[pi][:, :], mask4[:], ALU.mult, [dps[pi], d_cst], [d_MM[h]])
            pi = next_ps()
            K.op(pe, lambda e: e.matmul(ps[pi][:, 0:128], lhsT=qrF[prt, g, 0, :], rhs=nbhF[prt, g, :],
                                        start=True, stop=True), reads=[d_fm], writes=[dps[pi]])
            TT(dve, NN[h][:, 128:256], ps[pi][:, 0:128], maskT[:], ALU.mult, [dps[pi], d_cst], [d_NN[h]])
            TT(pool, XX[h][:], ident_b[:], MM[h][:, 256:384], ALU.subtract, [d_ident, d_MM[h]], [d_XX[h]])
        for lvl in range(1, 7):
            for h in range(16):
                nsrc = MM[h][:, 256:384] if lvl == 1 else NN[h][:, 0:128]
                ntsrc = NN[h][:, 128:256]
                rd = [d_MM[h], d_NN[h]]
                pi = next_ps()
                if lvl < 6:
                    K.op(pe, lambda e: e.matmul(ps[pi][:, 0:128], lhsT=ntsrc, rhs=nsrc, start=True, stop=True),
                         reads=rd, writes=[dps[pi]])
                K.op(pe, lambda e: e.matmul(ps[pi][:, 128:256], lhsT=nsrc, rhs=ntsrc, start=True, stop=True),
                     reads=rd, writes=[dps[pi]])
                if lvl < 6:
                    copy_op(ev_eng(), NN[h][:, 0:256], ps[pi][:, 0:256], [dps[pi]], [d_NN[h]])
                else:
                    copy_op(ev_eng(), NN[h][:, 128:256], ps[pi][:, 128:256], [dps[pi]], [d_NN[h]])
            for h in range(16):
                pi = next_ps()
                K.op(pe, lambda e: e.matmul(ps[pi][:, 0:128], lhsT=NN[h][:, 128:256], rhs=XX[h][:], start=True, stop=True),
                     reads=[d_NN[h], d_XX[h]], writes=[dps[pi]])
                TT(dve, XX[h][:], ps[pi][:, 0:128], XX[h][:], ALU.add, [dps[pi], d_XX[h]], [d_XX[h]])
        TT(pool, STs[:], ST[:], pcv[:, 0:16].rearrange("p (g two) -> p g two", two=2)[:, :, 0:1].to_broadcast([128, 8, 64]),
           ALU.mult, [d_ST, d_pcv], [d_STs])
        for half in range(2):
            pi = next_ps()
            for hh in range(8):
                h = half * 8 + hh
                g = h // 2
                prt = slice((h % 2) * 64, (h % 2) * 64 + 64)
                K.op(pe, lambda e: e.matmul(ps[pi][:, hh * 64:(hh + 1) * 64], lhsT=qrF[prt, g, 0, :], rhs=STb[prt, g, :],
                                            start=True, stop=False), reads=[d_fm, d_STb], writes=[dps[pi]])
                K.op(pe, lambda e: e.matmul(ps[pi][:, hh * 64:(hh + 1) * 64], lhsT=MM[h][:, 0:128],
                                            rhs=vT[:, h * 64:(h + 1) * 64], start=False, stop=True),
                     reads=[d_MM[h], d_tm], writes=[dps[pi]])
            copy_op(ev_eng(), RT[:, half * 512:(half + 1) * 512], ps[pi][:, :], [dps[pi]], [d_RT[half]])
        for half in range(2):
            pi = next_ps()
            for hh in range(8):
                h = half * 8 + hh
                K.op(pe, lambda e: e.matmul(ps[pi][:, hh * 64:(hh + 1) * 64], lhsT=XX[h][:], rhs=RT[:, h * 64:(h + 1) * 64],
                                            start=True, stop=True), reads=[d_XX[h], d_RT[half]], writes=[dps[pi]])
            copy_op(ev_eng(), WT[:, half * 512:(half + 1) * 512], ps[pi][:, :], [dps[pi]], [d_WT[half]])
        for half in range(2):
            pi = next_ps()
            for hh in range(8):
                h = half * 8 + hh
                g = h // 2
                prt = slice((h % 2) * 64, (h % 2) * 64 + 64)
                o = ps[pi][:, hh * 64:(hh + 1) * 64]
                K.op(pe, lambda e: e.matmul(o, lhsT=qrF[prt, g, 1, :], rhs=STb[prt, g, :], start=True, stop=False),
                     reads=[d_fm, d_STb], writes=[dps[pi]])
                K.op(pe, lambda e: e.matmul(o, lhsT=MM[h][:, 128:256], rhs=vT[:, h * 64:(h + 1) * 64], start=False, stop=False),
                     reads=[d_MM[h], d_tm], writes=[dps[pi]])
                K.op(pe, lambda e: e.matmul(o, lhsT=MM[h][:, 384:512], rhs=WT[:, h * 64:(h + 1) * 64], start=False, stop=True),
                     reads=[d_MM[h], d_WT[half]], writes=[dps[pi]])
            copy_op(ev_eng(), ys[:, half * 512:(half + 1) * 512], ps[pi][:, :], [dps[pi]], [d_ys])
        for half in range(2):
            pi = next_ps()
            for gg in range(4):
                g = half * 4 + gg
                o = ps[pi][:, gg * 128:(gg + 1) * 128]
                K.op(pe, lambda e: e.matmul(o, lhsT=khT[:, g * 128:(g + 1) * 128], rhs=vT[:, g * 128:(g + 1) * 128],
                                            start=True, stop=False), reads=[d_tm], writes=[dps[pi]])
                K.op(pe, lambda e: e.matmul(o, lhsT=nbhT[:, g * 128:(g + 1) * 128], rhs=WT[:, g * 128:(g + 1) * 128],
                                            start=False, stop=True), reads=[d_tm, d_WT[0], d_WT[1]], writes=[dps[pi]])
            psv = ps[pi][:, :].rearrange("p (g hl v) -> p g hl v", g=4, hl=2)
            for hl in range(2):
                prt = slice(hl * 64, hl * 64 + 64)
                pcb = pcv[prt, 0:16].rearrange("p (g two) -> p g two", two=2)[:, half * 4:half * 4 + 4, 0:1].to_broadcast([64, 4, 64])
                TT(dve, ST[prt, half * 4:half * 4 + 4, :], psv[prt, :, hl, :], pcb, ALU.mult, [dps[pi], d_pcv], [d_ST])
                TT(dve, ST[prt, half * 4:half * 4 + 4, :], ST[prt, half * 4:half * 4 + 4, :],
                   STs[prt, half * 4:half * 4 + 4, :], ALU.add, [d_ST, d_STs], [d_ST])
        K.op(act, lambda e: e.copy(out=STb[:], in_=ST[:]), reads=[d_ST], writes=[d_STb])
    def stepD(ci, is_s):
        r0 = 0 if is_s else ci * 128
        K.op(dve, lambda e: e.reduce_sum(out=st16[:, 32:48], in_=H3(ys[:]), axis=AX.X), reads=[d_ys], writes=[d_st16])
        K.op(dve, lambda e: e.tensor_scalar(out=st16[:, 32:48], in0=st16[:, 32:48], scalar1=1.0 / 64, scalar2=None,
                                            op0=ALU.mult), reads=[d_st16], writes=[d_st16])
        TT(dve, H3(ys[:]), H3(ys[:]), st16[:, 32:48].unsqueeze(2).to_broadcast([128, 16, 64]), ALU.subtract,
           [d_ys, d_st16], [d_ys])
        TT(pool, t2, ys[:], ys[:], ALU.mult, [d_ys], [d_pc])
        K.op(dve, lambda e: e.reduce_sum(out=st16[:, 48:64], in_=H3(t2), axis=AX.X), reads=[d_pc], writes=[d_st16])
        K.op(act, lambda e: e.activation(out=st16[:, 48:64], in_=st16[:, 48:64], func=AF.Sqrt, scale=1.0 / 64,
                                         bias=eps_t[:, 1:2]), reads=[d_st16, d_eps], writes=[d_st16])
        K.op(dve, lambda e: e.reciprocal(out=st16[:, 48:64], in_=st16[:, 48:64]), reads=[d_st16], writes=[d_st16])
        TT(dve, H3(ys[:]), H3(ys[:]), st16[:, 48:64].unsqueeze(2).to_broadcast([128, 16, 64]), ALU.mult,
           [d_ys, d_st16], [d_ys])
        TT(pool, ys[:], ys[:], lnxw, ALU.mult, [d_ys, d_cst], [d_ys])
        TT(pool, ys[:], ys[:], lnxb, ALU.add, [d_ys, d_cst], [d_ys])
        TT(dve, ys[:], ys[:], bon[:], ALU.add, [d_ys, d_bon], [d_ys])
        K.op(act, lambda e: e.activation(out=t2, in_=g_, func=AF.Silu), reads=[d_ppb], writes=[d_pc])
        TT(dve, rwo[:], ys[:], t2, ALU.mult, [d_ys, d_pc], [d_rwo])
        tb = cnt["ps"] % 2
        cnt["ps"] += 1
        ptv = ps[tb][:].bitcast(BF16)
        for g in range(8):
            K.op(pe, lambda e: e.transpose(out=ptv[:, g * 128:(g + 1) * 128], in_=rwo[:, g * 128:(g + 1) * 128],
                                           identity=ident_b[:]), reads=[d_rwo, d_ident], writes=[dps[tb]])
        copy_op(ev_eng(), rwF[:, :, :], ptv[:, :].rearrange("p (g t) -> p g t", g=8), [dps[tb]], [d_rwF])
        if is_s:
            K.dma(sp, lambda e: e.dma_start(out=L["s_rwTs"].rearrange("(g p) t -> p g t", p=128), in_=rwF[:, :, 0:NS_OWN]),
                  reads=[d_rwF], writes=[L["d_rwTs"]])
        else:
            K.dma(sp, lambda e: e.dma_start(out=s_rwT.rearrange("(g p) t -> p g t", p=128)[:, :, r0:r0 + 128],
                                            in_=rwF[:, :, :]), reads=[d_rwF], writes=[d_rwT])

    for ci in range(NT):
        stepA(ci, False)
        scan(ci)
        stepD(ci, False)
    for half in range(2):
        pi = next_ps()
        for gg in range(4):
            g = half * 4 + gg
            K.op(pe, lambda e: e.transpose(out=ps[pi][:64, gg * 128:(gg + 1) * 128], in_=ST[:, g, :], identity=ident_f[:]),
                 reads=[d_ST, d_ident], writes=[dps[pi]])
        K.op(dve, lambda e: e.tensor_copy(out=sto[:, half * 4:half * 4 + 4, :],
                                          in_=ps[pi][:64, :].rearrange("p (g x) -> p g x", g=4)),
             reads=[dps[pi]], writes=[d_sto])
    K.dma(sp, lambda e: e.dma_start(out=o_wkvp.rearrange("(g hl) v c -> v g hl c", hl=2),
                                    in_=sto[:, :, :].rearrange("p g (hl c) -> p g hl c", hl=2)),
          reads=[d_sto], writes=[d_owkvp])
    K.barrier()
    ph2.close()
    ph3 = contextlib.ExitStack()
    K.cur = ph3
    s_rs, s_ysm = L["s_rs"], L["s_ysm"]
    d_rs, d_ysm = Dep(), Dep()
    SS = [K.sb("rs_SS%d" % i, [128, 4096], F32) for i in range(2)]
    TM_ = [K.sb("rs_TM%d" % i, [128, 4096], F32) for i in range(2)]
    VV = [K.sb("rs_VV%d" % i, [128, 2, 6, 64], F32) for i in range(2)]
    YY = [K.sb("rs_YY%d" % i, [128, 4, 64], F32) for i in range(2)]
    SK = [K.sb("rs_SK%d" % i, [128, 64], F32) for i in range(2)]
    d_SS = [Dep(), Dep()]; d_TM = [Dep(), Dep()]; d_VV = [Dep(), Dep()]; d_YY = [Dep(), Dep()]; d_SK = [Dep(), Dep()]
    stepA("s", True)
    K.op(act, lambda e: e.activation(out=t1, in_=logd[:], func=AF.Exp), reads=[d_logd], writes=[d_pc])
    for qi, (src, dd) in enumerate(((kk[:NS_OWN, :], d_kk), (bb[:NS_OWN, :], d_bb), (kp[:NS_OWN, :], d_kp),
                                    (t1[:NS_OWN, :], d_pc), (r_[:NS_OWN, :], d_pp), (v_[:NS_OWN, :], d_pp))):
        K.dma(sp, lambda e: e.dma_start(out=s_rs[:, qi, :], in_=src), reads=[dd], writes=[d_rs])
    wkv_v = L["wkv_d"].rearrange("s h v k -> (s h) (v k)")
    owkv_v = L["o_wkvs"].rearrange("s h v k -> (s h) (v k)")
    for grp in range(2):
        K.dma(sp, lambda e: e.dma_start(out=SS[grp][:], in_=wkv_v[grp * 128:(grp + 1) * 128, :]), writes=[d_SS[grp]])
    for t in range(4):
        for grp in range(2):
            eng = dve if grp == 0 else pool
            for sl in range(8):
                sq_ = grp * 8 + sl
                K.dma(sp, lambda e: e.dma_start(out=VV[grp][sl * 16:(sl + 1) * 16, t % 2, :, :],
                                                in_=s_rs[sq_ * 4 + t, :, :].rearrange("q (h c) -> h q c", c=64)),
                      reads=[d_rs], writes=[d_VV[grp]])
            S3 = SS[grp][:].rearrange("p (v k) -> p v k", k=64)
            T3 = TM_[grp][:].rearrange("p (v k) -> p v k", k=64)
            bk = lambda q_: VV[grp][:, t % 2, q_, :].unsqueeze(1).to_broadcast([128, 64, 64])
            bv = lambda ap: ap.unsqueeze(2).to_broadcast([128, 64, 64])
            dS, dT, dV, dY, dK = d_SS[grp], d_TM[grp], d_VV[grp], d_YY[grp], d_SK[grp]
            TT(eng, T3, S3, bk(0), ALU.mult, [dS, dV], [dT])
            K.op(dve, lambda e: e.reduce_sum(out=SK[grp][:], in_=T3, axis=AX.X), reads=[dT], writes=[dK])
            TT(eng, S3, S3, bk(3), ALU.mult, [dS, dV], [dS])
            TT(eng, T3, bv(SK[grp][:]), bk(1), ALU.mult, [dK, dV], [dT])
            TT(eng, S3, S3, T3, ALU.subtract, [dS, dT], [dS])
            TT(eng, T3, bv(VV[grp][:, t % 2, 5, :]), bk(2), ALU.mult, [dV], [dT])
            TT(eng, S3, S3, T3, ALU.add, [dS, dT], [dS])
            TT(eng, T3, S3, bk(4), ALU.mult, [dS, dV], [dT])
            K.op(dve, lambda e: e.reduce_sum(out=YY[grp][:, t, :], in_=T3, axis=AX.X), reads=[dT], writes=[dY])
    for grp in range(2):
        K.dma(sp, lambda e: e.dma_start(out=owkv_v[grp * 128:(grp + 1) * 128, :], in_=SS[grp][:]), reads=[d_SS[grp]],
              writes=[L["d_owkvs"]])
        for sl in range(8):
            sq_ = grp * 8 + sl
            K.dma(sp, lambda e: e.dma_start(out=s_ysm[sq_ * 4:(sq_ + 1) * 4, :].rearrange("t (h v) -> h t v", v=64),
                                            in_=YY[grp][sl * 16:(sl + 1) * 16, :, :]), reads=[d_YY[grp]], writes=[d_ysm])
    K.op(dve, lambda e: e.memset(ys[:], 0.0), writes=[d_ys])
    K.dma(sp, lambda e: e.dma_start(out=ys[:NS_OWN, :], in_=s_ysm[:, :]), reads=[d_ysm], writes=[d_ys])
    stepD("s", True)
    K.barrier()
    ph3.close()
    ph.close()
    K.cur = None


def _phase_final(nc, K, L):
    pe, dve, act, pool, sp = K.pe, K.dve, K.act, K.pool, K.sp
    ps, dps = L["ps"], L["dps"]
    eps_t, d_eps = L["eps_t"], L["d_eps"]
    ph = contextlib.ExitStack()
    K.cur = ph
    d_cst = Dep()
    nfb = K.sb("fn_nfb", [128, D], F32)
    K.dma(sp, lambda e: e.dma_start(out=nfb[:], in_=L["normf_d"][0:1, :].partition_broadcast(128)), writes=[d_cst])
    wo = K.sb("fn_wo", [128, KC, D], BF16)
    d_wo = Dep()
    wst = [K.sb("fn_wst%d" % i, [128, KC, 128], F32) for i in range(2)]
    d_wst = [Dep(), Dep()]
    wbr = [K.sb("fn_wbr%d" % i, [128, 2, 8, 128], BF16) for i in range(2)]
    d_wbr = [Dep(), Dep()]
    wo_v = L["w_out_d"].rearrange("(kc p) n -> p kc n", p=128)
    for i in range(16):
        b = i % 2
        K.dma(sp, lambda e: e.dma_start(out=wst[b][:, :, :], in_=wo_v[:, :, i * 128:(i + 1) * 128]), writes=[d_wst[b]])
        K.op(pool, lambda e: e.tensor_copy(out=wo[:, :, i * 128:(i + 1) * 128], in_=wst[b][:, :, :]),
             reads=[d_wst[b]], writes=[d_wo])
    wr_v = L["w_brr_d"].rearrange("(kc p) n -> p kc n", p=128)
    wa_v = L["w_bra_d"].rearrange("(kc p) n -> p kc n", p=128)
    inT = [K.sb("fn_inT%d" % i, [128, 2, 8, 512], BF16) for i in range(1)]
    d_inT = [Dep()]
    sgt = [K.sb("fn_sg%d" % i, [128, 2, 512], BF16) for i in range(3)]
    d_sgt = [Dep() for _ in range(3)]
    mT = K.sb("fn_mT", [128, KC, 512], BF16)
    d_mT = Dep()
    m12 = [K.sb("fn_m%d" % i, [128, 2, 512], F32) for i in range(2)]
    d_m12 = [Dep(), Dep()]
    xt = [K.sb("fn_xt%d" % i, [128, D], F32) for i in range(2)]
    d_xt = [Dep(), Dep()]
    sq = K.sb("fn_sq", [128, D], BF16)
    d_sq = Dep()
    stat = [K.sb("fn_stat%d" % i, [128, 2], F32) for i in range(2)]
    d_stat = [Dep(), Dep()]
    cnt = {"w": 0, "sg": 0, "m": 0, "x": 0, "br": 0}

    groups = [("p", tg, 512) for tg in range(4)] + [("s", 0, NS_OWN)]
    import os
    if os.environ.get("SKIP_S"):
        groups = groups[:4]
    for kind, tg, n in groups:
        if kind == "p":
            at_src = L["s_atT"].rearrange("(kc p) t -> p kc t", p=128)[:, :, tg * 512:(tg + 1) * 512]
            rw_src = L["s_rwT"].rearrange("(kc p) t -> p kc t", p=128)[:, :, tg * 512:(tg + 1) * 512]
            rd = [L["d_atT"], L["d_rwT"]]
            sg_src = L["s_sg"].rearrange("(a c p) t -> p a c t", a=2, p=128)[:, :, :, tg * 512:(tg + 1) * 512]
            d_sgsrc = L["d_sg"]
        else:
            at_src = L["s_atTs"].rearrange("(kc p) t -> p kc t", p=128)
            rw_src = L["s_rwTs"].rearrange("(kc p) t -> p kc t", p=128)
            rd = [L["d_atTs"], L["d_rwTs"]]
            sg_src = L["s_sgs"].rearrange("(a c p) t -> p a c t", a=2, p=128)
            d_sgsrc = L["d_sgs"]
        K.dma(sp, lambda e: e.dma_start(out=inT[0][:, 0, :, :n], in_=rw_src), reads=rd, writes=[d_inT[0]])
        K.dma(sp, lambda e: e.dma_start(out=inT[0][:, 1, :, :n], in_=at_src), reads=rd, writes=[d_inT[0]])
        for cc in range(KC):
            wb_i = cnt["w"] % 2
            cnt["w"] += 1
            K.dma(sp, lambda e: e.dma_start(out=wst[wb_i][:, 0:8, :], in_=wr_v[:, :, cc * 128:(cc + 1) * 128]),
                  writes=[d_wst[wb_i]])
            K.dma(sp, lambda e: e.dma_start(out=wst[wb_i][:, 8:16, :], in_=wa_v[:, :, cc * 128:(cc + 1) * 128]),
                  writes=[d_wst[wb_i]])
            K.op(pool, lambda e: e.tensor_copy(out=wbr[wb_i][:, :, :, :].rearrange("p a k n -> p (a k) n"),
                                               in_=wst[wb_i][:, :, :]), reads=[d_wst[wb_i]], writes=[d_wbr[wb_i]])
            si = cnt["sg"] % 3
            cnt["sg"] += 1
            K.dma(sp, lambda e: e.dma_start(out=sgt[si][:, :, :n], in_=sg_src[:, :, cc, :]), reads=[d_sgsrc],
                  writes=[d_sgt[si]])
            banks = [4 + 2 * (cnt["br"] % 2), 5 + 2 * (cnt["br"] % 2)]
            cnt["br"] += 1
            for a in range(2):
                for kc in range(8):
                    K.op(pe, lambda e: e.matmul(ps[banks[a]][:, :n], lhsT=wbr[wb_i][:, a, kc, :], rhs=inT[0][:, a, kc, :n],
                                                start=(kc == 0), stop=(kc == 7)),
                         reads=[d_wbr[wb_i], d_inT[0]], writes=[dps[banks[a]]])
            mi = cnt["m"] % 2
            cnt["m"] += 1
            for a in range(2):
                K.op(dve, lambda e: e.tensor_tensor(out=m12[mi][:, a, :n], in0=ps[banks[a]][:, :n], in1=sgt[si][:, a, :n],
                                                    op=ALU.mult), reads=[dps[banks[a]], d_sgt[si]], writes=[d_m12[mi]])
            K.op(pool, lambda e: e.tensor_tensor(out=mT[:, cc, :n], in0=m12[mi][:, 0, :n], in1=m12[mi][:, 1, :n],
                                                 op=ALU.add), reads=[d_m12[mi]], writes=[d_mT])
        ntile = 4 if kind == "p" else 1
        for tt in range(ntile):
            m = 128 if kind == "p" else NS_OWN
            xi = cnt["x"] % 2
            cnt["x"] += 1
            if kind == "p":
                rows = slice(tg * 512 + tt * 128, tg * 512 + (tt + 1) * 128)
                xsrc, ydst, dy = L["xp"][rows, :], L["o_yp"][rows, :], L["d_oyp"]
            else:
                xsrc, ydst, dy = L["xs_own"][:, :], L["o_ys"][:, :], L["d_oys"]
            K.dma(sp, lambda e: e.dma_start(out=xt[xi][:m, :], in_=xsrc), writes=[d_xt[xi]])
            for cg in range(4):
                for kc in range(KC):
                    K.op(pe, lambda e: e.matmul(ps[cg][:m, :], lhsT=mT[:, kc, tt * 128:tt * 128 + m],
                                                rhs=wo[:, kc, cg * 512:(cg + 1) * 512], start=(kc == 0), stop=(kc == KC - 1)),
                         reads=[d_mT, d_wo], writes=[dps[cg]])
                K.op(dve, lambda e: e.tensor_tensor(out=xt[xi][:m, cg * 512:(cg + 1) * 512], in0=ps[cg][:m, :],
                                                    in1=xt[xi][:m, cg * 512:(cg + 1) * 512], op=ALU.add),
                     reads=[dps[cg], d_xt[xi]], writes=[d_xt[xi]])
            K.op(act, lambda e: e.activation(out=sq[:m, :], in_=xt[xi][:m, :], func=AF.Square,
                                             accum_out=stat[xi][:m, 0:1]), reads=[d_xt[xi]], writes=[d_sq, d_stat[xi]])
            K.op(act, lambda e: e.activation(out=stat[xi][:m, 1:2], in_=stat[xi][:m, 0:1], func=AF.Sqrt, scale=1.0 / D,
                                             bias=eps_t[:m, 0:1]), reads=[d_stat[xi], d_eps], writes=[d_stat[xi]])
            K.op(dve, lambda e: e.reciprocal(out=stat[xi][:m, 1:2], in_=stat[xi][:m, 1:2]), reads=[d_stat[xi]],
                 writes=[d_stat[xi]])
            K.op(dve, lambda e: e.scalar_tensor_tensor(out=xt[xi][:m, :], in0=xt[xi][:m, :], scalar=stat[xi][:m, 1:2],
                                                       in1=nfb[:m, :], op0=ALU.mult, op1=ALU.mult),
                 reads=[d_xt[xi], d_stat[xi], d_cst], writes=[d_xt[xi]])
            K.dma(sp, lambda e: e.dma_start(out=ydst, in_=xt[xi][:m, :]), reads=[d_xt[xi]], writes=[dy])
    K.barrier()
    ph.close()
    K.cur = None


def _phase_sattn(nc, K, L):
    pe, dve, act, pool, sp = K.pe, K.dve, K.act, K.pool, K.sp
    ps, dps = L["ps"], L["dps"]
    ident_b, d_ident = L["ident_b"], L["d_ident"]
    eps_t, d_eps = L["eps_t"], L["d_eps"]
    neglam, d_lam, subw, d_subw = L["neglam"], L["d_lam"], L["subw"], L["d_subw"]
    ck, cv = L["ck_d"], L["cv_d"]
    ph = contextlib.ExitStack()
    K.cur = ph
    d_c = Dep()
    pti = K.sb("sa_pti", [128, 256], I32)
    ptf = K.sb("sa_ptf", [128, 256], F32)
    iot = K.sb("sa_iot", [128, 1], F32)
    idx = K.sb("sa_idx", [128, 256], I32)
    K.dma(sp, lambda e: e.dma_start(out=pti[:], in_=L["pt_d"][0:1, :].partition_broadcast(128)), writes=[d_c])
    K.dma(sp, lambda e: e.dma_start(out=iot[:], in_=L["iota_d"][:, :]), writes=[d_c])
    K.op(dve, lambda e: e.tensor_copy(out=ptf[:], in_=pti[:]), reads=[d_c], writes=[d_c])
    K.op(dve, lambda e: e.tensor_scalar(out=ptf[:], in0=ptf[:], scalar1=128.0, scalar2=iot[:, 0:1], op0=ALU.mult,
                                        op1=ALU.add), reads=[d_c], writes=[d_c])
    K.op(dve, lambda e: e.tensor_copy(out=idx[:], in_=ptf[:]), reads=[d_c], writes=[d_c])
    QS = K.sb("sa_QS", [128, 8, 64], BF16)
    QB = K.sb("sa_QB", [128, 8, 16, 8], BF16)
    KN = K.sb("sa_KN", [128, 8, 64], BF16)
    VN = [K.sb("sa_VN%d" % i, [4, 8, 130], BF16) for i in range(2)]
    d_VN = [Dep(), Dep()]
    smk = K.sb("sa_smk", [4, 8], F32)
    smb = K.sb("sa_smb", [4, 8], BF16)
    self_ = K.sb("sa_sel", [8, 2, 16, 64], F32)
    WS = K.sb("sa_WS", [8, 16, 64], BF16)
    ON = K.sb("sa_ON", [8, 16, 8, 128], BF16)
    d_ON = Dep()
    K.dma(sp, lambda e: e.dma_start(out=QS[:], in_=L["s_qTs"].rearrange("(h p) t -> p h t", p=128)), reads=[L["d_qTs"]],
          writes=[d_c])
    K.dma(sp, lambda e: e.dma_start(out=KN[:], in_=L["s_kTs"].rearrange("(h p) t -> p h t", p=128)), reads=[L["d_kTs"]],
          writes=[d_c])
    K.op(dve, lambda e: e.memset(QB[:], 0.0), writes=[d_c])
    for c in range(2):
        prt = slice(c * 64, (c + 1) * 64)
        K.op(dve, lambda e: e.tensor_copy(out=QB[prt, :, :, c * 4:(c + 1) * 4],
                                          in_=QS[prt, :, :].rearrange("p h (s q) -> p h s q", q=4)),
             reads=[d_c], writes=[d_c])
    for i in range(2):
        K.op(dve, lambda e: e.memset(VN[i][:], 1.0), writes=[d_VN[i]])
    K.dma(sp, lambda e: e.dma_start(out=smk[:], in_=L["smask_d"][:, :]), writes=[d_c])
    K.op(dve, lambda e: e.tensor_copy(out=smb[:], in_=smk[:]), reads=[d_c], writes=[d_c])
    K.dma(sp, lambda e: e.dma_start(out=self_[:], in_=L["sel_d"][:, :, :, :]), writes=[d_c])
    K.op(dve, lambda e: e.scalar_tensor_tensor(out=WS[:], in0=self_[:, 1, :, :], scalar=neglam[0:8, :], in1=self_[:, 0, :, :],
                                               op0=ALU.mult, op1=ALU.add), reads=[d_c, d_lam], writes=[d_c])
    KP = [K.sb("sa_KP%d" % i, [128, 1024], BF16) for i in range(6)]
    d_KP = [Dep() for _ in range(6)]
    VG = [K.sb("sa_VG%d" % i, [128, 1024], BF16) for i in range(6)]
    d_VG = [Dep() for _ in range(6)]
    VP = [K.sb("sa_VP%d" % i, [128, 8, 8, 130], BF16) for i in range(2)]
    d_VP = [Dep(), Dep()]
    KTt = [K.sb("sa_KT%d" % i, [128, 8, 128], BF16) for i in range(2)]
    d_KT = [Dep(), Dep()]
    PT = [K.sb("sa_PT%d" % i, [128, 8, 64], BF16) for i in range(2)]
    d_PT = [Dep(), Dep()]
    PN = K.sb("sa_PN", [4, 64], BF16)
    d_PN = Dep()
    rd = K.sb("sa_rd", [8, 8], F32)
    d_rd = Dep()
    for i in range(2):
        K.op(dve, lambda e: e.memset(VP[i][:], 1.0), writes=[d_VP[i]])
    osb = K.sb("sa_osb", [64, 1024], F32); d_osb = Dep()
    osq = K.sb("sa_osq", [64, 1024], F32); d_osq = Dep()
    sst = K.sb("sa_sst", [64, 16], F32); d_sst = Dep()
    sag = K.sb("sa_sag", [64, 1024], BF16); d_sag = Dep()
    ofb = K.sb("sa_ofb", [64, 1024], BF16); d_ofb = Dep()
    atF = K.sb("sa_atF", [128, 8, 64], BF16); d_atF = Dep()
    K.cur = None
    L["_ph_sattn"] = ph
    yield
    cnt = {"kp": 0, "kt": 0, "half": 0}
    acc_banks = [4, 5, 6]
    hb = lambda h: (acc_banks[h // 3], (h % 3) * 129)
    for s_ in range(16):
        K.dma(sp, lambda e: e.dma_start(out=VN[s_ % 2][:, :, 0:128],
                                        in_=L["s_vs"][s_ * 4:(s_ + 1) * 4, :].rearrange("t (h e) -> t h e", h=8)),
              reads=[L["d_vs"]], writes=[d_VN[s_ % 2]])
        for bk_ in acc_banks:
            K.op(dve, lambda e: e.memset(ps[bk_][0:8, :], 0.0), writes=[dps[bk_]])
        for hf in range(2):
            hi = cnt["half"] % 2
            cnt["half"] += 1
            sbank = 2 + hi
            for jj in range(8):
                col = s_ * 16 + hf * 8 + jj
                ki = cnt["kp"] % 6
                cnt["kp"] += 1
                K.dma(pool, lambda e: e.indirect_dma_start(
                    out=KP[ki][:, :], out_offset=None, in_=ck[:, :],
                    in_offset=bass.IndirectOffsetOnAxis(ap=idx[:, col:col + 1], axis=0)), reads=[d_c], writes=[d_KP[ki]])
                K.dma(pool, lambda e: e.indirect_dma_start(
                    out=VG[ki][:, :], out_offset=None, in_=cv[:, :],
                    in_offset=bass.IndirectOffsetOnAxis(ap=idx[:, col:col + 1], axis=0)), reads=[d_c], writes=[d_VG[ki]])
                if jj % 2 == 0:
                    K.op(dve, lambda e: e.tensor_copy(out=VP[hi][:, jj, :, 0:128],
                                                      in_=VG[ki][:, :].rearrange("p (h e) -> p h e", h=8)),
                         reads=[d_VG[ki]], writes=[d_VP[hi]])
                else:
                    K.op(act, lambda e: e.copy(out=VP[hi][:, jj, :, 0:128],
                                               in_=VG[ki][:, :].rearrange("p (h e) -> p h e", h=8)),
                         reads=[d_VG[ki]], writes=[d_VP[hi]])
                tb = cnt["kt"] % 2
                cnt["kt"] += 1
                ptv = ps[tb][:].bitcast(BF16)
                for h in range(8):
                    K.op(pe, lambda e: e.transpose(out=ptv[:, h * 128:(h + 1) * 128], in_=KP[ki][:, h * 128:(h + 1) * 128],
                                                   identity=ident_b[:]), reads=[d_KP[ki], d_ident], writes=[dps[tb]])
                if tb == 0:
                    K.op(act, lambda e: e.copy(out=KTt[tb][:, :, :], in_=ptv[:, :].rearrange("p (h t) -> p h t", h=8)),
                         reads=[dps[tb]], writes=[d_KT[tb]])
                else:
                    K.op(dve, lambda e: e.tensor_copy(out=KTt[tb][:, :, :], in_=ptv[:, :].rearrange("p (h t) -> p h t", h=8)),
                         reads=[dps[tb]], writes=[d_KT[tb]])
                for h in range(8):
                    K.op(pe, lambda e: e.matmul(ps[sbank][:, jj * 64 + h * 8:jj * 64 + h * 8 + 8], lhsT=KTt[tb][:, h, :],
                                                rhs=QB[:, h, s_, :], start=True, stop=True),
                         reads=[d_KT[tb], d_c], writes=[dps[sbank]])
            K.op(act, lambda e: e.activation(out=PT[hi][:, :, :], in_=ps[sbank][:, :].rearrange("p (j x) -> p j x", j=8),
                                             func=AF.Exp), reads=[dps[sbank]], writes=[d_PT[hi]])
            for h in range(8):
                bk_, off = hb(h)
                for jj in range(8):
                    K.op(pe, lambda e: e.matmul(ps[bk_][0:8, off:off + 129], lhsT=PT[hi][:, jj, h * 8:(h + 1) * 8],
                                                rhs=VP[hi][:, jj, h, 0:129], start=False, stop=False, skip_group_check=True),
                         reads=[d_PT[hi], d_VP[hi]], writes=[dps[bk_]])
        for h in range(8):
            K.op(pe, lambda e: e.matmul(ps[7][0:4, h * 8:(h + 1) * 8], lhsT=KN[:, h, s_ * 4:(s_ + 1) * 4], rhs=QB[:, h, s_, :],
                                        start=True, stop=True), reads=[d_c], writes=[dps[7]])
        K.op(act, lambda e: e.activation(out=PN[:, :], in_=ps[7][0:4, 0:64], func=AF.Exp), reads=[dps[7]], writes=[d_PN])
        K.op(dve, lambda e: e.tensor_tensor(out=PN[:, :].rearrange("p (h x) -> p h x", h=8),
                                            in0=PN[:, :].rearrange("p (h x) -> p h x", h=8),
                                            in1=smb[:, :].unsqueeze(1).to_broadcast([4, 8, 8]), op=ALU.mult),
             reads=[d_PN, d_c], writes=[d_PN])
        for h in range(8):
            bk_, off = hb(h)
            K.op(pe, lambda e: e.matmul(ps[bk_][0:8, off:off + 129], lhsT=PN[0:4, h * 8:(h + 1) * 8], rhs=VN[s_ % 2][0:4, h, 0:129],
                                        start=False, stop=True, skip_group_check=True),
                 reads=[d_PN, d_VN[s_ % 2]], writes=[dps[bk_]])
        for bi_, bk_ in enumerate(acc_banks):
            nh = 3 if bi_ < 2 else 2
            v3 = ps[bk_][0:8, 0:nh * 129].rearrange("p (h e) -> p h e", e=129)
            K.op(dve, lambda e: e.reciprocal(out=rd[:, bi_ * 3:bi_ * 3 + nh].unsqueeze(2), in_=v3[:, :, 128:129]),
                 reads=[dps[bk_]], writes=[d_rd])
            K.op(dve, lambda e: e.tensor_tensor(out=ON[:, s_, bi_ * 3:bi_ * 3 + nh, :], in0=v3[:, :, 0:128],
                                                in1=rd[:, bi_ * 3:bi_ * 3 + nh].unsqueeze(2).to_broadcast([8, nh, 128]),
                                                op=ALU.mult), reads=[dps[bk_], d_rd], writes=[d_ON])
        yield
    K.dma(sp, lambda e: e.dma_start(out=sag[:], in_=L["s_ags"][:, :]), reads=[L["d_ags"]], writes=[d_sag])
    for hc in range(2):
        for s_ in range(16):
            K.op(pe, lambda e: e.matmul(ps[2 + hc][0:64, :], lhsT=WS[0:8, s_, :],
                                        rhs=ON[0:8, s_, hc * 4:(hc + 1) * 4, :], start=(s_ == 0), stop=(s_ == 15)),
                 reads=[d_ON, d_c], writes=[dps[2 + hc]])
        K.op(dve, lambda e: e.tensor_copy(out=osb[:, hc * 512:(hc + 1) * 512], in_=ps[2 + hc][0:64, :]),
             reads=[dps[2 + hc]], writes=[d_osb])
    H8 = lambda ap: ap.rearrange("p (h e) -> p h e", h=8)
    K.op(dve, lambda e: e.tensor_tensor(out=osq[:], in0=osb[:], in1=osb[:], op=ALU.mult), reads=[d_osb], writes=[d_osq])
    K.op(dve, lambda e: e.reduce_sum(out=sst[:, 0:8], in_=H8(osq[:]), axis=AX.X), reads=[d_osq], writes=[d_sst])
    K.op(act, lambda e: e.activation(out=sst[:, 0:8], in_=sst[:, 0:8], func=AF.Sqrt, scale=1.0 / 128, bias=eps_t[:64, 0:1]),
         reads=[d_sst, d_eps], writes=[d_sst])
    K.op(dve, lambda e: e.reciprocal(out=sst[:, 0:8], in_=sst[:, 0:8]), reads=[d_sst], writes=[d_sst])
    K.op(dve, lambda e: e.tensor_tensor(out=H8(osb[:]), in0=H8(osb[:]), in1=sst[:, 0:8].unsqueeze(2).to_broadcast([64, 8, 128]),
                                        op=ALU.mult), reads=[d_osb, d_sst], writes=[d_osb])
    K.op(dve, lambda e: e.tensor_tensor(out=H8(osb[:]), in0=H8(osb[:]), in1=subw[:64, :].unsqueeze(1).to_broadcast([64, 8, 128]),
                                        op=ALU.mult), reads=[d_osb, d_subw], writes=[d_osb])
    K.op(dve, lambda e: e.tensor_tensor(out=ofb[:], in0=osb[:], in1=sag[:], op=ALU.mult), reads=[d_osb, d_sag], writes=[d_ofb])
    ptv = ps[0][:].bitcast(BF16)
    for h in range(8):
        K.op(pe, lambda e: e.transpose(out=ptv[:, h * 64:(h + 1) * 64], in_=ofb[:, h * 128:(h + 1) * 128],
                                       identity=ident_b[:64, :64]), reads=[d_ofb, d_ident], writes=[dps[0]])
    K.op(dve, lambda e: e.tensor_copy(out=atF[:, :, :], in_=ptv[:, 0:512].rearrange("p (h t) -> p h t", h=8)),
         reads=[dps[0]], writes=[d_atF])
    K.dma(sp, lambda e: e.dma_start(out=L["s_atTs"].rearrange("(h p) t -> p h t", p=128), in_=atF[:, :, :]),
          reads=[d_atF], writes=[L["d_atTs"]])


_SU = np.triu(np.ones((128, 128), np.float32), 1)
_IU = np.triu(np.ones((128, 128), np.float32), 0)
_MASK4 = np.ascontiguousarray(np.concatenate([_SU, _IU, -_SU, _IU], axis=1))
_MASKT = np.ascontiguousarray(-_SU.T)
_UMAT = _IU

_SMASK = np.zeros((4, 8), np.float32)
_SEL = np.zeros((8, 2, 16, 64), np.float32)
for _c in range(2):
    for _q in range(4):
        for _t in range(4):
            if _t <= _q:
                _SMASK[_t, _c * 4 + _q] = 1.0
        for _s in range(16):
            _SEL[_c * 4 + _q, _c, _s, _s * 4 + _q] = 1.0

_NC_CACHE = {}


def _prep_inputs(inp, cores):
    ident = np.eye(128, dtype=np.float32)
    xs_all = np.ascontiguousarray(inp["x_sample"].reshape(NS_ALL, D))
    w_in = np.ascontiguousarray(inp["w_in"][0])
    ck_flat = inp["cache_k"].reshape(2560 * 128, 1024)
    cv_flat = inp["cache_v"].reshape(2560 * 128, 1024)
    maps = []
    for c in cores:
        m = {
            "xp": np.ascontiguousarray(inp["x_prompt"][c]),
            "xs_own": np.ascontiguousarray(xs_all[c * 64:(c + 1) * 64]),
            "w_in": w_in,
            "norm_in": np.ascontiguousarray(inp["norm_in"]),
            "ident": ident,
            "trimask": np.triu(np.ones((128, 128), np.float32)),
            "lam4": np.concatenate([inp["lambda_q1"][0], inp["lambda_k1"][0], inp["lambda_q2"][0],
                                    inp["lambda_k2"][0]]).reshape(1, 256).astype(np.float32),
            "subln": np.ascontiguousarray(inp["subln_w"]).reshape(1, 128),
            "mu": np.ascontiguousarray(inp["mu_shift"]).reshape(1, RW_IN),
            "shift_own": np.ascontiguousarray(inp["state_shift"][0, c * 16:(c + 1) * 16]),
            "wkv_own": np.ascontiguousarray(inp["state_wkv"][0, c * 16:(c + 1) * 16]),
            "rwp": np.stack([inp["w0"][0], inp["a0"][0], inp["k_k"][0], inp["k_a"][0], inp["k_a"][0],
                             inp["lnx_w"][0], inp["lnx_b"][0], inp["r_k"][0].reshape(1024)]).astype(np.float32),
            "w2": np.ascontiguousarray(inp["w2"][0]),
            "a2": np.ascontiguousarray(inp["a2"][0]),
            "normf": np.ascontiguousarray(inp["norm_f"]).reshape(1, D),
            "w_out": np.ascontiguousarray(inp["w_out"][0]),
            "w_brr": np.ascontiguousarray(inp["w_br_rwkv"][0]),
            "w_bra": np.ascontiguousarray(inp["w_br_attn"][0]),
            "mask4": _MASK4,
            "ck": ck_flat,
            "cv": cv_flat,
            "pt_own": np.ascontiguousarray(inp["page_table"][c * 16:(c + 1) * 16]).reshape(1, 256).astype(np.int32),
            "iota": np.arange(128, dtype=np.float32).reshape(128, 1),
            "smask": _SMASK,
            "sel": _SEL,
            "maskT": _MASKT,
            "umat": _UMAT,
        }
        maps.append(m)
    return maps


def kernel(**inp):
    cores = list(range(NCORES))
    if "nc" not in _NC_CACHE:
        _NC_CACHE["nc"] = build()
    nc = _NC_CACHE["nc"]
    maps = _prep_inputs(inp, cores)
    res = run_bass_kernel_spmd(nc, maps, core_ids=cores).results
    f = np.float32
    y_p = np.stack([res[c]["o_yp"] for c in cores]).astype(f)
    y_s = np.concatenate([res[c]["o_ys"] for c in cores]).reshape(128, 4, D).astype(f)
    kp = np.stack([res[c]["o_kp"] for c in cores]).reshape(1, 8, T, 8, 2, 64).astype(f)
    vp = np.stack([res[c]["o_vp"] for c in cores]).reshape(1, 8, T, 8, 128).astype(f)
    ks = np.concatenate([res[c]["o_ks"] for c in cores]).reshape(1, 128, 4, 8, 2, 64).astype(f)
    vs = np.concatenate([res[c]["o_vs"] for c in cores]).reshape(1, 128, 4, 8, 128).astype(f)
    wp = np.stack([res[c]["o_wkvp"] for c in cores]).reshape(1, 8, 16, 64, 64).astype(f)
    ws = np.concatenate([res[c]["o_wkvs"] for c in cores]).reshape(1, 128, 16, 64, 64).astype(f)
    shp = np.stack([res[c]["o_shp"][0] for c in cores]).reshape(1, 8, RW_IN).astype(f)
    shs = np.concatenate([res[c]["o_shs"] for c in cores]).reshape(1, 128, RW_IN).astype(f)
    return (y_p, y_s, kp, vp, ks, vs, wp, ws, shp, shs)
```

```python
import contextlib
import numpy as np
import concourse.bass as bass
import concourse.mybir as mybir
from concourse.bass_utils import run_bass_kernel_spmd

F32 = mybir.dt.float32
BF16 = mybir.dt.bfloat16
I32 = mybir.dt.int32
U32 = mybir.dt.uint32
ALU = mybir.AluOpType
AF = mybir.ActivationFunctionType
AX = mybir.AxisListType

NCORES = 8
D = 2048
KC = 16
T = 2048
NT = 16
RW_IN = 4288
AT0 = RW_IN
G0 = RW_IN + 4096
TOTAL_IN = 12480
NS_ALL = 512
NS_OWN = 64
ATTN_SCALE = 0.125
NORM_EPS = 1e-6
GN_EPS = 64e-5
LAM_INIT = 0.2


class Dep:
    __slots__ = ("w", "r")

    def __init__(self):
        self.w = {}
        self.r = {}


class Eng:
    def __init__(self, K, name, handle, is_pe=False):
        self.K = K
        self.name = name
        self.h = handle
        self.is_pe = is_pe
        self.sem = K.new_sem("e_" + name)
        self.cnt = 0
        self.seen = {}
        self.dsems = []
        self.dvals = []
        self.dnext = 0


class Kern:
    def __init__(self, nc, stack):
        self.nc = nc
        self.stack = stack
        self.sems = []
        self.pe = Eng(self, "pe", nc.tensor, is_pe=True)
        self.dve = Eng(self, "dve", nc.vector)
        self.act = Eng(self, "act", nc.scalar)
        self.pool = Eng(self, "pool", nc.gpsimd)
        self.sp = Eng(self, "sp", nc.sync)
        for e, n in ((self.sp, 8), (self.pool, 12), (self.act, 4)):
            for i in range(n):
                e.dsems.append(self.new_sem("d_%s%d" % (e.name, i)))
                e.dvals.append(0)
        self.n_ins = 0
        self.cur = None

    def new_sem(self, name):
        s = self.stack.enter_context(self.nc.semaphore(name))
        self.sems.append(s)
        return len(self.sems) - 1

    def sb(self, name, shape, dtype):
        st = self.cur if self.cur is not None else self.stack
        return st.enter_context(self.nc.sbuf_tensor(name, list(shape), dtype))

    def barrier(self):
        engs = [self.pe, self.dve, self.act, self.pool, self.sp]
        for e in engs:
            for o in engs:
                if o is e or o.cnt == 0:
                    continue
                if e.seen.get(o.sem, 0) < o.cnt:
                    e.h.wait_ge(self.sems[o.sem], o.cnt)
                    e.seen[o.sem] = o.cnt
                for s_, v_ in zip(o.dsems, o.dvals):
                    if v_ and e.seen.get(s_, 0) < v_:
                        e.h.wait_ge(self.sems[s_], v_)
                        e.seen[s_] = v_
            for s_, v_ in zip(e.dsems, e.dvals):
                if v_ and e.seen.get(s_, 0) < v_:
                    e.h.wait_ge(self.sems[s_], v_)
                    e.seen[s_] = v_

    def _wait(self, eng, reads, writes):
        need = {}
        for d in reads:
            for s, v in d.w.items():
                if need.get(s, 0) < v:
                    need[s] = v
        for d in writes:
            for s, v in d.w.items():
                if need.get(s, 0) < v:
                    need[s] = v
            for s, v in d.r.items():
                if need.get(s, 0) < v:
                    need[s] = v
        for s, v in need.items():
            if s == eng.sem and eng.is_pe:
                continue
            if eng.seen.get(s, 0) >= v:
                continue
            eng.h.wait_ge(self.sems[s], v)
            eng.seen[s] = v

    def op(self, eng, fn, reads=(), writes=()):
        self._wait(eng, reads, writes)
        ins = fn(eng.h)
        eng.cnt += 1
        ins.then_inc(self.sems[eng.sem], 1)
        ev = (eng.sem, eng.cnt)
        for d in reads:
            if d.r.get(ev[0], 0) < ev[1]:
                d.r[ev[0]] = ev[1]
        for d in writes:
            d.w[ev[0]] = ev[1]
        self.n_ins += 1
        return ins

    def dma(self, eng, fn, reads=(), writes=()):
        i = eng.dnext
        eng.dnext = (i + 1) % len(eng.dsems)
        s = eng.dsems[i]
        if eng.seen.get(s, 0) < eng.dvals[i]:
            eng.h.wait_ge(self.sems[s], eng.dvals[i])
            eng.seen[s] = eng.dvals[i]
        self._wait(eng, reads, writes)
        ins = fn(eng.h)
        eng.dvals[i] += 16
        ins.then_inc(self.sems[s], 16)
        ev = (s, eng.dvals[i])
        for d in reads:
            if d.r.get(ev[0], 0) < ev[1]:
                d.r[ev[0]] = ev[1]
        for d in writes:
            d.w[ev[0]] = ev[1]
        self.n_ins += 1
        return ins

    def finish(self, deps):
        self._wait(self.sp, deps, ())


def build(dbg=False):
    nc = bass.Bass("TRN2", target_bir_lowering=False)
    stack = contextlib.ExitStack()
    with stack:
        K = Kern(nc, stack)
        _build_body(nc, K, dbg)
    return nc


def _dram_in(nc, name, shape, dtype=F32):
    return nc.dram_tensor(name, list(shape), dtype, kind="ExternalInput").ap()


def _dram_out(nc, name, shape, dtype=F32):
    return nc.dram_tensor(name, list(shape), dtype, kind="ExternalOutput").ap()


def _dram_tmp(nc, name, shape, dtype):
    return nc.dram_tensor(name, list(shape), dtype, kind="Internal").ap()


def _build_body(nc, K, dbg):
    pe, dve, act, pool, sp = K.pe, K.dve, K.act, K.pool, K.sp
    xp = _dram_in(nc, "xp", [T, D])
    xs_own = _dram_in(nc, "xs_own", [NS_OWN, D])
    w_in = _dram_in(nc, "w_in", [D, TOTAL_IN])
    norm_in = _dram_in(nc, "norm_in", [1, D])
    ident_d = _dram_in(nc, "ident", [128, 128])
    trimask_d = _dram_in(nc, "trimask", [128, 128])
    lam4_d = _dram_in(nc, "lam4", [1, 256])
    subln_d = _dram_in(nc, "subln", [1, 128])
    s_atT = _dram_tmp(nc, "s_atT", [1024, T], BF16)
    s_rwT = _dram_tmp(nc, "s_rwT", [1024, T], BF16)
    s_atTs = _dram_tmp(nc, "s_atTs", [1024, NS_OWN], BF16)
    s_qTs = _dram_tmp(nc, "s_qTs", [1024, NS_OWN], BF16)
    s_kTs = _dram_tmp(nc, "s_kTs", [1024, NS_OWN], BF16)
    s_vs = _dram_tmp(nc, "s_vs", [NS_OWN, 1024], BF16)
    s_ags = _dram_tmp(nc, "s_ags", [NS_OWN, 1024], BF16)
    d_qTs, d_kTs, d_vs, d_ags = Dep(), Dep(), Dep(), Dep()
    s_rwTs = _dram_tmp(nc, "s_rwTs", [1024, NS_OWN], BF16)
    d_atTs, d_rwTs = Dep(), Dep()
    normf_d = _dram_in(nc, "normf", [1, D])
    w_out_d = _dram_in(nc, "w_out", [D, D])
    w_brr_d = _dram_in(nc, "w_brr", [1024, D])
    w_bra_d = _dram_in(nc, "w_bra", [1024, D])
    d_rwT = Dep()
    mu_d = _dram_in(nc, "mu", [1, RW_IN])
    ck_d = _dram_in(nc, "ck", [2560 * 128, 1024])
    cv_d = _dram_in(nc, "cv", [2560 * 128, 1024])
    pt_d = _dram_in(nc, "pt_own", [1, 256], I32)
    iota_d = _dram_in(nc, "iota", [128, 1])
    smask_d = _dram_in(nc, "smask", [4, 8])
    sel_d = _dram_in(nc, "sel", [8, 2, 16, 64])
    shift_d = _dram_in(nc, "shift_own", [16, RW_IN])
    wkv_d = _dram_in(nc, "wkv_own", [16, 16, 64, 64])
    s_rs = _dram_tmp(nc, "s_rs", [NS_OWN, 6, 1024], F32)
    s_ysm = _dram_tmp(nc, "s_ysm", [NS_OWN, 1024], F32)
    rwp_d = _dram_in(nc, "rwp", [8, 1024])
    w2_d = _dram_in(nc, "w2", [96, 1024])
    a2_d = _dram_in(nc, "a2", [96, 1024])
    mask4_d = _dram_in(nc, "mask4", [128, 512])
    maskT_d = _dram_in(nc, "maskT", [128, 128])
    umat_d = _dram_in(nc, "umat", [128, 128])
    d_atT = Dep()

    o_kp = _dram_out(nc, "o_kp", [T, 1024])
    o_vp = _dram_out(nc, "o_vp", [T, 1024])
    o_ks = _dram_out(nc, "o_ks", [NS_OWN, 1024])
    o_vs = _dram_out(nc, "o_vs", [NS_OWN, 1024])
    o_shp = _dram_out(nc, "o_shp", [1, RW_IN])
    o_shs = _dram_out(nc, "o_shs", [16, RW_IN])
    o_yp = _dram_out(nc, "o_yp", [T, D])
    o_ys = _dram_out(nc, "o_ys", [NS_OWN, D])
    o_wkvp = _dram_out(nc, "o_wkvp", [16, 64, 64])
    o_wkvs = _dram_out(nc, "o_wkvs", [16, 16, 64, 64])
    d_oyp, d_oys, d_owkvp, d_owkvs = (Dep() for _ in range(4))

    s_prw = _dram_tmp(nc, "s_prw", [T, RW_IN], F32)
    s_prws = _dram_tmp(nc, "s_prws", [NS_OWN, RW_IN], F32)
    s_qT = _dram_tmp(nc, "s_qT", [1024, T], BF16)
    s_kT = _dram_tmp(nc, "s_kT", [1024, T], BF16)
    s_v = _dram_tmp(nc, "s_v", [T, 1024], BF16)
    s_ag = _dram_tmp(nc, "s_ag", [T, 1024], BF16)
    s_sg = _dram_tmp(nc, "s_sg", [4096, T], BF16)
    s_sgs = _dram_tmp(nc, "s_sgs", [4096, NS_OWN], BF16)
    d_prw, d_prws, d_qT, d_kT, d_v, d_ag, d_sg, d_sgs = (Dep() for _ in range(8))
    d_okp, d_ovp, d_oks, d_ovs, d_oshp, d_oshs = (Dep() for _ in range(6))
    out_deps = [d_okp, d_ovp, d_oks, d_ovs, d_oshp, d_oshs]

    ps = [K.stack.enter_context(nc.psum_tensor("ps%d" % i, [128, 512], F32)) for i in range(8)]
    dps = [Dep() for _ in range(8)]
    ident_f = K.sb("ident_f", [128, 128], F32)
    ident_b = K.sb("ident_b", [128, 128], BF16)
    d_ident = Dep()
    eps_t = K.sb("eps_t", [128, 2], F32)
    d_eps = Dep()
    K.op(dve, lambda e: e.memset(eps_t[:, 0:1], NORM_EPS), writes=[d_eps])
    K.op(dve, lambda e: e.memset(eps_t[:, 1:2], GN_EPS), writes=[d_eps])
    phA = contextlib.ExitStack()
    K.cur = phA
    nin_b = K.sb("nin_b", [128, D], F32)
    d_nin = Dep()
    hT = K.sb("hT", [128, KC, T], BF16)
    d_hT = [Dep() for _ in range(NT)]
    hTa, d_hTa = None, None
    hTo = K.sb("hTo", [128, KC, NS_OWN], BF16)
    d_hTo = Dep()

    K.dma(sp, lambda e: e.dma_start(out=ident_f[:], in_=ident_d[:, :]), writes=[d_ident])
    K.op(dve, lambda e: e.tensor_copy(out=ident_b[:], in_=ident_f[:]), reads=[d_ident], writes=[d_ident])
    K.dma(sp, lambda e: e.dma_start(out=nin_b[:], in_=norm_in[0:1, :].partition_broadcast(128)),
          writes=[d_nin])

    xt = [K.sb("xt%d" % i, [128, D], F32) for i in range(2)]
    d_xt = [Dep(), Dep()]
    hb = [K.sb("hb%d" % i, [128, D], BF16) for i in range(2)]
    d_hb = [Dep(), Dep()]
    sq = K.sb("sq", [128, D], BF16)
    d_sq = Dep()
    stat = [K.sb("stat%d" % i, [128, 2], F32) for i in range(2)]
    d_stat = [Dep(), Dep()]

    tiles = [("p", i, 128) for i in range(NT)] + [("o", 0, NS_OWN)]
    for ti, (kind, i, n) in enumerate(tiles):
        b = ti % 2
        if kind == "p":
            src = xp[i * 128:(i + 1) * 128, :]
            dstT, ddst, toff = hT, d_hT[i], i * 128
        elif kind == "a":
            src = xs_all[i * 128:(i + 1) * 128, :]
            dstT, ddst, toff = hTa, d_hTa[i], i * 128
        else:
            src = xs_own[:, :]
            dstT, ddst, toff = hTo, d_hTo, 0
        K.dma(sp, lambda e: e.dma_start(out=xt[b][:n, :], in_=src), writes=[d_xt[b]])
        K.op(act, lambda e: e.activation(out=sq[:n, :], in_=xt[b][:n, :], func=AF.Square,
                                         accum_out=stat[b][:n, 0:1]),
             reads=[d_xt[b]], writes=[d_sq, d_stat[b]])
        K.op(act, lambda e: e.activation(out=stat[b][:n, 1:2], in_=stat[b][:n, 0:1], func=AF.Sqrt,
                                         scale=1.0 / D, bias=eps_t[:n, 0:1]),
             reads=[d_stat[b], d_eps], writes=[d_stat[b]])
        K.op(dve, lambda e: e.reciprocal(out=stat[b][:n, 1:2], in_=stat[b][:n, 1:2]),
             reads=[d_stat[b]], writes=[d_stat[b]])
        K.op(dve, lambda e: e.scalar_tensor_tensor(out=hb[b][:n, :], in0=xt[b][:n, :], scalar=stat[b][:n, 1:2],
                                                   in1=nin_b[:n, :], op0=ALU.mult, op1=ALU.mult),
             reads=[d_xt[b], d_stat[b], d_nin], writes=[d_hb[b]])
        for g in range(4):
            pb = (ti * 4 + g) % 2
            ptv = ps[pb][:].bitcast(BF16)
            for j in range(4):
                kc = g * 4 + j
                K.op(pe, lambda e: e.transpose(out=ptv[:, j * 128:j * 128 + n],
                                               in_=hb[b][:n, kc * 128:(kc + 1) * 128],
                                               identity=ident_b[:n, :n]),
                     reads=[d_hb[b], d_ident], writes=[dps[pb]])
            eng = act if g % 2 == 0 else dve
            src_v = ptv[:, 0:512].rearrange("p (j t) -> p j t", j=4)[:, :, :n]
            if eng is act:
                K.op(act, lambda e: e.copy(out=dstT[:, g * 4:(g + 1) * 4, toff:toff + n], in_=src_v),
                     reads=[dps[pb]], writes=[ddst])
            else:
                K.op(dve, lambda e: e.tensor_copy(out=dstT[:, g * 4:(g + 1) * 4, toff:toff + n], in_=src_v),
                     reads=[dps[pb]], writes=[ddst])

    wst = [K.sb("wst%d" % i, [128, KC, 256], F32) for i in range(2)]
    d_wst = [Dep(), Dep()]
    wb = [K.sb("wb%d" % i, [128, KC, 256], BF16) for i in range(2)]
    d_wb = [Dep(), Dep()]
    ob = [K.sb("ob%d" % i, [128, 512], F32) for i in range(3)]
    d_ob = [Dep() for _ in range(3)]
    obb = [K.sb("obb%d" % i, [128, 512], BF16) for i in range(3)]
    d_obb = [Dep() for _ in range(3)]
    cnt = {"ps": 0, "ob": 0, "obb": 0, "ev": 0}

    def next_ps():
        i = 2 + cnt["ps"] % 6
        cnt["ps"] += 1
        return i

    def next_ob():
        i = cnt["ob"] % 3
        cnt["ob"] += 1
        return i

    def next_obb():
        i = cnt["obb"] % 3
        cnt["obb"] += 1
        return i

    def evac_eng():
        cnt["ev"] += 1
        return act if cnt["ev"] % 2 == 0 else dve

    def copy_op(eng, out, in_, reads, writes):
        if eng is act:
            K.op(act, lambda e: e.copy(out=out, in_=in_), reads=reads, writes=writes)
        else:
            K.op(eng, lambda e: e.tensor_copy(out=out, in_=in_), reads=reads, writes=writes)

    w_in_v = w_in.rearrange("(kc p) n -> p kc n", p=128)

    def load_block(bi, src_v, c0, ncols):
        b = bi % 2
        for half in range(2):
            K.dma(sp, lambda e: e.dma_start(out=wst[b][:, half * 8:(half + 1) * 8, :ncols],
                                            in_=src_v[:, half * 8:(half + 1) * 8, c0:c0 + ncols]),
                  writes=[d_wst[b]])
        K.op(pool, lambda e: e.tensor_copy(out=wb[b][:, :, :ncols], in_=wst[b][:, :, :ncols]),
             reads=[d_wst[b]], writes=[d_wb[b]])
        return b

    def tok_major(b, j0, ncols, lhs, dl, toff, n, handler):
        pi = next_ps()
        for kc in range(KC):
            K.op(pe, lambda e: e.matmul(ps[pi][:n, :ncols], lhsT=lhs[:, kc, toff:toff + n],
                                        rhs=wb[b][:, kc, j0:j0 + ncols], start=(kc == 0), stop=(kc == KC - 1)),
                 reads=[dl, d_wb[b]], writes=[dps[pi]])
        handler(ps[pi][:n, :ncols], dps[pi])

    def feat_major(b, j0, rhs, dr, toff, n, handler):
        pi = next_ps()
        for kc in range(KC):
            K.op(pe, lambda e: e.matmul(ps[pi][:, :n], lhsT=wb[b][:, kc, j0:j0 + 128],
                                        rhs=rhs[:, kc, toff:toff + n], start=(kc == 0), stop=(kc == KC - 1)),
                 reads=[dr, d_wb[b]], writes=[dps[pi]])
        handler(ps[pi][:, :n], dps[pi])

    blocks = []
    for (sa, sb_) in ((0, RW_IN), (RW_IN, G0), (G0, TOTAL_IN)):
        c0 = sa
        while c0 < sb_:
            ncols = min(256, sb_ - c0)
            blocks.append((w_in_v, c0, ncols, "main"))
            c0 += ncols

    def to_dram_f32(psum_ap, dp, n, ncols, dsts):
        oi = next_ob()
        copy_op(evac_eng(), ob[oi][:n, :ncols], psum_ap, [dp], [d_ob[oi]])
        for dst, dd in dsts:
            K.dma(sp, lambda e: e.dma_start(out=dst, in_=ob[oi][:n, :ncols]), reads=[d_ob[oi]], writes=[dd])
        return oi

    import os
    nblk = int(os.environ.get("NBLK", "999"))
    blocks = blocks[:nblk] if nblk < 900 else blocks
    skipb = int(os.environ.get("SKIPB", "0"))
    blocks = blocks[skipb:]
    if blocks:
        load_block(0, blocks[0][0], blocks[0][1], blocks[0][2])
    for bi, (src_v, c0, ncols, kind) in enumerate(blocks):
        b = bi % 2
        if bi + 1 < len(blocks):
            nb = blocks[bi + 1]
            load_block(bi + 1, nb[0], nb[1], nb[2])
        if kind == "main" and c0 < RW_IN:
            for t in range(NT):
                def h(pa, dp, t=t):
                    dsts = [(s_prw[t * 128:(t + 1) * 128, c0:c0 + ncols], d_prw)]
                    oi = to_dram_f32(pa, dp, 128, ncols, dsts)
                    if t == NT - 1:
                        K.dma(sp, lambda e: e.dma_start(out=o_shp[0:1, c0:c0 + ncols],
                                                        in_=ob[oi][127:128, :ncols]),
                              reads=[d_ob[oi]], writes=[d_oshp])
                tok_major(b, 0, ncols, hT, d_hT[t], t * 128, 128, h)

            def hs(pa, dp):
                to_dram_f32(pa, dp, NS_OWN, ncols, [(s_prws[:, c0:c0 + ncols], d_prws)])
            tok_major(b, 0, ncols, hTo, d_hTo, 0, NS_OWN, hs)
        elif kind == "main" and c0 < G0:
            a0 = c0 - AT0
            which = a0 // 1024
            r0 = a0 % 1024
            if which in (0, 1):
                dstT, dd = (s_qT, d_qT) if which == 0 else (s_kT, d_kT)
                for j in range(2):
                    for tg in range(4):
                        def h(pa, dp, j=j, tg=tg):
                            oi = next_obb()
                            if which == 0:
                                K.op(act, lambda e: e.mul(out=obb[oi][:, :], in_=pa, mul=ATTN_SCALE),
                                     reads=[dp], writes=[d_obb[oi]])
                            else:
                                copy_op(evac_eng(), obb[oi][:, :], pa, [dp], [d_obb[oi]])
                            K.dma(sp, lambda e: e.dma_start(
                                out=dstT[r0 + j * 128:r0 + (j + 1) * 128, tg * 512:(tg + 1) * 512],
                                in_=obb[oi][:, :]), reads=[d_obb[oi]], writes=[dd])
                        feat_major(b, j * 128, hT, d_hT[tg * 4 + 3], tg * 512, 512, h)
            if which in (0, 1):
                dstTs, dds = (s_qTs, d_qTs) if which == 0 else (s_kTs, d_kTs)
                for j in range(2):
                    def hsq(pa, dp, j=j):
                        oi = next_obb()
                        if which == 0:
                            K.op(act, lambda e: e.mul(out=obb[oi][:, :NS_OWN], in_=pa, mul=ATTN_SCALE),
                                 reads=[dp], writes=[d_obb[oi]])
                        else:
                            copy_op(evac_eng(), obb[oi][:, :NS_OWN], pa, [dp], [d_obb[oi]])
                        K.dma(sp, lambda e: e.dma_start(out=dstTs[r0 + j * 128:r0 + (j + 1) * 128, :],
                                                        in_=obb[oi][:, :NS_OWN]), reads=[d_obb[oi]], writes=[dds])
                    feat_major(b, j * 128, hTo, d_hTo, 0, NS_OWN, hsq)
            if which in (1, 2, 3):
                def hso(pa, dp):
                    if which == 1:
                        to_dram_f32(pa, dp, NS_OWN, ncols, [(o_ks[:, r0:r0 + ncols], d_oks)])
                    elif which == 2:
                        oi = to_dram_f32(pa, dp, NS_OWN, ncols, [(o_vs[:, r0:r0 + ncols], d_ovs)])
                        bi2 = next_obb()
                        K.op(pool, lambda e: e.tensor_copy(out=obb[bi2][:NS_OWN, :ncols], in_=ob[oi][:NS_OWN, :ncols]),
                             reads=[d_ob[oi]], writes=[d_obb[bi2]])
                        K.dma(sp, lambda e: e.dma_start(out=s_vs[:, r0:r0 + ncols], in_=obb[bi2][:NS_OWN, :ncols]),
                              reads=[d_obb[bi2]], writes=[d_vs])
                    else:
                        bi2 = next_obb()
                        K.op(act, lambda e: e.activation(out=obb[bi2][:NS_OWN, :ncols], in_=pa, func=AF.Silu),
                             reads=[dp], writes=[d_obb[bi2]])
                        K.dma(sp, lambda e: e.dma_start(out=s_ags[:, r0:r0 + ncols], in_=obb[bi2][:NS_OWN, :ncols]),
                              reads=[d_obb[bi2]], writes=[d_ags])
                tok_major(b, 0, ncols, hTo, d_hTo, 0, NS_OWN, hso)
                for t in range(NT):
                    def h(pa, dp, t=t):
                        rows = slice(t * 128, (t + 1) * 128)
                        if which == 1:
                            to_dram_f32(pa, dp, 128, ncols, [(o_kp[rows, r0:r0 + ncols], d_okp)])
                        elif which == 2:
                            oi = to_dram_f32(pa, dp, 128, ncols, [(o_vp[rows, r0:r0 + ncols], d_ovp)])
                            bi2 = next_obb()
                            K.op(pool, lambda e: e.tensor_copy(out=obb[bi2][:, :ncols], in_=ob[oi][:, :ncols]),
                                 reads=[d_ob[oi]], writes=[d_obb[bi2]])
                            K.dma(sp, lambda e: e.dma_start(out=s_v[rows, r0:r0 + ncols],
                                                            in_=obb[bi2][:, :ncols]),
                                  reads=[d_obb[bi2]], writes=[d_v])
                        else:
                            bi2 = next_obb()
                            K.op(act, lambda e: e.activation(out=obb[bi2][:, :ncols], in_=pa, func=AF.Silu),
                                 reads=[dp], writes=[d_obb[bi2]])
                            K.dma(sp, lambda e: e.dma_start(out=s_ag[rows, r0:r0 + ncols],
                                                            in_=obb[bi2][:, :ncols]),
                                  reads=[d_obb[bi2]], writes=[d_ag])
                    tok_major(b, 0, ncols, hT, d_hT[t], t * 128, 128, h)
        elif kind == "main":
            g0 = c0 - G0
            for j in range(ncols // 128):
                for tg in range(4):
                    def h(pa, dp, j=j, tg=tg):
                        oi = next_obb()
                        K.op(act, lambda e: e.activation(out=obb[oi][:, :], in_=pa, func=AF.Sigmoid),
                             reads=[dp], writes=[d_obb[oi]])
                        K.dma(sp, lambda e: e.dma_start(
                            out=s_sg[g0 + j * 128:g0 + (j + 1) * 128, tg * 512:(tg + 1) * 512],
                            in_=obb[oi][:, :]), reads=[d_obb[oi]], writes=[d_sg])
                    feat_major(b, j * 128, hT, d_hT[tg * 4 + 3], tg * 512, 512, h)

                def hs(pa, dp, j=j):
                    oi = next_obb()
                    K.op(act, lambda e: e.activation(out=obb[oi][:, :NS_OWN], in_=pa, func=AF.Sigmoid),
                         reads=[dp], writes=[d_obb[oi]])
                    K.dma(sp, lambda e: e.dma_start(out=s_sgs[g0 + j * 128:g0 + (j + 1) * 128, :],
                                                    in_=obb[oi][:, :NS_OWN]),
                          reads=[d_obb[oi]], writes=[d_sgs])
                feat_major(b, j * 128, hTo, d_hTo, 0, NS_OWN, hs)

    K.dma(sp, lambda e: e.dma_start(out=o_shs[:, :],
                                    in_=s_prws.rearrange("(s t) n -> s t n", t=4)[:, 3, :]),
          reads=[d_prws], writes=[d_oshs])

    K.barrier()
    phA.close()
    K.cur = None
    LL = dict(locals())
    g1 = _phase_attn(nc, K, LL)
    g2 = _phase_sattn(nc, K, LL)
    next(g1)
    next(g2)
    for _i in range(16):
        next(g1)
        next(g2)
    for _g in (g1, g2):
        for _ in _g:
            pass
    K.barrier()
    LL["_ph_sattn"].close()
    LL["_ph_attn"].close()
    K.cur = None
    _phase_rwkv(nc, K, LL)
    _phase_final(nc, K, LL)
    if dbg:
        d_dbg = Dep()
        out_deps.append(d_dbg)
        for nm, src, dd, shp in (("dbg_atTs", s_atTs, d_atTs, [1024, NS_OWN]), ("dbg_rwTs", s_rwTs, d_rwTs, [1024, NS_OWN]),
                                 ("dbg_qTs", s_qTs, d_qTs, [1024, NS_OWN]), ("dbg_ags", s_ags, d_ags, [NS_OWN, 1024])):
            o_ = _dram_out(nc, nm, shp, BF16)
            K.dma(sp, lambda e: e.dma_start(out=o_[:, :], in_=src[:, :]), reads=[dd], writes=[d_dbg])
    K.finish(out_deps + [d_oyp, d_oys, d_owkvp, d_owkvs, d_prw, d_prws, d_qT, d_kT, d_v, d_ag, d_sg, d_sgs])
    print("instructions:", K.n_ins)


def _phase_attn(nc, K, L):
    pe, dve, act, pool, sp = K.pe, K.dve, K.act, K.pool, K.sp
    ps, dps = L["ps"], L["dps"]
    ident_b, d_ident = L["ident_b"], L["d_ident"]
    eps_t, d_eps = L["eps_t"], L["d_eps"]
    s_qT, s_kT, s_v, s_ag, s_atT = L["s_qT"], L["s_kT"], L["s_v"], L["s_ag"], L["s_atT"]
    d_qT, d_kT, d_v, d_ag, d_atT = L["d_qT"], L["d_kT"], L["d_v"], L["d_ag"], L["d_atT"]
    trif = K.sb("trif", [128, 128], F32)
    trib = K.sb("trib", [128, 128], BF16)
    d_tri = Dep()
    K.dma(sp, lambda e: e.dma_start(out=trif[:], in_=L["trimask_d"][:, :]), writes=[d_tri])
    K.op(dve, lambda e: e.tensor_copy(out=trib[:], in_=trif[:]), reads=[d_tri], writes=[d_tri])
    lamt = K.sb("lamt", [128, 256], F32)
    lamw = K.sb("lamw", [128, 136], F32)
    d_lam = Dep()
    K.dma(sp, lambda e: e.dma_start(out=lamt[:], in_=L["lam4_d"][0:1, :].partition_broadcast(128)), writes=[d_lam])
    lv = lamt[:].rearrange("p (a b n) -> p a b n", a=2, b=2)
    K.op(dve, lambda e: e.tensor_tensor(out=lamw[:, 0:128].rearrange("p (a n) -> p a n", a=2),
                                        in0=lv[:, :, 0, :], in1=lv[:, :, 1, :], op=ALU.mult),
         reads=[d_lam], writes=[d_lam])
    K.op(dve, lambda e: e.reduce_sum(out=lamw[:, 128:130], in_=lamw[:, 0:128].rearrange("p (a n) -> p a n", a=2),
                                     axis=AX.X), reads=[d_lam], writes=[d_lam])
    K.op(act, lambda e: e.activation(out=lamw[:, 130:132], in_=lamw[:, 128:130], func=AF.Exp),
         reads=[d_lam], writes=[d_lam])
    K.op(dve, lambda e: e.tensor_tensor(out=lamw[:, 132:133], in0=lamw[:, 130:131], in1=lamw[:, 131:132],
                                        op=ALU.subtract), reads=[d_lam], writes=[d_lam])
    K.op(dve, lambda e: e.tensor_scalar(out=lamw[:, 133:134], in0=lamw[:, 132:133], scalar1=LAM_INIT, scalar2=-1.0,
                                        op0=ALU.add, op1=ALU.mult), reads=[d_lam], writes=[d_lam])
    neglam = lamw[:, 133:134]
    subw = K.sb("subw", [128, 128], F32)
    d_subw = Dep()
    K.dma(sp, lambda e: e.dma_start(out=subw[:], in_=L["subln_d"][0:1, :].partition_broadcast(128)), writes=[d_subw])
    K.op(dve, lambda e: e.tensor_scalar(out=subw[:], in0=subw[:], scalar1=1.0 - LAM_INIT, scalar2=None,
                                        op0=ALU.mult), reads=[d_subw], writes=[d_subw])
    L["neglam"], L["d_lam"], L["subw"], L["d_subw"], L["trib"], L["d_tri"] = neglam, d_lam, subw, d_subw, trib, d_tri

    ph = contextlib.ExitStack()
    K.cur = ph
    QT = [K.sb("QT%d" % i, [128, T], BF16) for i in range(2)]
    KT = [K.sb("KT%d" % i, [128, T], BF16) for i in range(2)]
    VA = [K.sb("VA%d" % i, [128, NT, 130], BF16) for i in range(2)]
    SAG = [K.sb("SAG%d" % i, [128, NT, 128], BF16) for i in range(2)]
    ATT = [K.sb("ATT%d" % i, [128, T], BF16) for i in range(2)]
    d_in = [Dep(), Dep()]
    d_att = [Dep(), Dep()]
    fin = [K.sb("fin%d" % i, [128, 264], F32) for i in range(2)]
    d_fin = [Dep(), Dep()]
    fb = [K.sb("fb%d" % i, [128, 128], BF16) for i in range(2)]
    d_fb = [Dep(), Dep()]
    junk = K.sb("junk", [128, 128], BF16)
    d_junk = Dep()
    for i in range(2):
        K.op(pool, lambda e: e.memset(VA[i][:, :, 128:130], 1.0), writes=[d_in[i]])

    def load_head(h):
        b = h % 2
        rows = slice(h * 128, (h + 1) * 128)
        K.dma(sp, lambda e: e.dma_start(out=QT[b][:], in_=s_qT[rows, :]), reads=[d_qT], writes=[d_in[b]])
        K.dma(sp, lambda e: e.dma_start(out=KT[b][:], in_=s_kT[rows, :]), reads=[d_kT], writes=[d_in[b]])
        K.dma(sp, lambda e: e.dma_start(out=VA[b][:, :, 0:128],
                                        in_=s_v.rearrange("(t p) n -> p t n", p=128)[:, :, rows]),
              reads=[d_v], writes=[d_in[b]])
        K.dma(sp, lambda e: e.dma_start(out=SAG[b][:],
                                        in_=s_ag.rearrange("(t p) n -> p t n", p=128)[:, :, rows]),
              reads=[d_ag], writes=[d_in[b]])

    pT4 = [K.sb("pTq%d" % i, [128, 512], BF16) for i in range(4)]
    d_pT4 = [Dep() for _ in range(4)]
    accs = [K.sb("accs%d" % i, [128, 2, 130], F32) for i in range(2)]
    d_accs = [Dep(), Dep()]
    K.cur = None
    L["_ph_attn"] = ph
    yield
    cnt = {"g": 0, "q": 0}
    load_head(0)
    for h in range(8):
        b = h % 2
        if h + 1 < 8:
            load_head(h + 1)
        groups = []
        for qt in range(NT):
            for g0 in range(0, qt + 1, 4):
                groups.append((qt, g0, min(4, qt + 1 - g0)))

        def emit_S(gidx, qt, g0, nk):
            for c in range(2):
                prt = slice(c * 64, (c + 1) * 64)
                sb_ = 2 + 2 * c + gidx % 2
                for j in range(nk):
                    kt = g0 + j
                    K.op(pe, lambda e: e.matmul(ps[sb_][:, j * 128:(j + 1) * 128], lhsT=KT[b][prt, kt * 128:(kt + 1) * 128],
                                                rhs=QT[b][prt, qt * 128:(qt + 1) * 128], start=True, stop=True),
                         reads=[d_in[b]], writes=[dps[sb_]])

        def emit_rest(gidx, qt, g0, nk):
            for c in range(2):
                sb_ = 2 + 2 * c + gidx % 2
                pi = 2 * c + gidx % 2
                K.op(act, lambda e: e.activation(out=pT4[pi][:, :nk * 128], in_=ps[sb_][:, :nk * 128], func=AF.Exp),
                     reads=[dps[sb_]], writes=[d_pT4[pi]])
            if g0 + nk - 1 == qt:
                for c in range(2):
                    pi = 2 * c + gidx % 2
                    j = nk - 1
                    K.op(dve, lambda e: e.tensor_tensor(out=pT4[pi][:, j * 128:(j + 1) * 128],
                                                        in0=pT4[pi][:, j * 128:(j + 1) * 128], in1=trib[:], op=ALU.mult),
                         reads=[d_pT4[pi], d_tri], writes=[d_pT4[pi]])
            for c in range(2):
                pi = 2 * c + gidx % 2
                pa = 6 + c
                for j in range(nk):
                    kt = g0 + j
                    K.op(pe, lambda e: e.matmul(ps[pa][:, 0:129], lhsT=pT4[pi][:, j * 128:(j + 1) * 128],
                                                rhs=VA[b][:, kt, 0:129], start=(kt == 0), stop=(kt == qt)),
                         reads=[d_pT4[pi], d_in[b]], writes=[dps[pa]])

        def finalize(qt):
            qi = cnt["q"] % 2
            cnt["q"] += 1
            A_ = accs[qi]
            dA = d_accs[qi]
            K.op(dve, lambda e: e.tensor_copy(out=A_[:, 0, 0:129], in_=ps[6][:, 0:129]), reads=[dps[6]], writes=[dA])
            K.op(dve, lambda e: e.tensor_copy(out=A_[:, 1, 0:129], in_=ps[7][:, 0:129]), reads=[dps[7]], writes=[dA])
            f = fin[qi]
            df = d_fin[qi]
            K.op(dve, lambda e: e.reciprocal(out=f[:, 256:258], in_=A_[:, :, 128]), reads=[dA], writes=[df])
            K.op(dve, lambda e: e.tensor_tensor(out=f[:, 257:258], in0=f[:, 257:258], in1=neglam, op=ALU.mult),
                 reads=[df, d_lam], writes=[df])
            K.op(dve, lambda e: e.tensor_scalar(out=f[:, 0:128], in0=A_[:, 0, 0:128], scalar1=f[:, 256:257], scalar2=None,
                                                op0=ALU.mult), reads=[dA, df], writes=[df])
            K.op(dve, lambda e: e.scalar_tensor_tensor(out=f[:, 128:256], in0=A_[:, 1, 0:128], scalar=f[:, 257:258],
                                                       in1=f[:, 0:128], op0=ALU.mult, op1=ALU.add),
                 reads=[dA, df], writes=[df])
            K.op(dve, lambda e: e.tensor_tensor(out=f[:, 0:128], in0=f[:, 128:256], in1=f[:, 128:256], op=ALU.mult),
                 reads=[df], writes=[df])
            K.op(dve, lambda e: e.reduce_sum(out=f[:, 258:259], in_=f[:, 0:128], axis=AX.X), reads=[df], writes=[df])
            K.op(act, lambda e: e.activation(out=f[:, 259:260], in_=f[:, 258:259], func=AF.Ln, scale=1.0 / 128,
                                             bias=eps_t[:, 0:1]), reads=[df, d_eps], writes=[df])
            K.op(act, lambda e: e.activation(out=f[:, 259:260], in_=f[:, 259:260], func=AF.Exp, scale=-0.5),
                 reads=[df], writes=[df])
            K.op(dve, lambda e: e.scalar_tensor_tensor(out=f[:, 0:128], in0=f[:, 128:256], scalar=f[:, 259:260],
                                                       in1=L["subw"][:], op0=ALU.mult, op1=ALU.mult),
                 reads=[df, d_subw], writes=[df])
            K.op(dve, lambda e: e.tensor_tensor(out=fb[qi][:], in0=f[:, 0:128], in1=SAG[b][:, qt, :], op=ALU.mult),
                 reads=[df, d_in[b]], writes=[d_fb[qi]])
            ptv = ps[0][:].bitcast(BF16)
            K.op(pe, lambda e: e.transpose(out=ptv[:, 0:128], in_=fb[qi][:], identity=ident_b[:]),
                 reads=[d_fb[qi], d_ident], writes=[dps[0]])
            K.op(dve, lambda e: e.tensor_copy(out=ATT[b][:, qt * 128:(qt + 1) * 128], in_=ptv[:, 0:128]),
                 reads=[dps[0]], writes=[d_att[b]])

        base = cnt["g"]
        s_done = [False] * (len(groups) + 1)
        for i, (qt, g0, nk) in enumerate(groups):
            if not s_done[i]:
                emit_S(base + i, qt, g0, nk)
                s_done[i] = True
            last_of_qt = (g0 + nk - 1 == qt)
            will_yield = last_of_qt and qt == 10
            if i + 1 < len(groups) and not will_yield:
                emit_S(base + i + 1, *groups[i + 1])
                s_done[i + 1] = True
            emit_rest(base + i, qt, g0, nk)
            if last_of_qt:
                finalize(qt)
                if will_yield:
                    yield
        cnt["g"] += len(groups)
        K.dma(sp, lambda e: e.dma_start(out=s_atT[h * 128:(h + 1) * 128, :], in_=ATT[b][:]),
              reads=[d_att[b]], writes=[d_atT])
        yield


def _phase_rwkv(nc, K, L):
    pe, dve, act, pool, sp = K.pe, K.dve, K.act, K.pool, K.sp
    ps, dps = L["ps"], L["dps"]
    ident_b, ident_f, d_ident = L["ident_b"], L["ident_f"], L["d_ident"]
    eps_t, d_eps = L["eps_t"], L["d_eps"]
    s_prw, d_prw, s_rwT, d_rwT = L["s_prw"], L["d_prw"], L["s_rwT"], L["d_rwT"]
    o_wkvp, d_owkvp = L["o_wkvp"], L["d_owkvp"]
    ph = contextlib.ExitStack()
    K.cur = ph
    H3 = lambda ap: ap.rearrange("p (h c) -> p h c", c=64)

    def TT(eng, out, in0, in1, op, reads, writes):
        K.op(eng, lambda e: e.tensor_tensor(out=out, in0=in0, in1=in1, op=op), reads=reads, writes=writes)

    cst = K.sb("rw_cst", [128, 8, 1024], F32)
    d_cst = Dep()
    for i in range(8):
        K.dma(sp, lambda e: e.dma_start(out=cst[:, i, :], in_=L["rwp_d"][i:i + 1, :].partition_broadcast(128)),
              writes=[d_cst])
    w0b, a0b, kkb, kab, omka, lnxw, lnxb, rkb = (cst[:, i, :] for i in range(8))
    K.op(dve, lambda e: e.tensor_scalar(out=omka, in0=omka, scalar1=-1.0, scalar2=1.0, op0=ALU.mult, op1=ALU.add),
         reads=[d_cst], writes=[d_cst])
    mub = K.sb("rw_mub", [128, RW_IN], F32)
    K.dma(sp, lambda e: e.dma_start(out=mub[:], in_=L["mu_d"][0:1, :].partition_broadcast(128)), writes=[d_cst])
    w2a2 = K.sb("rw_w2a2", [96, 2, 1024], F32)
    K.dma(sp, lambda e: e.dma_start(out=w2a2[:, 0, :], in_=L["w2_d"][:, :]), writes=[d_cst])
    K.dma(sp, lambda e: e.dma_start(out=w2a2[:, 1, :], in_=L["a2_d"][:, :]), writes=[d_cst])
    mask4 = K.sb("rw_mask4", [128, 512], BF16)
    maskT = K.sb("rw_maskT", [128, 128], BF16)
    umat = K.sb("rw_umat", [128, 128], F32)
    ones2 = K.sb("rw_ones2", [128, 2], F32)
    K.op(dve, lambda e: e.memset(ones2[:], 1.0), writes=[d_cst])

    pc = K.sb("rw_pc", [128, RW_IN], F32); d_pc = Dep()
    pp = K.sb("rw_pp", [128, RW_IN], F32); d_pp = Dep(); d_ppb = Dep()
    lw = K.sb("rw_lw", [128, 192], F32); d_lw = Dep()
    lwT = K.sb("rw_lwT", [96, 256], F32); d_lwT = Dep()
    logd = K.sb("rw_logd", [128, 1024], F32); d_logd = Dep()
    aa = K.sb("rw_aa", [128, 1024], F32); d_aa = Dep()
    kk = K.sb("rw_kk", [128, 1024], F32); d_kk = Dep()
    kp = K.sb("rw_kp", [128, 1024], F32); d_kp = Dep()
    bb = K.sb("rw_bb", [128, 1024], F32); d_bb = Dep()
    bon = K.sb("rw_bon", [128, 1024], F32); d_bon = Dep()
    st16 = K.sb("rw_st16", [128, 64], F32); d_st16 = Dep()
    pcv = K.sb("rw_pcv", [128, 16], F32); d_pcv = Dep()
    t1, t2, eneg, epos = (pc[:, i * 1024:(i + 1) * 1024] for i in range(4))
    ys = K.sb("rw_ys", [128, 1024], F32); d_ys = Dep()
    rwo = K.sb("rw_rwo", [128, 1024], BF16); d_rwo = Dep()
    rwF = K.sb("rw_rwF", [128, 8, 128], BF16); d_rwF = Dep()
    ph2 = contextlib.ExitStack()
    K.cur = ph2
    mk_f = K.sb("rw_mkf", [128, 768], F32)
    K.dma(sp, lambda e: e.dma_start(out=mk_f[:, 0:512], in_=L["mask4_d"][:, :]), writes=[d_cst])
    K.dma(sp, lambda e: e.dma_start(out=mk_f[:, 512:640], in_=L["maskT_d"][:, :]), writes=[d_cst])
    K.dma(sp, lambda e: e.dma_start(out=umat[:], in_=L["umat_d"][:, :]), writes=[d_cst])
    K.op(dve, lambda e: e.tensor_copy(out=mask4[:], in_=mk_f[:, 0:512]), reads=[d_cst], writes=[d_cst])
    K.op(dve, lambda e: e.tensor_copy(out=maskT[:], in_=mk_f[:, 512:640]), reads=[d_cst], writes=[d_cst])
    khT = K.sb("rw_khT", [128, 1024], BF16)
    nbhT = K.sb("rw_nbhT", [128, 1024], BF16)
    kktT = K.sb("rw_kktT", [128, 1024], BF16)
    rtT = K.sb("rw_rtT", [128, 1024], BF16)
    vT = K.sb("rw_vT", [128, 1024], BF16)
    d_tm = Dep()
    khF = K.sb("rw_khF", [128, 8, 128], BF16)
    nbhF = K.sb("rw_nbhF", [128, 8, 128], BF16)
    qrF = K.sb("rw_qrF", [128, 8, 2, 128], BF16)
    d_fm = Dep()
    MM = [K.sb("rw_MM%d" % h, [128, 512], BF16) for h in range(16)]
    NN = [K.sb("rw_NN%d" % h, [128, 256], BF16) for h in range(16)]
    XX = [K.sb("rw_XX%d" % h, [128, 128], BF16) for h in range(16)]
    d_MM = [Dep() for _ in range(16)]
    d_NN = [Dep() for _ in range(16)]
    d_XX = [Dep() for _ in range(16)]
    RT = K.sb("rw_RT", [128, 1024], BF16); d_RT = [Dep(), Dep()]
    WT = K.sb("rw_WT", [128, 1024], BF16); d_WT = [Dep(), Dep()]
    ST = K.sb("rw_ST", [128, 8, 64], F32)
    STs = K.sb("rw_STs", [128, 8, 64], F32)
    STb = K.sb("rw_STb", [128, 8, 64], BF16)
    d_ST = Dep(); d_STs = Dep(); d_STb = Dep()
    sto = K.sb("rw_sto", [64, 8, 128], F32); d_sto = Dep()
    K.cur = ph
    K.op(dve, lambda e: e.memset(ST[:], 0.0), writes=[d_ST])
    K.op(dve, lambda e: e.memset(STb[:], 0.0), writes=[d_STb])

    cnt = {"ps": 0, "ev": 0}

    def next_ps():
        i = 2 + cnt["ps"] % 6
        cnt["ps"] += 1
        return i

    def ev_eng():
        cnt["ev"] += 1
        return act if cnt["ev"] % 2 == 0 else dve

    def copy_op(eng, out, in_, reads, writes):
        if eng is act:
            K.op(act, lambda e: e.copy(out=out, in_=in_), reads=reads, writes=writes)
        else:
            K.op(eng, lambda e: e.tensor_copy(out=out, in_=in_), reads=reads, writes=writes)

    r_, k_, v_, g_ = (pp[:, i * 1024:(i + 1) * 1024] for i in range(4))
    def stepA(ci, is_s):
        r0 = 0 if is_s else ci * 128
        if is_s:
            K.op(pool, lambda e: e.memset(pc[:], 0.0), writes=[d_pc])
            K.op(pool, lambda e: e.memset(pp[:], 0.0), writes=[d_pp, d_ppb])
            K.dma(sp, lambda e: e.dma_start(out=pc[:NS_OWN, :], in_=L["s_prws"][:, :]), reads=[L["d_prws"]], writes=[d_pc])
            K.dma(sp, lambda e: e.dma_start(out=pp[1:NS_OWN, :], in_=L["s_prws"][0:NS_OWN - 1, :]), reads=[L["d_prws"]],
                  writes=[d_pp, d_ppb])
            for sq_ in range(16):
                K.dma(sp, lambda e: e.dma_start(out=pp[4 * sq_:4 * sq_ + 1, :], in_=L["shift_d"][sq_:sq_ + 1, :]),
                      writes=[d_pp, d_ppb])
        elif ci == 0:
            K.dma(sp, lambda e: e.dma_start(out=pc[:], in_=s_prw[r0:r0 + 128, :]), reads=[d_prw], writes=[d_pc])
            K.op(pool, lambda e: e.memset(pp[0:1, :], 0.0), writes=[d_pp, d_ppb])
            K.dma(sp, lambda e: e.dma_start(out=pp[1:128, :], in_=s_prw[0:127, :]), reads=[d_prw], writes=[d_pp, d_ppb])
        else:
            K.dma(sp, lambda e: e.dma_start(out=pc[:], in_=s_prw[r0:r0 + 128, :]), reads=[d_prw], writes=[d_pc])
            K.dma(sp, lambda e: e.dma_start(out=pp[:], in_=s_prw[r0 - 1:r0 + 127, :]), reads=[d_prw], writes=[d_pp, d_ppb])
        ca, cb_ = slice(0, 3072), slice(3072, RW_IN)
        for eng_, cs_, dd_ in ((dve, ca, d_pp), (pool, cb_, d_ppb)):
            TT(eng_, pp[:, cs_], pp[:, cs_], pc[:, cs_], ALU.subtract, [dd_, d_pc], [dd_])
            TT(eng_, pp[:, cs_], pp[:, cs_], mub[:, cs_], ALU.mult, [dd_, d_cst], [dd_])
            TT(eng_, pp[:, cs_], pp[:, cs_], pc[:, cs_], ALU.add, [dd_, d_pc], [dd_])
        K.op(act, lambda e: e.activation(out=lw[:, 0:96], in_=pp[:, 4096:4192], func=AF.Tanh), reads=[d_ppb], writes=[d_lw])
        K.op(dve, lambda e: e.tensor_copy(out=lw[:, 96:192], in_=pp[:, 4192:4288]), reads=[d_ppb], writes=[d_lw])
        for j in range(2):
            K.op(pe, lambda e: e.transpose(out=ps[0][:96, j * 128:(j + 1) * 128], in_=lw[:, j * 96:(j + 1) * 96],
                                           identity=ident_f[:]), reads=[d_lw, d_ident], writes=[dps[0]])
        K.op(dve, lambda e: e.tensor_copy(out=lwT[:, :], in_=ps[0][:96, 0:256]), reads=[dps[0]], writes=[d_lwT])
        for which, dst, dd, cb in ((0, logd, d_logd, w0b), (1, aa, d_aa, a0b)):
            for hf in range(2):
                pi = next_ps()
                K.op(pe, lambda e: e.matmul(ps[pi][:, :], lhsT=lwT[:, which * 128:(which + 1) * 128],
                                            rhs=w2a2[:, which, hf * 512:(hf + 1) * 512], start=True, stop=True),
                     reads=[d_lwT, d_cst], writes=[dps[pi]])
                TT(dve, dst[:, hf * 512:(hf + 1) * 512], ps[pi][:, :], cb[:, hf * 512:(hf + 1) * 512], ALU.add,
                   [dps[pi], d_cst], [dd])
            K.op(act, lambda e: e.activation(out=dst[:], in_=dst[:], func=AF.Sigmoid), reads=[dd], writes=[dd])
        K.op(pool, lambda e: e.tensor_scalar(out=logd[:], in0=logd[:], scalar1=-0.6065306597126334, scalar2=None,
                                             op0=ALU.mult), reads=[d_logd], writes=[d_logd])
        TT(pool, kk[:], k_, kkb, ALU.mult, [d_pp, d_cst], [d_kk])
        TT(pool, t1, kk[:], kk[:], ALU.mult, [d_kk], [d_pc])
        K.op(dve, lambda e: e.reduce_sum(out=st16[:, 0:16], in_=H3(t1), axis=AX.X), reads=[d_pc], writes=[d_st16])
        K.op(act, lambda e: e.activation(out=st16[:, 0:16], in_=st16[:, 0:16], func=AF.Sqrt), reads=[d_st16], writes=[d_st16])
        K.op(dve, lambda e: e.tensor_scalar(out=st16[:, 0:16], in0=st16[:, 0:16], scalar1=1e-12, scalar2=None,
                                            op0=ALU.max), reads=[d_st16], writes=[d_st16])
        K.op(dve, lambda e: e.reciprocal(out=st16[:, 0:16], in_=st16[:, 0:16]), reads=[d_st16], writes=[d_st16])
        TT(dve, H3(kk[:]), H3(kk[:]), st16[:, 0:16].unsqueeze(2).to_broadcast([128, 16, 64]), ALU.mult,
           [d_kk, d_st16], [d_kk])
        TT(pool, t1, aa[:], kab, ALU.mult, [d_aa, d_cst], [d_pc])
        TT(pool, t1, t1, omka, ALU.add, [d_pc, d_cst], [d_pc])
        TT(dve, kp[:], k_, t1, ALU.mult, [d_pp, d_pc], [d_kp])
        TT(pool, bb[:], kk[:], aa[:], ALU.mult, [d_kk, d_aa], [d_bb])
        TT(dve, t2, r_, kp[:], ALU.mult, [d_pp, d_kp], [d_pc])
        TT(pool, t2, t2, rkb, ALU.mult, [d_pc, d_cst], [d_pc])
        K.op(dve, lambda e: e.reduce_sum(out=st16[:, 16:32], in_=H3(t2), axis=AX.X), reads=[d_pc], writes=[d_st16])
        TT(dve, H3(bon[:]), H3(v_), st16[:, 16:32].unsqueeze(2).to_broadcast([128, 16, 64]), ALU.mult,
           [d_pp, d_st16], [d_bon])
    def scan(ci):
        r0 = ci * 128
        cps = []
        for hf in range(2):
            pi = next_ps()
            cps.append(pi)
            K.op(pe, lambda e: e.matmul(ps[pi][:, :], lhsT=umat[:], rhs=logd[:, hf * 512:(hf + 1) * 512],
                                        start=True, stop=True), reads=[d_logd, d_cst], writes=[dps[pi]])
        for hf in range(2):
            pi = cps[hf]
            sl = slice(hf * 512, (hf + 1) * 512)
            K.op(act, lambda e: e.activation(out=eneg[:, sl], in_=ps[pi][:, :], func=AF.Exp, scale=-1.0),
                 reads=[dps[pi]], writes=[d_pc])
            K.op(act, lambda e: e.activation(out=epos[:, sl], in_=ps[pi][:, :], func=AF.Exp), reads=[dps[pi]], writes=[d_pc])
            TT(dve, t1[:, sl], ps[pi][:, :], logd[:, sl], ALU.subtract, [dps[pi], d_logd], [d_pc])
        K.op(act, lambda e: e.activation(out=t1, in_=t1, func=AF.Exp), reads=[d_pc], writes=[d_pc])
        pi = next_ps()
        for g in range(8):
            K.op(pe, lambda e: e.matmul(ps[pi][:, g * 2:g * 2 + 2], lhsT=logd[:, g * 128:(g + 1) * 128], rhs=ones2[:],
                                        start=True, stop=True), reads=[d_logd, d_cst], writes=[dps[pi]])
        K.op(act, lambda e: e.activation(out=pcv[:, 0:16], in_=ps[pi][:, 0:16], func=AF.Exp), reads=[dps[pi]], writes=[d_pcv])
        TT(dve, khT[:], kp[:], eneg, ALU.mult, [d_kp, d_pc], [d_tm])
        K.op(dve, lambda e: e.scalar_tensor_tensor(out=nbhT[:], in0=bb[:], scalar=-1.0, in1=eneg, op0=ALU.mult,
                                                   op1=ALU.mult), reads=[d_bb, d_pc], writes=[d_tm])
        TT(pool, kktT[:], kk[:], t1, ALU.mult, [d_kk, d_pc], [d_tm])
        TT(pool, rtT[:], r_, epos, ALU.mult, [d_pp, d_pc], [d_tm])
        K.op(act, lambda e: e.copy(out=vT[:], in_=v_), reads=[d_pp], writes=[d_tm])
        for src, dstv in ((khT, khF[:, :, :]), (nbhT, nbhF[:, :, :]), (kktT, qrF[:, :, 0, :]), (rtT, qrF[:, :, 1, :])):
            tb = cnt["ps"] % 2
            cnt["ps"] += 1
            ptv = ps[tb][:].bitcast(BF16)
            for g in range(8):
                K.op(pe, lambda e: e.transpose(out=ptv[:, g * 128:(g + 1) * 128], in_=src[:, g * 128:(g + 1) * 128],
                                               identity=ident_b[:]), reads=[d_tm, d_ident], writes=[dps[tb]])
            copy_op(ev_eng(), dstv, ptv[:, :].rearrange("p (g t) -> p g t", g=8), [dps[tb]], [d_fm])
        for h in range(16):
            g = h // 2
            prt = slice((h % 2) * 64, (h % 2) * 64 + 64)
            pi = next_ps()
            K.op(pe, lambda e: e.matmul(ps[pi][:, 0:256], lhsT=khF[prt, g, :], rhs=qrF[prt, g, :, :],
                                        start=True, stop=True), reads=[d_fm], writes=[dps[pi]])
            K.op(pe, lambda e: e.matmul(ps[pi][:, 256:512], lhsT=nbhF[prt, g, :], rhs=qrF[prt, g, :, :],
                                        start=True, stop=True), reads=[d_fm], writes=[dps[pi]])
            TT(dve, MM[h][:], ps[pi][:, :], mask4[:], ALU.mult, [dps[pi], d_cst], [d_MM[h]])
            pi = next_ps()
            K.op(pe, lambda e: e.matmul(ps[pi][:, 0:128], lhsT=qrF[prt, g, 0, :], rhs=nbhF[prt, g, :],
                                        start=True, stop=True), reads=[d_fm], writes=[dps[pi]])
            TT(dve, NN[h][:, 128:256], ps[pi][:, 0:128], maskT[:], ALU.mult, [dps[pi], d_cst], [d_NN[h]])
            TT(pool, XX[h][:], ident_b[:], MM[h][:, 256:384], ALU.subtract, [d_ident, d_MM[h]], [d_XX[h]])
        for lvl in range(1, 7):
            for h in range(16):
                nsrc = MM[h][:, 256:384] if lvl == 1 else NN[h][:, 0:128]
                ntsrc = NN[h][:, 128:256]
                rd = [d_MM[h], d_NN[h]]
                pi = next_ps()
                if lvl < 6:
                    K.op(pe, lambda e: e.matmul(ps[pi][:, 0:128], lhsT=ntsrc, rhs=nsrc, start=True, stop=True),
                         reads=rd, writes=[dps[pi]])
                K.op(pe, lambda e: e.matmul(ps[pi][:, 128:256], lhsT=nsrc, rhs=ntsrc, start=True, stop=True),
                     reads=rd, writes=[dps[pi]])
                if lvl < 6:
                    copy_op(ev_eng(), NN[h][:, 0:256], ps[pi][:, 0:256], [dps[pi]], [d_NN[h]])
                else:
                    copy_op(ev_eng(), NN[h][:, 128:256], ps[pi][:, 128:256], [dps[pi]], [d_NN[h]])
            for h in range(16):
                pi = next_ps()
                K.op(pe, lambda e: e.matmul(ps[pi][:, 0:128], lhsT=NN[h][:, 128:256], rhs=XX[h][:], start=True, stop=True),
                     reads=[d_NN[h], d_XX[h]], writes=[dps[pi]])
                TT(dve, XX[h][:], ps[pi][:, 0:128], XX[h][:], ALU.add, [dps[pi], d_XX[h]], [d_XX[h]])
        TT(pool, STs[:], ST[:], pcv[:, 0:16].rearrange("p (g two) -> p g two", two=2)[:, :, 0:1].to_broadcast([128, 8, 64]),
           ALU.mult, [d_ST, d_pcv], [d_STs])
        for half in range(2):
            pi = next_ps()
            for hh in range(8):
                h = half * 8 + hh
                g = h // 2
                prt = slice((h % 2) * 64, (h % 2) * 64 + 64)
                K.op(pe, lambda e: e.matmul(ps[pi][:, hh * 64:(hh + 1) * 64], lhsT=qrF[prt, g, 0, :], rhs=STb[prt, g, :],
                                            start=True, stop=False), reads=[d_fm, d_STb], writes=[dps[pi]])
                K.op(pe, lambda e: e.matmul(ps[pi][:, hh * 64:(hh + 1) * 64], lhsT=MM[h][:, 0:128],
                                            rhs=vT[:, h * 64:(h + 1) * 64], start=False, stop=True),
                     reads=[d_MM[h], d_tm], writes=[dps[pi]])
            copy_op(ev_eng(), RT[:, half * 512:(half + 1) * 512], ps[pi][:, :], [dps[pi]], [d_RT[half]])
        for half in range(2):
            pi = next_ps()
            for hh in range(8):
                h = half * 8 + hh
                K.op(pe, lambda e: e.matmul(ps[pi][:, hh * 64:(hh + 1) * 64], lhsT=XX[h][:], rhs=RT[:, h * 64:(h + 1) * 64],
                                            start=True, stop=True), reads=[d_XX[h], d_RT[half]], writes=[dps[pi]])
            copy_op(ev_eng(), WT[:, half * 512:(half + 1) * 512], ps[pi][:, :], [dps[pi]], [d_WT[half]])
        for half in range(2):
            pi = next_ps()
            for hh in range(8):
                h = half * 8 + hh
                g = h // 2
                prt = slice((h % 2) * 64, (h % 2) * 64 + 64)
                o = ps[pi][:, hh * 64:(hh + 1) * 64]
                K.op(pe, lambda e: e.matmul(o, lhsT=qrF[prt, g, 1, :], rhs=STb[prt, g, :], start=True, stop=False),
                     reads=[d_fm, d_STb], writes=[dps[pi]])
                K.op(pe, lambda e: e.matmul(o, lhsT=MM[h][:, 128:256], rhs=vT[:, h * 64:(h + 1) * 64], start=False, stop=False),
                     reads=[d_MM[h], d_tm], writes=[dps[pi]])
                K.op(pe, lambda e: e.matmul(o, lhsT=MM[h][:, 384:512], rhs=WT[:, h * 64:(h + 1) * 64], start=False, stop=True),
                     reads=[d_MM[h], d_WT[half]], writes=[dps[pi]])
            copy_op(ev_eng(), ys[:, half * 512:(half + 1) * 512], ps[pi][:, :], [dps[pi]], [d_ys])
        for half in range(2):
            pi = next_ps()
            for gg in range(4):
                g = half * 4 + gg
                o = ps[pi][:, gg * 128:(gg + 1) * 128]
                K.op(pe, lambda e: e.matmul(o, lhsT=khT[:, g * 128:(g + 1) * 128], rhs=vT[:, g * 128:(g + 1) * 128],
                                            start=True, stop=False), reads=[d_tm], writes=[dps[pi]])
                K.op(pe, lambda e: e.matmul(o, lhsT=nbhT[:, g * 128:(g + 1) * 128], rhs=WT[:, g * 128:(g + 1) * 128],
                                            start=False, stop=True), reads=[d_tm, d_WT[0], d_WT[1]], writes=[dps[pi]])
            psv = ps[pi][:, :].rearrange("p (g hl v) -> p g hl v", g=4, hl=2)
            for hl in range(2):
                prt = slice(hl * 64, hl * 64 + 64)
                pcb = pcv[prt, 0:16].rearrange("p (g two) -> p g two", two=2)[:, half * 4:half * 4 + 4, 0:1].to_broadcast([64, 4, 64])
                TT(dve, ST[prt, half * 4:half * 4 + 4, :], psv[prt, :, hl, :], pcb, ALU.mult, [dps[pi], d_pcv], [d_ST])
                TT(dve, ST[prt, half * 4:half * 4 + 4, :], ST[prt, half * 4:half * 4 + 4, :],
                   STs[prt, half * 4:half * 4 + 4, :], ALU.add, [d_ST, d_STs], [d_ST])
        K.op(act, lambda e: e.copy(out=STb[:], in_=ST[:]), reads=[d_ST], writes=[d_STb])
    def stepD(ci, is_s):
        r0 = 0 if is_s else ci * 128
        K.op(dve, lambda e: e.reduce_sum(out=st16[:, 32:48], in_=H3(ys[:]), axis=AX.X), reads=[d_ys], writes=[d_st16])
        K.op(dve, lambda e: e.tensor_scalar(out=st16[:, 32:48], in0=st16[:, 32:48], scalar1=1.0 / 64, scalar2=None,
                                            op0=ALU.mult), reads=[d_st16], writes=[d_st16])
        TT(dve, H3(ys[:]), H3(ys[:]), st16[:, 32:48].unsqueeze(2).to_broadcast([128, 16, 64]), ALU.subtract,
           [d_ys, d_st16], [d_ys])
        TT(pool, t2, ys[:], ys[:], ALU.mult, [d_ys], [d_pc])
        K.op(dve, lambda e: e.reduce_sum(out=st16[:, 48:64], in_=H3(t2), axis=AX.X), reads=[d_pc], writes=[d_st16])
        K.op(act, lambda e: e.activation(out=st16[:, 48:64], in_=st16[:, 48:64], func=AF.Sqrt, scale=1.0 / 64,
                                         bias=eps_t[:, 1:2]), reads=[d_st16, d_eps], writes=[d_st16])
        K.op(dve, lambda e: e.reciprocal(out=st16[:, 48:64], in_=st16[:, 48:64]), reads=[d_st16], writes=[d_st16])
        TT(dve, H3(ys[:]), H3(ys[:]), st16[:, 48:64].unsqueeze(2).to_broadcast([128, 16, 64]), ALU.mult,
           [d_ys, d_st16], [d_ys])
        TT(pool, ys[:], ys[:], lnxw, ALU.mult, [d_ys, d_cst], [d_ys])
        TT(pool, ys[:], ys[:], lnxb, ALU.add, [d_ys, d_cst], [d_ys])
        TT(dve, ys[:], ys[:], bon[:], ALU.add, [d_ys, d_bon], [d_ys])
        K.op(act, lambda e: e.activation(out=t2, in_=g_, func=AF.Silu), reads=[d_ppb], writes=[d_pc])
        TT(dve, rwo[:], ys[:], t2, ALU.mult, [d_ys, d_pc], [d_rwo])
        tb = cnt["ps"] % 2
        cnt["ps"] += 1
        ptv = ps[tb][:].bitcast(BF16)
        for g in range(8):
            K.op(pe, lambda e: e.transpose(out=ptv[:, g * 128:(g + 1) * 128], in_=rwo[:, g * 128:(g + 1) * 128],
                                           identity=ident_b[:]), reads=[d_rwo, d_ident], writes=[dps[tb]])
        copy_op(ev_eng(), rwF[:, :, :], ptv[:, :].rearrange("p (g t) -> p g t", g=8), [dps[tb]], [d_rwF])
        if is_s:
            K.dma(sp, lambda e: e.dma_start(out=L["s_rwTs"].rearrange("(g p) t -> p g t", p=128), in_=rwF[:, :, 0:NS_OWN]),
                  reads=[d_rwF], writes=[L["d_rwTs"]])
        else:
            K.dma(sp, lambda e: e.dma_start(out=s_rwT.rearrange("(g p) t -> p g t", p=128)[:, :, r0:r0 + 128],
                                            in_=rwF[:, :, :]), reads=[d_rwF], writes=[d_rwT])

    for ci in range(NT):
        stepA(ci, False)
        scan(ci)
        stepD(ci, False)
    for half in range(2):
        pi = next_ps()
        for gg in range(4):
            g = half * 4 + gg
            K.op(pe, lambda e: e.transpose(out=ps[pi][:64, gg * 128:(gg + 1) * 128], in_=ST[:, g, :], identity=ident_f[:]),
                 reads=[d_ST, d_ident], writes=[dps[pi]])
        K.op(dve, lambda e: e.tensor_copy(out=sto[:, half * 4:half * 4 + 4, :],
                                          in_=ps[pi][:64, :].rearrange("p (g x) -> p g x", g=4)),
             reads=[dps[pi]], writes=[d_sto])
    K.dma(sp, lambda e: e.dma_start(out=o_wkvp.rearrange("(g hl) v c -> v g hl c", hl=2),
                                    in_=sto[:, :, :].rearrange("p g (hl c) -> p g hl c", hl=2)),
          reads=[d_sto], writes=[d_owkvp])
    K.barrier()
    ph2.close()
    ph3 = contextlib.ExitStack()
    K.cur = ph3
    s_rs, s_ysm = L["s_rs"], L["s_ysm"]
    d_rs, d_ysm = Dep(), Dep()
    SS = [K.sb("rs_SS%d" % i, [128, 4096], F32) for i in range(2)]
    TM_ = [K.sb("rs_TM%d" % i, [128, 4096], F32) for i in range(2)]
    VV = [K.sb("rs_VV%d" % i, [128, 2, 6, 64], F32) for i in range(2)]
    YY = [K.sb("rs_YY%d" % i, [128, 4, 64], F32) for i in range(2)]
    SK = [K.sb("rs_SK%d" % i, [128, 64], F32) for i in range(2)]
    d_SS = [Dep(), Dep()]; d_TM = [Dep(), Dep()]; d_VV = [Dep(), Dep()]; d_YY = [Dep(), Dep()]; d_SK = [Dep(), Dep()]
    stepA("s", True)
    K.op(act, lambda e: e.activation(out=t1, in_=logd[:], func=AF.Exp), reads=[d_logd], writes=[d_pc])
    for qi, (src, dd) in enumerate(((kk[:NS_OWN, :], d_kk), (bb[:NS_OWN, :], d_bb), (kp[:NS_OWN, :], d_kp),
                                    (t1[:NS_OWN, :], d_pc), (r_[:NS_OWN, :], d_pp), (v_[:NS_OWN, :], d_pp))):
        K.dma(sp, lambda e: e.dma_start(out=s_rs[:, qi, :], in_=src), reads=[dd], writes=[d_rs])
    wkv_v = L["wkv_d"].rearrange("s h v k -> (s h) (v k)")
    owkv_v = L["o_wkvs"].rearrange("s h v k -> (s h) (v k)")
    for grp in range(2):
        K.dma(sp, lambda e: e.dma_start(out=SS[grp][:], in_=wkv_v[grp * 128:(grp + 1) * 128, :]), writes=[d_SS[grp]])
    for t in range(4):
        for grp in range(2):
            eng = dve if grp == 0 else pool
            for sl in range(8):
                sq_ = grp * 8 + sl
                K.dma(sp, lambda e: e.dma_start(out=VV[grp][sl * 16:(sl + 1) * 16, t % 2, :, :],
                                                in_=s_rs[sq_ * 4 + t, :, :].rearrange("q (h c) -> h q c", c=64)),
                      reads=[d_rs], writes=[d_VV[grp]])
            S3 = SS[grp][:].rearrange("p (v k) -> p v k", k=64)
            T3 = TM_[grp][:].rearrange("p (v k) -> p v k", k=64)
            bk = lambda q_: VV[grp][:, t % 2, q_, :].unsqueeze(1).to_broadcast([128, 64, 64])
            bv = lambda ap: ap.unsqueeze(2).to_broadcast([128, 64, 64])
            dS, dT, dV, dY, dK = d_SS[grp], d_TM[grp], d_VV[grp], d_YY[grp], d_SK[grp]
            TT(eng, T3, S3, bk(0), ALU.mult, [dS, dV], [dT])
            K.op(dve, lambda e: e.reduce_sum(out=SK[grp][:], in_=T3, axis=AX.X), reads=[dT], writes=[dK])
            TT(eng, S3, S3, bk(3), ALU.mult, [dS, dV], [dS])
            TT(eng, T3, bv(SK[grp][:]), bk(1), ALU.mult, [dK, dV], [dT])
            TT(eng, S3, S3, T3, ALU.subtract, [dS, dT], [dS])
            TT(eng, T3, bv(VV[grp][:, t % 2, 5, :]), bk(2), ALU.mult, [dV], [dT])
            TT(eng, S3, S3, T3, ALU.add, [dS, dT], [dS])
            TT(eng, T3, S3, bk(4), ALU.mult, [dS, dV], [dT])
            K.op(dve, lambda e: e.reduce_sum(out=YY[grp][:, t, :], in_=T3, axis=AX.X), reads=[dT], writes=[dY])
    for grp in range(2):
        K.dma(sp, lambda e: e.dma_start(out=owkv_v[grp * 128:(grp + 1) * 128, :], in_=SS[grp][:]), reads=[d_SS[grp]],
              writes=[L["d_owkvs"]])
        for sl in range(8):
            sq_ = grp * 8 + sl
            K.dma(sp, lambda e: e.dma_start(out=s_ysm[sq_ * 4:(sq_ + 1) * 4, :].rearrange("t (h v) -> h t v", v=64),
                                            in_=YY[grp][sl * 16:(sl + 1) * 16, :, :]), reads=[d_YY[grp]], writes=[d_ysm])
    K.op(dve, lambda e: e.memset(ys[:], 0.0), writes=[d_ys])
    K.dma(sp, lambda e: e.dma_start(out=ys[:NS_OWN, :], in_=s_ysm[:, :]), reads=[d_ysm], writes=[d_ys])
    stepD("s", True)
    K.barrier()
    ph3.close()
    ph.close()
    K.cur = None


def _phase_final(nc, K, L):
    pe, dve, act, pool, sp = K.pe, K.dve, K.act, K.pool, K.sp
    ps, dps = L["ps"], L["dps"]
    eps_t, d_eps = L["eps_t"], L["d_eps"]
    ph = contextlib.ExitStack()
    K.cur = ph
    d_cst = Dep()
    nfb = K.sb("fn_nfb", [128, D], F32)
    K.dma(sp, lambda e: e.dma_start(out=nfb[:], in_=L["normf_d"][0:1, :].partition_broadcast(128)), writes=[d_cst])
    wo = K.sb("fn_wo", [128, KC, D], BF16)
    d_wo = Dep()
    wst = [K.sb("fn_wst%d" % i, [128, KC, 128], F32) for i in range(2)]
    d_wst = [Dep(), Dep()]
    wbr = [K.sb("fn_wbr%d" % i, [128, 2, 8, 128], BF16) for i in range(2)]
    d_wbr = [Dep(), Dep()]
    wo_v = L["w_out_d"].rearrange("(kc p) n -> p kc n", p=128)
    for i in range(16):
        b = i % 2
        K.dma(sp, lambda e: e.dma_start(out=wst[b][:, :, :], in_=wo_v[:, :, i * 128:(i + 1) * 128]), writes=[d_wst[b]])
        K.op(pool, lambda e: e.tensor_copy(out=wo[:, :, i * 128:(i + 1) * 128], in_=wst[b][:, :, :]),
             reads=[d_wst[b]], writes=[d_wo])
    wr_v = L["w_brr_d"].rearrange("(kc p) n -> p kc n", p=128)
    wa_v = L["w_bra_d"].rearrange("(kc p) n -> p kc n", p=128)
    inT = [K.sb("fn_inT%d" % i, [128, 2, 8, 512], BF16) for i in range(1)]
    d_inT = [Dep()]
    sgt = [K.sb("fn_sg%d" % i, [128, 2, 512], BF16) for i in range(3)]
    d_sgt = [Dep() for _ in range(3)]
    mT = K.sb("fn_mT", [128, KC, 512], BF16)
    d_mT = Dep()
    m12 = [K.sb("fn_m%d" % i, [128, 2, 512], F32) for i in range(2)]
    d_m12 = [Dep(), Dep()]
    xt = [K.sb("fn_xt%d" % i, [128, D], F32) for i in range(2)]
    d_xt = [Dep(), Dep()]
    sq = K.sb("fn_sq", [128, D], BF16)
    d_sq = Dep()
    stat = [K.sb("fn_stat%d" % i, [128, 2], F32) for i in range(2)]
    d_stat = [Dep(), Dep()]
    cnt = {"w": 0, "sg": 0, "m": 0, "x": 0, "br": 0}

    groups = [("p", tg, 512) for tg in range(4)] + [("s", 0, NS_OWN)]
    import os
    if os.environ.get("SKIP_S"):
        groups = groups[:4]
    for kind, tg, n in groups:
        if kind == "p":
            at_src = L["s_atT"].rearrange("(kc p) t -> p kc t", p=128)[:, :, tg * 512:(tg + 1) * 512]
            rw_src = L["s_rwT"].rearrange("(kc p) t -> p kc t", p=128)[:, :, tg * 512:(tg + 1) * 512]
            rd = [L["d_atT"], L["d_rwT"]]
            sg_src = L["s_sg"].rearrange("(a c p) t -> p a c t", a=2, p=128)[:, :, :, tg * 512:(tg + 1) * 512]
            d_sgsrc = L["d_sg"]
        else:
            at_src = L["s_atTs"].rearrange("(kc p) t -> p kc t", p=128)
            rw_src = L["s_rwTs"].rearrange("(kc p) t -> p kc t", p=128)
            rd = [L["d_atTs"], L["d_rwTs"]]
            sg_src = L["s_sgs"].rearrange("(a c p) t -> p a c t", a=2, p=128)
            d_sgsrc = L["d_sgs"]
        K.dma(sp, lambda e: e.dma_start(out=inT[0][:, 0, :, :n], in_=rw_src), reads=rd, writes=[d_inT[0]])
        K.dma(sp, lambda e: e.dma_start(out=inT[0][:, 1, :, :n], in_=at_src), reads=rd, writes=[d_inT[0]])
        for cc in range(KC):
            wb_i = cnt["w"] % 2
            cnt["w"] += 1
            K.dma(sp, lambda e: e.dma_start(out=wst[wb_i][:, 0:8, :], in_=wr_v[:, :, cc * 128:(cc + 1) * 128]),
                  writes=[d_wst[wb_i]])
            K.dma(sp, lambda e: e.dma_start(out=wst[wb_i][:, 8:16, :], in_=wa_v[:, :, cc * 128:(cc + 1) * 128]),
                  writes=[d_wst[wb_i]])
            K.op(pool, lambda e: e.tensor_copy(out=wbr[wb_i][:, :, :, :].rearrange("p a k n -> p (a k) n"),
                                               in_=wst[wb_i][:, :, :]), reads=[d_wst[wb_i]], writes=[d_wbr[wb_i]])
            si = cnt["sg"] % 3
            cnt["sg"] += 1
            K.dma(sp, lambda e: e.dma_start(out=sgt[si][:, :, :n], in_=sg_src[:, :, cc, :]), reads=[d_sgsrc],
                  writes=[d_sgt[si]])
            banks = [4 + 2 * (cnt["br"] % 2), 5 + 2 * (cnt["br"] % 2)]
            cnt["br"] += 1
            for a in range(2):
                for kc in range(8):
                    K.op(pe, lambda e: e.matmul(ps[banks[a]][:, :n], lhsT=wbr[wb_i][:, a, kc, :], rhs=inT[0][:, a, kc, :n],
                                                start=(kc == 0), stop=(kc == 7)),
                         reads=[d_wbr[wb_i], d_inT[0]], writes=[dps[banks[a]]])
            mi = cnt["m"] % 2
            cnt["m"] += 1
            for a in range(2):
                K.op(dve, lambda e: e.tensor_tensor(out=m12[mi][:, a, :n], in0=ps[banks[a]][:, :n], in1=sgt[si][:, a, :n],
                                                    op=ALU.mult), reads=[dps[banks[a]], d_sgt[si]], writes=[d_m12[mi]])
            K.op(pool, lambda e: e.tensor_tensor(out=mT[:, cc, :n], in0=m12[mi][:, 0, :n], in1=m12[mi][:, 1, :n],
                                                 op=ALU.add), reads=[d_m12[mi]], writes=[d_mT])
        ntile = 4 if kind == "p" else 1
        for tt in range(ntile):
            m = 128 if kind == "p" else NS_OWN
            xi = cnt["x"] % 2
            cnt["x"] += 1
            if kind == "p":
                rows = slice(tg * 512 + tt * 128, tg * 512 + (tt + 1) * 128)
                xsrc, ydst, dy = L["xp"][rows, :], L["o_yp"][rows, :], L["d_oyp"]
            else:
                xsrc, ydst, dy = L["xs_own"][:, :], L["o_ys"][:, :], L["d_oys"]
            K.dma(sp, lambda e: e.dma_start(out=xt[xi][:m, :], in_=xsrc), writes=[d_xt[xi]])
            for cg in range(4):
                for kc in range(KC):
                    K.op(pe, lambda e: e.matmul(ps[cg][:m, :], lhsT=mT[:, kc, tt * 128:tt * 128 + m],
                                                rhs=wo[:, kc, cg * 512:(cg + 1) * 512], start=(kc == 0), stop=(kc == KC - 1)),
                         reads=[d_mT, d_wo], writes=[dps[cg]])
                K.op(dve, lambda e: e.tensor_tensor(out=xt[xi][:m, cg * 512:(cg + 1) * 512], in0=ps[cg][:m, :],
                                                    in1=xt[xi][:m, cg * 512:(cg + 1) * 512], op=ALU.add),
                     reads=[dps[cg], d_xt[xi]], writes=[d_xt[xi]])
            K.op(act, lambda e: e.activation(out=sq[:m, :], in_=xt[xi][:m, :], func=AF.Square,
                                             accum_out=stat[xi][:m, 0:1]), reads=[d_xt[xi]], writes=[d_sq, d_stat[xi]])
            K.op(act, lambda e: e.activation(out=stat[xi][:m, 1:2], in_=stat[xi][:m, 0:1], func=AF.Sqrt, scale=1.0 / D,
                                             bias=eps_t[:m, 0:1]), reads=[d_stat[xi], d_eps], writes=[d_stat[xi]])
            K.op(dve, lambda e: e.reciprocal(out=stat[xi][:m, 1:2], in_=stat[xi][:m, 1:2]), reads=[d_stat[xi]],
                 writes=[d_stat[xi]])
            K.op(dve, lambda e: e.scalar_tensor_tensor(out=xt[xi][:m, :], in0=xt[xi][:m, :], scalar=stat[xi][:m, 1:2],
                                                       in1=nfb[:m, :], op0=ALU.mult, op1=ALU.mult),
                 reads=[d_xt[xi], d_stat[xi], d_cst], writes=[d_xt[xi]])
            K.dma(sp, lambda e: e.dma_start(out=ydst, in_=xt[xi][:m, :]), reads=[d_xt[xi]], writes=[dy])
    K.barrier()
    ph.close()
    K.cur = None


def _phase_sattn(nc, K, L):
    pe, dve, act, pool, sp = K.pe, K.dve, K.act, K.pool, K.sp
    ps, dps = L["ps"], L["dps"]
    ident_b, d_ident = L["ident_b"], L["d_ident"]
    eps_t, d_eps = L["eps_t"], L["d_eps"]
    neglam, d_lam, subw, d_subw = L["neglam"], L["d_lam"], L["subw"], L["d_subw"]
    ck, cv = L["ck_d"], L["cv_d"]
    ph = contextlib.ExitStack()
    K.cur = ph
    d_c = Dep()
    pti = K.sb("sa_pti", [128, 256], I32)
    ptf = K.sb("sa_ptf", [128, 256], F32)
    iot = K.sb("sa_iot", [128, 1], F32)
    idx = K.sb("sa_idx", [128, 256], I32)
    K.dma(sp, lambda e: e.dma_start(out=pti[:], in_=L["pt_d"][0:1, :].partition_broadcast(128)), writes=[d_c])
    K.dma(sp, lambda e: e.dma_start(out=iot[:], in_=L["iota_d"][:, :]), writes=[d_c])
    K.op(dve, lambda e: e.tensor_copy(out=ptf[:], in_=pti[:]), reads=[d_c], writes=[d_c])
    K.op(dve, lambda e: e.tensor_scalar(out=ptf[:], in0=ptf[:], scalar1=128.0, scalar2=iot[:, 0:1], op0=ALU.mult,
                                        op1=ALU.add), reads=[d_c], writes=[d_c])
    K.op(dve, lambda e: e.tensor_copy(out=idx[:], in_=ptf[:]), reads=[d_c], writes=[d_c])
    QS = K.sb("sa_QS", [128, 8, 64], BF16)
    QB = K.sb("sa_QB", [128, 8, 16, 8], BF16)
    KN = K.sb("sa_KN", [128, 8, 64], BF16)
    VN = [K.sb("sa_VN%d" % i, [4, 8, 130], BF16) for i in range(2)]
    d_VN = [Dep(), Dep()]
    smk = K.sb("sa_smk", [4, 8], F32)
    smb = K.sb("sa_smb", [4, 8], BF16)
    self_ = K.sb("sa_sel", [8, 2, 16, 64], F32)
    WS = K.sb("sa_WS", [8, 16, 64], BF16)
    ON = K.sb("sa_ON", [8, 16, 8, 128], BF16)
    d_ON = Dep()
    K.dma(sp, lambda e: e.dma_start(out=QS[:], in_=L["s_qTs"].rearrange("(h p) t -> p h t", p=128)), reads=[L["d_qTs"]],
          writes=[d_c])
    K.dma(sp, lambda e: e.dma_start(out=KN[:], in_=L["s_kTs"].rearrange("(h p) t -> p h t", p=128)), reads=[L["d_kTs"]],
          writes=[d_c])
    K.op(dve, lambda e: e.memset(QB[:], 0.0), writes=[d_c])
    for c in range(2):
        prt = slice(c * 64, (c + 1) * 64)
        K.op(dve, lambda e: e.tensor_copy(out=QB[prt, :, :, c * 4:(c + 1) * 4],
                                          in_=QS[prt, :, :].rearrange("p h (s q) -> p h s q", q=4)),
             reads=[d_c], writes=[d_c])
    for i in range(2):
        K.op(dve, lambda e: e.memset(VN[i][:], 1.0), writes=[d_VN[i]])
    K.dma(sp, lambda e: e.dma_start(out=smk[:], in_=L["smask_d"][:, :]), writes=[d_c])
    K.op(dve, lambda e: e.tensor_copy(out=smb[:], in_=smk[:]), reads=[d_c], writes=[d_c])
    K.dma(sp, lambda e: e.dma_start(out=self_[:], in_=L["sel_d"][:, :, :, :]), writes=[d_c])
    K.op(dve, lambda e: e.scalar_tensor_tensor(out=WS[:], in0=self_[:, 1, :, :], scalar=neglam[0:8, :], in1=self_[:, 0, :, :],
                                               op0=ALU.mult, op1=ALU.add), reads=[d_c, d_lam], writes=[d_c])
    KP = [K.sb("sa_KP%d" % i, [128, 1024], BF16) for i in range(6)]
    d_KP = [Dep() for _ in range(6)]
    VG = [K.sb("sa_VG%d" % i, [128, 1024], BF16) for i in range(6)]
    d_VG = [Dep() for _ in range(6)]
    VP = [K.sb("sa_VP%d" % i, [128, 8, 8, 130], BF16) for i in range(2)]
    d_VP = [Dep(), Dep()]
    KTt = [K.sb("sa_KT%d" % i, [128, 8, 128], BF16) for i in range(2)]
    d_KT = [Dep(), Dep()]
    PT = [K.sb("sa_PT%d" % i, [128, 8, 64], BF16) for i in range(2)]
    d_PT = [Dep(), Dep()]
    PN = K.sb("sa_PN", [4, 64], BF16)
    d_PN = Dep()
    rd = K.sb("sa_rd", [8, 8], F32)
    d_rd = Dep()
    for i in range(2):
        K.op(dve, lambda e: e.memset(VP[i][:], 1.0), writes=[d_VP[i]])
    osb = K.sb("sa_osb", [64, 1024], F32); d_osb = Dep()
    osq = K.sb("sa_osq", [64, 1024], F32); d_osq = Dep()
    sst = K.sb("sa_sst", [64, 16], F32); d_sst = Dep()
    sag = K.sb("sa_sag", [64, 1024], BF16); d_sag = Dep()
    ofb = K.sb("sa_ofb", [64, 1024], BF16); d_ofb = Dep()
    atF = K.sb("sa_atF", [128, 8, 64], BF16); d_atF = Dep()
    K.cur = None
    L["_ph_sattn"] = ph
    yield
    cnt = {"kp": 0, "kt": 0, "half": 0}
    acc_banks = [4, 5, 6]
    hb = lambda h: (acc_banks[h // 3], (h % 3) * 129)
    for s_ in range(16):
        K.dma(sp, lambda e: e.dma_start(out=VN[s_ % 2][:, :, 0:128],
                                        in_=L["s_vs"][s_ * 4:(s_ + 1) * 4, :].rearrange("t (h e) -> t h e", h=8)),
              reads=[L["d_vs"]], writes=[d_VN[s_ % 2]])
        for bk_ in acc_banks:
            K.op(dve, lambda e: e.memset(ps[bk_][0:8, :], 0.0), writes=[dps[bk_]])
        for hf in range(2):
            hi = cnt["half"] % 2
            cnt["half"] += 1
            sbank = 2 + hi
            for jj in range(8):
                col = s_ * 16 + hf * 8 + jj
                ki = cnt["kp"] % 6
                cnt["kp"] += 1
                K.dma(pool, lambda e: e.indirect_dma_start(
                    out=KP[ki][:, :], out_offset=None, in_=ck[:, :],
                    in_offset=bass.IndirectOffsetOnAxis(ap=idx[:, col:col + 1], axis=0)), reads=[d_c], writes=[d_KP[ki]])
                K.dma(pool, lambda e: e.indirect_dma_start(
                    out=VG[ki][:, :], out_offset=None, in_=cv[:, :],
                    in_offset=bass.IndirectOffsetOnAxis(ap=idx[:, col:col + 1], axis=0)), reads=[d_c], writes=[d_VG[ki]])
                if True:
                    K.op(dve, lambda e: e.tensor_copy(out=VP[hi][:, jj, :, 0:128],
                                                      in_=VG[ki][:, :].rearrange("p (h e) -> p h e", h=8)),
                         reads=[d_VG[ki]], writes=[d_VP[hi]])
                else:
                    K.op(act, lambda e: e.copy(out=VP[hi][:, jj, :, 0:128],
                                               in_=VG[ki][:, :].rearrange("p (h e) -> p h e", h=8)),
                         reads=[d_VG[ki]], writes=[d_VP[hi]])
                tb = cnt["kt"] % 2
                cnt["kt"] += 1
                ptv = ps[tb][:].bitcast(BF16)
                for h in range(8):
                    K.op(pe, lambda e: e.transpose(out=ptv[:, h * 128:(h + 1) * 128], in_=KP[ki][:, h * 128:(h + 1) * 128],
                                                   identity=ident_b[:]), reads=[d_KP[ki], d_ident], writes=[dps[tb]])
                if False:
                    pass
                else:
                    K.op(dve, lambda e: e.tensor_copy(out=KTt[tb][:, :, :], in_=ptv[:, :].rearrange("p (h t) -> p h t", h=8)),
                         reads=[dps[tb]], writes=[d_KT[tb]])
                for h in range(8):
                    K.op(pe, lambda e: e.matmul(ps[sbank][:, jj * 64 + h * 8:jj * 64 + h * 8 + 8], lhsT=KTt[tb][:, h, :],
                                                rhs=QB[:, h, s_, :], start=True, stop=True),
                         reads=[d_KT[tb], d_c], writes=[dps[sbank]])
            K.op(act, lambda e: e.activation(out=PT[hi][:, :, :], in_=ps[sbank][:, :].rearrange("p (j x) -> p j x", j=8),
                                             func=AF.Exp), reads=[dps[sbank]], writes=[d_PT[hi]])
            for h in range(8):
                bk_, off = hb(h)
                for jj in range(8):
                    K.op(pe, lambda e: e.matmul(ps[bk_][0:8, off:off + 129], lhsT=PT[hi][:, jj, h * 8:(h + 1) * 8],
                                                rhs=VP[hi][:, jj, h, 0:129], start=False, stop=False, skip_group_check=True),
                         reads=[d_PT[hi], d_VP[hi]], writes=[dps[bk_]])
        for h in range(8):
            K.op(pe, lambda e: e.matmul(ps[7][0:4, h * 8:(h + 1) * 8], lhsT=KN[:, h, s_ * 4:(s_ + 1) * 4], rhs=QB[:, h, s_, :],
                                        start=True, stop=True), reads=[d_c], writes=[dps[7]])
        K.op(act, lambda e: e.activation(out=PN[:, :], in_=ps[7][0:4, 0:64], func=AF.Exp), reads=[dps[7]], writes=[d_PN])
        K.op(dve, lambda e: e.tensor_tensor(out=PN[:, :].rearrange("p (h x) -> p h x", h=8),
                                            in0=PN[:, :].rearrange("p (h x) -> p h x", h=8),
                                            in1=smb[:, :].unsqueeze(1).to_broadcast([4, 8, 8]), op=ALU.mult),
             reads=[d_PN, d_c], writes=[d_PN])
        for h in range(8):
            bk_, off = hb(h)
            K.op(pe, lambda e: e.matmul(ps[bk_][0:8, off:off + 129], lhsT=PN[0:4, h * 8:(h + 1) * 8], rhs=VN[s_ % 2][0:4, h, 0:129],
                                        start=False, stop=True, skip_group_check=True),
                 reads=[d_PN, d_VN[s_ % 2]], writes=[dps[bk_]])
        for bi_, bk_ in enumerate(acc_banks):
            nh = 3 if bi_ < 2 else 2
            v3 = ps[bk_][0:8, 0:nh * 129].rearrange("p (h e) -> p h e", e=129)
            K.op(dve, lambda e: e.reciprocal(out=rd[:, bi_ * 3:bi_ * 3 + nh].unsqueeze(2), in_=v3[:, :, 128:129]),
                 reads=[dps[bk_]], writes=[d_rd])
            K.op(dve, lambda e: e.tensor_tensor(out=ON[:, s_, bi_ * 3:bi_ * 3 + nh, :], in0=v3[:, :, 0:128],
                                                in1=rd[:, bi_ * 3:bi_ * 3 + nh].unsqueeze(2).to_broadcast([8, nh, 128]),
                                                op=ALU.mult), reads=[dps[bk_], d_rd], writes=[d_ON])
        yield
    K.dma(sp, lambda e: e.dma_start(out=sag[:], in_=L["s_ags"][:, :]), reads=[L["d_ags"]], writes=[d_sag])
    for hc in range(2):
        for s_ in range(16):
            K.op(pe, lambda e: e.matmul(ps[2 + hc][0:64, :], lhsT=WS[0:8, s_, :],
                                        rhs=ON[0:8, s_, hc * 4:(hc + 1) * 4, :], start=(s_ == 0), stop=(s_ == 15)),
                 reads=[d_ON, d_c], writes=[dps[2 + hc]])
        K.op(dve, lambda e: e.tensor_copy(out=osb[:, hc * 512:(hc + 1) * 512], in_=ps[2 + hc][0:64, :]),
             reads=[dps[2 + hc]], writes=[d_osb])
    H8 = lambda ap: ap.rearrange("p (h e) -> p h e", h=8)
    K.op(dve, lambda e: e.tensor_tensor(out=osq[:], in0=osb[:], in1=osb[:], op=ALU.mult), reads=[d_osb], writes=[d_osq])
    K.op(dve, lambda e: e.reduce_sum(out=sst[:, 0:8], in_=H8(osq[:]), axis=AX.X), reads=[d_osq], writes=[d_sst])
    K.op(act, lambda e: e.activation(out=sst[:, 0:8], in_=sst[:, 0:8], func=AF.Sqrt, scale=1.0 / 128, bias=eps_t[:64, 0:1]),
         reads=[d_sst, d_eps], writes=[d_sst])
    K.op(dve, lambda e: e.reciprocal(out=sst[:, 0:8], in_=sst[:, 0:8]), reads=[d_sst], writes=[d_sst])
    K.op(dve, lambda e: e.tensor_tensor(out=H8(osb[:]), in0=H8(osb[:]), in1=sst[:, 0:8].unsqueeze(2).to_broadcast([64, 8, 128]),
                                        op=ALU.mult), reads=[d_osb, d_sst], writes=[d_osb])
    K.op(dve, lambda e: e.tensor_tensor(out=H8(osb[:]), in0=H8(osb[:]), in1=subw[:64, :].unsqueeze(1).to_broadcast([64, 8, 128]),
                                        op=ALU.mult), reads=[d_osb, d_subw], writes=[d_osb])
    K.op(dve, lambda e: e.tensor_tensor(out=ofb[:], in0=osb[:], in1=sag[:], op=ALU.mult), reads=[d_osb, d_sag], writes=[d_ofb])
    ptv = ps[0][:].bitcast(BF16)
    for h in range(8):
        K.op(pe, lambda e: e.transpose(out=ptv[:, h * 64:(h + 1) * 64], in_=ofb[:, h * 128:(h + 1) * 128],
                                       identity=ident_b[:64, :64]), reads=[d_ofb, d_ident], writes=[dps[0]])
    K.op(dve, lambda e: e.tensor_copy(out=atF[:, :, :], in_=ptv[:, 0:512].rearrange("p (h t) -> p h t", h=8)),
         reads=[dps[0]], writes=[d_atF])
    K.dma(sp, lambda e: e.dma_start(out=L["s_atTs"].rearrange("(h p) t -> p h t", p=128), in_=atF[:, :, :]),
          reads=[d_atF], writes=[L["d_atTs"]])


_SU = np.triu(np.ones((128, 128), np.float32), 1)
_IU = np.triu(np.ones((128, 128), np.float32), 0)
_MASK4 = np.ascontiguousarray(np.concatenate([_SU, _IU, -_SU, _IU], axis=1))
_MASKT = np.ascontiguousarray(-_SU.T)
_UMAT = _IU

_SMASK = np.zeros((4, 8), np.float32)
_SEL = np.zeros((8, 2, 16, 64), np.float32)
for _c in range(2):
    for _q in range(4):
        for _t in range(4):
            if _t <= _q:
                _SMASK[_t, _c * 4 + _q] = 1.0
        for _s in range(16):
            _SEL[_c * 4 + _q, _c, _s, _s * 4 + _q] = 1.0

_NC_CACHE = {}


def _prep_inputs(inp, cores):
    ident = np.eye(128, dtype=np.float32)
    xs_all = np.ascontiguousarray(inp["x_sample"].reshape(NS_ALL, D))
    w_in = np.ascontiguousarray(inp["w_in"][0])
    ck_flat = inp["cache_k"].reshape(2560 * 128, 1024)
    cv_flat = inp["cache_v"].reshape(2560 * 128, 1024)
    maps = []
    for c in cores:
        m = {
            "xp": np.ascontiguousarray(inp["x_prompt"][c]),
            "xs_own": np.ascontiguousarray(xs_all[c * 64:(c + 1) * 64]),
            "w_in": w_in,
            "norm_in": np.ascontiguousarray(inp["norm_in"]),
            "ident": ident,
            "trimask": np.triu(np.ones((128, 128), np.float32)),
            "lam4": np.concatenate([inp["lambda_q1"][0], inp["lambda_k1"][0], inp["lambda_q2"][0],
                                    inp["lambda_k2"][0]]).reshape(1, 256).astype(np.float32),
            "subln": np.ascontiguousarray(inp["subln_w"]).reshape(1, 128),
            "mu": np.ascontiguousarray(inp["mu_shift"]).reshape(1, RW_IN),
            "shift_own": np.ascontiguousarray(inp["state_shift"][0, c * 16:(c + 1) * 16]),
            "wkv_own": np.ascontiguousarray(inp["state_wkv"][0, c * 16:(c + 1) * 16]),
            "rwp": np.stack([inp["w0"][0], inp["a0"][0], inp["k_k"][0], inp["k_a"][0], inp["k_a"][0],
                             inp["lnx_w"][0], inp["lnx_b"][0], inp["r_k"][0].reshape(1024)]).astype(np.float32),
            "w2": np.ascontiguousarray(inp["w2"][0]),
            "a2": np.ascontiguousarray(inp["a2"][0]),
            "normf": np.ascontiguousarray(inp["norm_f"]).reshape(1, D),
            "w_out": np.ascontiguousarray(inp["w_out"][0]),
            "w_brr": np.ascontiguousarray(inp["w_br_rwkv"][0]),
            "w_bra": np.ascontiguousarray(inp["w_br_attn"][0]),
            "mask4": _MASK4,
            "ck": ck_flat,
            "cv": cv_flat,
            "pt_own": np.ascontiguousarray(inp["page_table"][c * 16:(c + 1) * 16]).reshape(1, 256).astype(np.int32),
            "iota": np.arange(128, dtype=np.float32).reshape(128, 1),
            "smask": _SMASK,
            "sel": _SEL,
            "maskT": _MASKT,
            "umat": _UMAT,
        }
        maps.append(m)
    return maps


def kernel(**inp):
    cores = list(range(NCORES))
    if "nc" not in _NC_CACHE:
        _NC_CACHE["nc"] = build()
    nc = _NC_CACHE["nc"]
    maps = _prep_inputs(inp, cores)
    res = run_bass_kernel_spmd(nc, maps, core_ids=cores).results
    f = np.float32
    y_p = np.stack([res[c]["o_yp"] for c in cores]).astype(f)
    y_s = np.concatenate([res[c]["o_ys"] for c in cores]).reshape(128, 4, D).astype(f)
    kp = np.stack([res[c]["o_kp"] for c in cores]).reshape(1, 8, T, 8, 2, 64).astype(f)
    vp = np.stack([res[c]["o_vp"] for c in cores]).reshape(1, 8, T, 8, 128).astype(f)
    ks = np.concatenate([res[c]["o_ks"] for c in cores]).reshape(1, 128, 4, 8, 2, 64).astype(f)
    vs = np.concatenate([res[c]["o_vs"] for c in cores]).reshape(1, 128, 4, 8, 128).astype(f)
    wp = np.stack([res[c]["o_wkvp"] for c in cores]).reshape(1, 8, 16, 64, 64).astype(f)
    ws = np.concatenate([res[c]["o_wkvs"] for c in cores]).reshape(1, 128, 16, 64, 64).astype(f)
    shp = np.stack([res[c]["o_shp"][0] for c in cores]).reshape(1, 8, RW_IN).astype(f)
    shs = np.concatenate([res[c]["o_shs"] for c in cores]).reshape(1, 128, RW_IN).astype(f)
    return (y_p, y_s, kp, vp, ks, vs, wp, ws, shp, shs)
```

```python
import contextlib
import numpy as np
import concourse.bass as bass
import concourse.mybir as mybir
from concourse.bass_utils import run_bass_kernel_spmd

F32 = mybir.dt.float32
BF16 = mybir.dt.bfloat16
I32 = mybir.dt.int32
U32 = mybir.dt.uint32
ALU = mybir.AluOpType
AF = mybir.ActivationFunctionType
AX = mybir.AxisListType

NCORES = 8
D = 2048
KC = 16
T = 2048
NT = 16
RW_IN = 4288
AT0 = RW_IN
G0 = RW_IN + 4096
TOTAL_IN = 12480
NS_ALL = 512
NS_OWN = 64
ATTN_SCALE = 0.125
NORM_EPS = 1e-6
GN_EPS = 64e-5
LAM_INIT = 0.2


class Dep:
    __slots__ = ("w", "r")

    def __init__(self):
        self.w = {}
        self.r = {}


class Eng:
    def __init__(self, K, name, handle, is_pe=False):
        self.K = K
        self.name = name
        self.h = handle
        self.is_pe = is_pe
        self.sem = K.new_sem("e_" + name)
        self.cnt = 0
        self.seen = {}
        self.dsems = []
        self.dvals = []
        self.dnext = 0


class Kern:
    def __init__(self, nc, stack):
        self.nc = nc
        self.stack = stack
        self.sems = []
        self.pe = Eng(self, "pe", nc.tensor, is_pe=True)
        self.dve = Eng(self, "dve", nc.vector)
        self.act = Eng(self, "act", nc.scalar)
        self.pool = Eng(self, "pool", nc.gpsimd)
        self.sp = Eng(self, "sp", nc.sync)
        for e, n in ((self.sp, 8), (self.pool, 12), (self.act, 4)):
            for i in range(n):
                e.dsems.append(self.new_sem("d_%s%d" % (e.name, i)))
                e.dvals.append(0)
        self.n_ins = 0
        self.cur = None

    def new_sem(self, name):
        s = self.stack.enter_context(self.nc.semaphore(name))
        self.sems.append(s)
        return len(self.sems) - 1

    def sb(self, name, shape, dtype):
        st = self.cur if self.cur is not None else self.stack
        return st.enter_context(self.nc.sbuf_tensor(name, list(shape), dtype))

    def barrier(self):
        engs = [self.pe, self.dve, self.act, self.pool, self.sp]
        for e in engs:
            for o in engs:
                if o is e or o.cnt == 0:
                    continue
                if e.seen.get(o.sem, 0) < o.cnt:
                    e.h.wait_ge(self.sems[o.sem], o.cnt)
                    e.seen[o.sem] = o.cnt
                for s_, v_ in zip(o.dsems, o.dvals):
                    if v_ and e.seen.get(s_, 0) < v_:
                        e.h.wait_ge(self.sems[s_], v_)
                        e.seen[s_] = v_
            for s_, v_ in zip(e.dsems, e.dvals):
                if v_ and e.seen.get(s_, 0) < v_:
                    e.h.wait_ge(self.sems[s_], v_)
                    e.seen[s_] = v_

    def _wait(self, eng, reads, writes):
        need = {}
        for d in reads:
            for s, v in d.w.items():
                if need.get(s, 0) < v:
                    need[s] = v
        for d in writes:
            for s, v in d.w.items():
                if need.get(s, 0) < v:
                    need[s] = v
            for s, v in d.r.items():
                if need.get(s, 0) < v:
                    need[s] = v
        for s, v in need.items():
            if s == eng.sem and eng.is_pe:
                continue
            if eng.seen.get(s, 0) >= v:
                continue
            eng.h.wait_ge(self.sems[s], v)
            eng.seen[s] = v

    def op(self, eng, fn, reads=(), writes=()):
        self._wait(eng, reads, writes)
        ins = fn(eng.h)
        eng.cnt += 1
        ins.then_inc(self.sems[eng.sem], 1)
        ev = (eng.sem, eng.cnt)
        for d in reads:
            if d.r.get(ev[0], 0) < ev[1]:
                d.r[ev[0]] = ev[1]
        for d in writes:
            d.w[ev[0]] = ev[1]
        self.n_ins += 1
        return ins

    def dma(self, eng, fn, reads=(), writes=()):
        i = eng.dnext
        eng.dnext = (i + 1) % len(eng.dsems)
        s = eng.dsems[i]
        if eng.seen.get(s, 0) < eng.dvals[i]:
            eng.h.wait_ge(self.sems[s], eng.dvals[i])
            eng.seen[s] = eng.dvals[i]
        self._wait(eng, reads, writes)
        ins = fn(eng.h)
        eng.dvals[i] += 16
        ins.then_inc(self.sems[s], 16)
        ev = (s, eng.dvals[i])
        for d in reads:
            if d.r.get(ev[0], 0) < ev[1]:
                d.r[ev[0]] = ev[1]
        for d in writes:
            d.w[ev[0]] = ev[1]
        self.n_ins += 1
        return ins

    def finish(self, deps):
        self._wait(self.sp, deps, ())


def build(dbg=False):
    nc = bass.Bass("TRN2", target_bir_lowering=False)
    stack = contextlib.ExitStack()
    with stack:
        K = Kern(nc, stack)
        _build_body(nc, K, dbg)
    return nc


def _dram_in(nc, name, shape, dtype=F32):
    return nc.dram_tensor(name, list(shape), dtype, kind="ExternalInput").ap()


def _dram_out(nc, name, shape, dtype=F32):
    return nc.dram_tensor(name, list(shape), dtype, kind="ExternalOutput").ap()


def _dram_tmp(nc, name, shape, dtype):
    return nc.dram_tensor(name, list(shape), dtype, kind="Internal").ap()


def _build_body(nc, K, dbg):
    pe, dve, act, pool, sp = K.pe, K.dve, K.act, K.pool, K.sp
    xp = _dram_in(nc, "xp", [T, D])
    xs_own = _dram_in(nc, "xs_own", [NS_OWN, D])
    w_in = _dram_in(nc, "w_in", [D, TOTAL_IN])
    norm_in = _dram_in(nc, "norm_in", [1, D])
    ident_d = _dram_in(nc, "ident", [128, 128])
    trimask_d = _dram_in(nc, "trimask", [128, 128])
    lam4_d = _dram_in(nc, "lam4", [1, 256])
    subln_d = _dram_in(nc, "subln", [1, 128])
    s_atT = _dram_tmp(nc, "s_atT", [1024, T], BF16)
    s_rwT = _dram_tmp(nc, "s_rwT", [1024, T], BF16)
    s_atTs = _dram_tmp(nc, "s_atTs", [1024, NS_OWN], BF16)
    s_qTs = _dram_tmp(nc, "s_qTs", [1024, NS_OWN], BF16)
    s_kTs = _dram_tmp(nc, "s_kTs", [1024, NS_OWN], BF16)
    s_vs = _dram_tmp(nc, "s_vs", [NS_OWN, 1024], BF16)
    s_ags = _dram_tmp(nc, "s_ags", [NS_OWN, 1024], BF16)
    d_qTs, d_kTs, d_vs, d_ags = Dep(), Dep(), Dep(), Dep()
    s_rwTs = _dram_tmp(nc, "s_rwTs", [1024, NS_OWN], BF16)
    d_atTs, d_rwTs = Dep(), Dep()
    normf_d = _dram_in(nc, "normf", [1, D])
    w_out_d = _dram_in(nc, "w_out", [D, D])
    w_brr_d = _dram_in(nc, "w_brr", [1024, D])
    w_bra_d = _dram_in(nc, "w_bra", [1024, D])
    d_rwT = Dep()
    mu_d = _dram_in(nc, "mu", [1, RW_IN])
    ck_d = _dram_in(nc, "ck", [2560 * 128, 1024])
    cv_d = _dram_in(nc, "cv", [2560 * 128, 1024])
    pt_d = _dram_in(nc, "pt_own", [1, 256], I32)
    iota_d = _dram_in(nc, "iota", [128, 1])
    smask_d = _dram_in(nc, "smask", [4, 8])
    sel_d = _dram_in(nc, "sel", [8, 2, 16, 64])
    shift_d = _dram_in(nc, "shift_own", [16, RW_IN])
    wkv_d = _dram_in(nc, "wkv_own", [16, 16, 64, 64])
    s_rs = _dram_tmp(nc, "s_rs", [NS_OWN, 6, 1024], F32)
    s_ysm = _dram_tmp(nc, "s_ysm", [NS_OWN, 1024], F32)
    rwp_d = _dram_in(nc, "rwp", [8, 1024])
    w2_d = _dram_in(nc, "w2", [96, 1024])
    a2_d = _dram_in(nc, "a2", [96, 1024])
    mask4_d = _dram_in(nc, "mask4", [128, 512])
    maskT_d = _dram_in(nc, "maskT", [128, 128])
    umat_d = _dram_in(nc, "umat", [128, 128])
    d_atT = Dep()

    o_kp = _dram_out(nc, "o_kp", [T, 1024])
    o_vp = _dram_out(nc, "o_vp", [T, 1024])
    o_ks = _dram_out(nc, "o_ks", [NS_OWN, 1024])
    o_vs = _dram_out(nc, "o_vs", [NS_OWN, 1024])
    o_shp = _dram_out(nc, "o_shp", [1, RW_IN])
    o_shs = _dram_out(nc, "o_shs", [16, RW_IN])
    o_yp = _dram_out(nc, "o_yp", [T, D])
    o_ys = _dram_out(nc, "o_ys", [NS_OWN, D])
    o_wkvp = _dram_out(nc, "o_wkvp", [16, 64, 64])
    o_wkvs = _dram_out(nc, "o_wkvs", [16, 16, 64, 64])
    d_oyp, d_oys, d_owkvp, d_owkvs = (Dep() for _ in range(4))

    s_prw = _dram_tmp(nc, "s_prw", [T, RW_IN], F32)
    s_prws = _dram_tmp(nc, "s_prws", [NS_OWN, RW_IN], F32)
    s_qT = _dram_tmp(nc, "s_qT", [1024, T], BF16)
    s_kT = _dram_tmp(nc, "s_kT", [1024, T], BF16)
    s_v = _dram_tmp(nc, "s_v", [T, 1024], BF16)
    s_ag = _dram_tmp(nc, "s_ag", [T, 1024], BF16)
    s_sg = _dram_tmp(nc, "s_sg", [4096, T], BF16)
    s_sgs = _dram_tmp(nc, "s_sgs", [4096, NS_OWN], BF16)
    d_prw, d_prws, d_qT, d_kT, d_v, d_ag, d_sg, d_sgs = (Dep() for _ in range(8))
    d_okp, d_ovp, d_oks, d_ovs, d_oshp, d_oshs = (Dep() for _ in range(6))
    out_deps = [d_okp, d_ovp, d_oks, d_ovs, d_oshp, d_oshs]

    ps = [K.stack.enter_context(nc.psum_tensor("ps%d" % i, [128, 512], F32)) for i in range(8)]
    dps = [Dep() for _ in range(8)]
    ident_f = K.sb("ident_f", [128, 128], F32)
    ident_b = K.sb("ident_b", [128, 128], BF16)
    d_ident = Dep()
    eps_t = K.sb("eps_t", [128, 2], F32)
    d_eps = Dep()
    K.op(dve, lambda e: e.memset(eps_t[:, 0:1], NORM_EPS), writes=[d_eps])
    K.op(dve, lambda e: e.memset(eps_t[:, 1:2], GN_EPS), writes=[d_eps])
    phA = contextlib.ExitStack()
    K.cur = phA
    nin_b = K.sb("nin_b", [128, D], F32)
    d_nin = Dep()
    hT = K.sb("hT", [128, KC, T], BF16)
    d_hT = [Dep() for _ in range(NT)]
    hTa, d_hTa = None, None
    hTo = K.sb("hTo", [128, KC, NS_OWN], BF16)
    d_hTo = Dep()

    K.dma(sp, lambda e: e.dma_start(out=ident_f[:], in_=ident_d[:, :]), writes=[d_ident])
    K.op(dve, lambda e: e.tensor_copy(out=ident_b[:], in_=ident_f[:]), reads=[d_ident], writes=[d_ident])
    K.dma(sp, lambda e: e.dma_start(out=nin_b[:], in_=norm_in[0:1, :].partition_broadcast(128)),
          writes=[d_nin])

    xt = [K.sb("xt%d" % i, [128, D], F32) for i in range(2)]
    d_xt = [Dep(), Dep()]
    hb = [K.sb("hb%d" % i, [128, D], BF16) for i in range(2)]
    d_hb = [Dep(), Dep()]
    sq = K.sb("sq", [128, D], BF16)
    d_sq = Dep()
    stat = [K.sb("stat%d" % i, [128, 2], F32) for i in range(2)]
    d_stat = [Dep(), Dep()]

    tiles = [("p", i, 128) for i in range(NT)] + [("o", 0, NS_OWN)]
    for ti, (kind, i, n) in enumerate(tiles):
        b = ti % 2
        if kind == "p":
            src = xp[i * 128:(i + 1) * 128, :]
            dstT, ddst, toff = hT, d_hT[i], i * 128
        elif kind == "a":
            src = xs_all[i * 128:(i + 1) * 128, :]
            dstT, ddst, toff = hTa, d_hTa[i], i * 128
        else:
            src = xs_own[:, :]
            dstT, ddst, toff = hTo, d_hTo, 0
        K.dma(sp, lambda e: e.dma_start(out=xt[b][:n, :], in_=src), writes=[d_xt[b]])
        K.op(act, lambda e: e.activation(out=sq[:n, :], in_=xt[b][:n, :], func=AF.Square,
                                         accum_out=stat[b][:n, 0:1]),
             reads=[d_xt[b]], writes=[d_sq, d_stat[b]])
        K.op(act, lambda e: e.activation(out=stat[b][:n, 1:2], in_=stat[b][:n, 0:1], func=AF.Sqrt,
                                         scale=1.0 / D, bias=eps_t[:n, 0:1]),
             reads=[d_stat[b], d_eps], writes=[d_stat[b]])
        K.op(dve, lambda e: e.reciprocal(out=stat[b][:n, 1:2], in_=stat[b][:n, 1:2]),
             reads=[d_stat[b]], writes=[d_stat[b]])
        K.op(dve, lambda e: e.scalar_tensor_tensor(out=hb[b][:n, :], in0=xt[b][:n, :], scalar=stat[b][:n, 1:2],
                                                   in1=nin_b[:n, :], op0=ALU.mult, op1=ALU.mult),
             reads=[d_xt[b], d_stat[b], d_nin], writes=[d_hb[b]])
        for g in range(4):
            pb = (ti * 4 + g) % 2
            ptv = ps[pb][:].bitcast(BF16)
            for j in range(4):
                kc = g * 4 + j
                K.op(pe, lambda e: e.transpose(out=ptv[:, j * 128:j * 128 + n],
                                               in_=hb[b][:n, kc * 128:(kc + 1) * 128],
                                               identity=ident_b[:n, :n]),
                     reads=[d_hb[b], d_ident], writes=[dps[pb]])
            eng = act if g % 2 == 0 else dve
            src_v = ptv[:, 0:512].rearrange("p (j t) -> p j t", j=4)[:, :, :n]
            if eng is act:
                K.op(act, lambda e: e.copy(out=dstT[:, g * 4:(g + 1) * 4, toff:toff + n], in_=src_v),
                     reads=[dps[pb]], writes=[ddst])
            else:
                K.op(dve, lambda e: e.tensor_copy(out=dstT[:, g * 4:(g + 1) * 4, toff:toff + n], in_=src_v),
                     reads=[dps[pb]], writes=[ddst])

    wst = [K.sb("wst%d" % i, [128, KC, 256], F32) for i in range(2)]
    d_wst = [Dep(), Dep()]
    wb = [K.sb("wb%d" % i, [128, KC, 256], BF16) for i in range(2)]
    d_wb = [Dep(), Dep()]
    ob = [K.sb("ob%d" % i, [128, 512], F32) for i in range(3)]
    d_ob = [Dep() for _ in range(3)]
    obb = [K.sb("obb%d" % i, [128, 512], BF16) for i in range(3)]
    d_obb = [Dep() for _ in range(3)]
    cnt = {"ps": 0, "ob": 0, "obb": 0, "ev": 0}

    def next_ps():
        i = 2 + cnt["ps"] % 6
        cnt["ps"] += 1
        return i

    def next_ob():
        i = cnt["ob"] % 3
        cnt["ob"] += 1
        return i

    def next_obb():
        i = cnt["obb"] % 3
        cnt["obb"] += 1
        return i

    def evac_eng():
        cnt["ev"] += 1
        return act if cnt["ev"] % 2 == 0 else dve

    def copy_op(eng, out, in_, reads, writes):
        if eng is act:
            K.op(act, lambda e: e.copy(out=out, in_=in_), reads=reads, writes=writes)
        else:
            K.op(eng, lambda e: e.tensor_copy(out=out, in_=in_), reads=reads, writes=writes)

    w_in_v = w_in.rearrange("(kc p) n -> p kc n", p=128)

    def load_block(bi, src_v, c0, ncols):
        b = bi % 2
        for half in range(2):
            K.dma(sp, lambda e: e.dma_start(out=wst[b][:, half * 8:(half + 1) * 8, :ncols],
                                            in_=src_v[:, half * 8:(half + 1) * 8, c0:c0 + ncols]),
                  writes=[d_wst[b]])
        K.op(pool, lambda e: e.tensor_copy(out=wb[b][:, :, :ncols], in_=wst[b][:, :, :ncols]),
             reads=[d_wst[b]], writes=[d_wb[b]])
        return b

    def tok_major(b, j0, ncols, lhs, dl, toff, n, handler):
        pi = next_ps()
        for kc in range(KC):
            K.op(pe, lambda e: e.matmul(ps[pi][:n, :ncols], lhsT=lhs[:, kc, toff:toff + n],
                                        rhs=wb[b][:, kc, j0:j0 + ncols], start=(kc == 0), stop=(kc == KC - 1)),
                 reads=[dl, d_wb[b]], writes=[dps[pi]])
        handler(ps[pi][:n, :ncols], dps[pi])

    def feat_major(b, j0, rhs, dr, toff, n, handler):
        pi = next_ps()
        for kc in range(KC):
            K.op(pe, lambda e: e.matmul(ps[pi][:, :n], lhsT=wb[b][:, kc, j0:j0 + 128],
                                        rhs=rhs[:, kc, toff:toff + n], start=(kc == 0), stop=(kc == KC - 1)),
                 reads=[dr, d_wb[b]], writes=[dps[pi]])
        handler(ps[pi][:, :n], dps[pi])

    blocks = []
    for (sa, sb_) in ((0, RW_IN), (RW_IN, G0), (G0, TOTAL_IN)):
        c0 = sa
        while c0 < sb_:
            ncols = min(256, sb_ - c0)
            blocks.append((w_in_v, c0, ncols, "main"))
            c0 += ncols

    def to_dram_f32(psum_ap, dp, n, ncols, dsts):
        oi = next_ob()
        copy_op(evac_eng(), ob[oi][:n, :ncols], psum_ap, [dp], [d_ob[oi]])
        for dst, dd in dsts:
            K.dma(sp, lambda e: e.dma_start(out=dst, in_=ob[oi][:n, :ncols]), reads=[d_ob[oi]], writes=[dd])
        return oi

    import os
    nblk = int(os.environ.get("NBLK", "999"))
    blocks = blocks[:nblk] if nblk < 900 else blocks
    skipb = int(os.environ.get("SKIPB", "0"))
    blocks = blocks[skipb:]
    if blocks:
        load_block(0, blocks[0][0], blocks[0][1], blocks[0][2])
    for bi, (src_v, c0, ncols, kind) in enumerate(blocks):
        b = bi % 2
        if bi + 1 < len(blocks):
            nb = blocks[bi + 1]
            load_block(bi + 1, nb[0], nb[1], nb[2])
        if kind == "main" and c0 < RW_IN:
            for t in range(NT):
                def h(pa, dp, t=t):
                    dsts = [(s_prw[t * 128:(t + 1) * 128, c0:c0 + ncols], d_prw)]
                    oi = to_dram_f32(pa, dp, 128, ncols, dsts)
                    if t == NT - 1:
                        K.dma(sp, lambda e: e.dma_start(out=o_shp[0:1, c0:c0 + ncols],
                                                        in_=ob[oi][127:128, :ncols]),
                              reads=[d_ob[oi]], writes=[d_oshp])
                tok_major(b, 0, ncols, hT, d_hT[t], t * 128, 128, h)

            def hs(pa, dp):
                to_dram_f32(pa, dp, NS_OWN, ncols, [(s_prws[:, c0:c0 + ncols], d_prws)])
            tok_major(b, 0, ncols, hTo, d_hTo, 0, NS_OWN, hs)
        elif kind == "main" and c0 < G0:
            a0 = c0 - AT0
            which = a0 // 1024
            r0 = a0 % 1024
            if which in (0, 1):
                dstT, dd = (s_qT, d_qT) if which == 0 else (s_kT, d_kT)
                for j in range(2):
                    for tg in range(4):
                        def h(pa, dp, j=j, tg=tg):
                            oi = next_obb()
                            if which == 0:
                                K.op(act, lambda e: e.mul(out=obb[oi][:, :], in_=pa, mul=ATTN_SCALE),
                                     reads=[dp], writes=[d_obb[oi]])
                            else:
                                copy_op(evac_eng(), obb[oi][:, :], pa, [dp], [d_obb[oi]])
                            K.dma(sp, lambda e: e.dma_start(
                                out=dstT[r0 + j * 128:r0 + (j + 1) * 128, tg * 512:(tg + 1) * 512],
                                in_=obb[oi][:, :]), reads=[d_obb[oi]], writes=[dd])
                        feat_major(b, j * 128, hT, d_hT[tg * 4 + 3], tg * 512, 512, h)
            if which in (0, 1):
                dstTs, dds = (s_qTs, d_qTs) if which == 0 else (s_kTs, d_kTs)
                for j in range(2):
                    def hsq(pa, dp, j=j):
                        oi = next_obb()
                        if which == 0:
                            K.op(act, lambda e: e.mul(out=obb[oi][:, :NS_OWN], in_=pa, mul=ATTN_SCALE),
                                 reads=[dp], writes=[d_obb[oi]])
                        else:
                            copy_op(evac_eng(), obb[oi][:, :NS_OWN], pa, [dp], [d_obb[oi]])
                        K.dma(sp, lambda e: e.dma_start(out=dstTs[r0 + j * 128:r0 + (j + 1) * 128, :],
                                                        in_=obb[oi][:, :NS_OWN]), reads=[d_obb[oi]], writes=[dds])
                    feat_major(b, j * 128, hTo, d_hTo, 0, NS_OWN, hsq)
            if which in (1, 2, 3):
                def hso(pa, dp):
                    if which == 1:
                        to_dram_f32(pa, dp, NS_OWN, ncols, [(o_ks[:, r0:r0 + ncols], d_oks)])
                    elif which == 2:
                        oi = to_dram_f32(pa, dp, NS_OWN, ncols, [(o_vs[:, r0:r0 + ncols], d_ovs)])
                        bi2 = next_obb()
                        K.op(pool, lambda e: e.tensor_copy(out=obb[bi2][:NS_OWN, :ncols], in_=ob[oi][:NS_OWN, :ncols]),
                             reads=[d_ob[oi]], writes=[d_obb[bi2]])
                        K.dma(sp, lambda e: e.dma_start(out=s_vs[:, r0:r0 + ncols], in_=obb[bi2][:NS_OWN, :ncols]),
                              reads=[d_obb[bi2]], writes=[d_vs])
                    else:
                        bi2 = next_obb()
                        K.op(act, lambda e: e.activation(out=obb[bi2][:NS_OWN, :ncols], in_=pa, func=AF.Silu),
                             reads=[dp], writes=[d_obb[bi2]])
                        K.dma(sp, lambda e: e.dma_start(out=s_ags[:, r0:r0 + ncols], in_=obb[bi2][:NS_OWN, :ncols]),
                              reads=[d_obb[bi2]], writes=[d_ags])
                tok_major(b, 0, ncols, hTo, d_hTo, 0, NS_OWN, hso)
                for t in range(NT):
                    def h(pa, dp, t=t):
                        rows = slice(t * 128, (t + 1) * 128)
                        if which == 1:
                            to_dram_f32(pa, dp, 128, ncols, [(o_kp[rows, r0:r0 + ncols], d_okp)])
                        elif which == 2:
                            oi = to_dram_f32(pa, dp, 128, ncols, [(o_vp[rows, r0:r0 + ncols], d_ovp)])
                            bi2 = next_obb()
                            K.op(pool, lambda e: e.tensor_copy(out=obb[bi2][:, :ncols], in_=ob[oi][:, :ncols]),
                                 reads=[d_ob[oi]], writes=[d_obb[bi2]])
                            K.dma(sp, lambda e: e.dma_start(out=s_v[rows, r0:r0 + ncols],
                                                            in_=obb[bi2][:, :ncols]),
                                  reads=[d_obb[bi2]], writes=[d_v])
                        else:
                            bi2 = next_obb()
                            K.op(act, lambda e: e.activation(out=obb[bi2][:, :ncols], in_=pa, func=AF.Silu),
                                 reads=[dp], writes=[d_obb[bi2]])
                            K.dma(sp, lambda e: e.dma_start(out=s_ag[rows, r0:r0 + ncols],
                                                            in_=obb[bi2][:, :ncols]),
                                  reads=[d_obb[bi2]], writes=[d_ag])
                    tok_major(b, 0, ncols, hT, d_hT[t], t * 128, 128, h)
        elif kind == "main":
            g0 = c0 - G0
            for j in range(ncols // 128):
                for tg in range(4):
                    def h(pa, dp, j=j, tg=tg):
                        oi = next_obb()
                        K.op(act, lambda e: e.activation(out=obb[oi][:, :], in_=pa, func=AF.Sigmoid),
                             reads=[dp], writes=[d_obb[oi]])
                        K.dma(sp, lambda e: e.dma_start(
                            out=s_sg[g0 + j * 128:g0 + (j + 1) * 128, tg * 512:(tg + 1) * 512],
                            in_=obb[oi][:, :]), reads=[d_obb[oi]], writes=[d_sg])
                    feat_major(b, j * 128, hT, d_hT[tg * 4 + 3], tg * 512, 512, h)

                def hs(pa, dp, j=j):
                    oi = next_obb()
                    K.op(act, lambda e: e.activation(out=obb[oi][:, :NS_OWN], in_=pa, func=AF.Sigmoid),
                         reads=[dp], writes=[d_obb[oi]])
                    K.dma(sp, lambda e: e.dma_start(out=s_sgs[g0 + j * 128:g0 + (j + 1) * 128, :],
                                                    in_=obb[oi][:, :NS_OWN]),
                          reads=[d_obb[oi]], writes=[d_sgs])
                feat_major(b, j * 128, hTo, d_hTo, 0, NS_OWN, hs)

    K.dma(sp, lambda e: e.dma_start(out=o_shs[:, :],
                                    in_=s_prws.rearrange("(s t) n -> s t n", t=4)[:, 3, :]),
          reads=[d_prws], writes=[d_oshs])

    K.barrier()
    phA.close()
    K.cur = None
    LL = dict(locals())
    g1 = _phase_attn(nc, K, LL)
    g2 = _phase_sattn(nc, K, LL)
    next(g1)
    next(g2)
    for _i in range(16):
        next(g1)
        next(g2)
    for _g in (g1, g2):
        for _ in _g:
            pass
    K.barrier()
    LL["_ph_sattn"].close()
    LL["_ph_attn"].close()
    K.cur = None
    _phase_rwkv(nc, K, LL)
    _phase_final(nc, K, LL)
    if dbg:
        d_dbg = Dep()
        out_deps.append(d_dbg)
        for nm, src, dd, shp in (("dbg_atTs", s_atTs, d_atTs, [1024, NS_OWN]), ("dbg_rwTs", s_rwTs, d_rwTs, [1024, NS_OWN]),
                                 ("dbg_qTs", s_qTs, d_qTs, [1024, NS_OWN]), ("dbg_ags", s_ags, d_ags, [NS_OWN, 1024])):
            o_ = _dram_out(nc, nm, shp, BF16)
            K.dma(sp, lambda e: e.dma_start(out=o_[:, :], in_=src[:, :]), reads=[dd], writes=[d_dbg])
    K.finish(out_deps + [d_oyp, d_oys, d_owkvp, d_owkvs, d_prw, d_prws, d_qT, d_kT, d_v, d_ag, d_sg, d_sgs])
    print("instructions:", K.n_ins)


def _phase_attn(nc, K, L):
    pe, dve, act, pool, sp = K.pe, K.dve, K.act, K.pool, K.sp
    ps, dps = L["ps"], L["dps"]
    ident_b, d_ident = L["ident_b"], L["d_ident"]
    eps_t, d_eps = L["eps_t"], L["d_eps"]
    s_qT, s_kT, s_v, s_ag, s_atT = L["s_qT"], L["s_kT"], L["s_v"], L["s_ag"], L["s_atT"]
    d_qT, d_kT, d_v, d_ag, d_atT = L["d_qT"], L["d_kT"], L["d_v"], L["d_ag"], L["d_atT"]
    trif = K.sb("trif", [128, 128], F32)
    trib = K.sb("trib", [128, 128], BF16)
    d_tri = Dep()
    K.dma(sp, lambda e: e.dma_start(out=trif[:], in_=L["trimask_d"][:, :]), writes=[d_tri])
    K.op(dve, lambda e: e.tensor_copy(out=trib[:], in_=trif[:]), reads=[d_tri], writes=[d_tri])
    lamt = K.sb("lamt", [128, 256], F32)
    lamw = K.sb("lamw", [128, 136], F32)
    d_lam = Dep()
    K.dma(sp, lambda e: e.dma_start(out=lamt[:], in_=L["lam4_d"][0:1, :].partition_broadcast(128)), writes=[d_lam])
    lv = lamt[:].rearrange("p (a b n) -> p a b n", a=2, b=2)
    K.op(dve, lambda e: e.tensor_tensor(out=lamw[:, 0:128].rearrange("p (a n) -> p a n", a=2),
                                        in0=lv[:, :, 0, :], in1=lv[:, :, 1, :], op=ALU.mult),
         reads=[d_lam], writes=[d_lam])
    K.op(dve, lambda e: e.reduce_sum(out=lamw[:, 128:130], in_=lamw[:, 0:128].rearrange("p (a n) -> p a n", a=2),
                                     axis=AX.X), reads=[d_lam], writes=[d_lam])
    K.op(act, lambda e: e.activation(out=lamw[:, 130:132], in_=lamw[:, 128:130], func=AF.Exp),
         reads=[d_lam], writes=[d_lam])
    K.op(dve, lambda e: e.tensor_tensor(out=lamw[:, 132:133], in0=lamw[:, 130:131], in1=lamw[:, 131:132],
                                        op=ALU.subtract), reads=[d_lam], writes=[d_lam])
    K.op(dve, lambda e: e.tensor_scalar(out=lamw[:, 133:134], in0=lamw[:, 132:133], scalar1=LAM_INIT, scalar2=-1.0,
                                        op0=ALU.add, op1=ALU.mult), reads=[d_lam], writes=[d_lam])
    neglam = lamw[:, 133:134]
    subw = K.sb("subw", [128, 128], F32)
    d_subw = Dep()
    K.dma(sp, lambda e: e.dma_start(out=subw[:], in_=L["subln_d"][0:1, :].partition_broadcast(128)), writes=[d_subw])
    K.op(dve, lambda e: e.tensor_scalar(out=subw[:], in0=subw[:], scalar1=1.0 - LAM_INIT, scalar2=None,
                                        op0=ALU.mult), reads=[d_subw], writes=[d_subw])
    L["neglam"], L["d_lam"], L["subw"], L["d_subw"], L["trib"], L["d_tri"] = neglam, d_lam, subw, d_subw, trib, d_tri

    ph = contextlib.ExitStack()
    K.cur = ph
    QT = [K.sb("QT%d" % i, [128, T], BF16) for i in range(2)]
    KT = [K.sb("KT%d" % i, [128, T], BF16) for i in range(2)]
    VA = [K.sb("VA%d" % i, [128, NT, 130], BF16) for i in range(2)]
    SAG = [K.sb("SAG%d" % i, [128, NT, 128], BF16) for i in range(2)]
    ATT = [K.sb("ATT%d" % i, [128, T], BF16) for i in range(2)]
    d_in = [Dep(), Dep()]
    d_att = [Dep(), Dep()]
    fin = [K.sb("fin%d" % i, [128, 264], F32) for i in range(2)]
    d_fin = [Dep(), Dep()]
    fb = [K.sb("fb%d" % i, [128, 128], BF16) for i in range(2)]
    d_fb = [Dep(), Dep()]
    junk = K.sb("junk", [128, 128], BF16)
    d_junk = Dep()
    for i in range(2):
        K.op(pool, lambda e: e.memset(VA[i][:, :, 128:130], 1.0), writes=[d_in[i]])

    def load_head(h):
        b = h % 2
        rows = slice(h * 128, (h + 1) * 128)
        K.dma(sp, lambda e: e.dma_start(out=QT[b][:], in_=s_qT[rows, :]), reads=[d_qT], writes=[d_in[b]])
        K.dma(sp, lambda e: e.dma_start(out=KT[b][:], in_=s_kT[rows, :]), reads=[d_kT], writes=[d_in[b]])
        K.dma(sp, lambda e: e.dma_start(out=VA[b][:, :, 0:128],
                                        in_=s_v.rearrange("(t p) n -> p t n", p=128)[:, :, rows]),
              reads=[d_v], writes=[d_in[b]])
        K.dma(sp, lambda e: e.dma_start(out=SAG[b][:],
                                        in_=s_ag.rearrange("(t p) n -> p t n", p=128)[:, :, rows]),
              reads=[d_ag], writes=[d_in[b]])

    pT4 = [K.sb("pTq%d" % i, [128, 512], BF16) for i in range(4)]
    d_pT4 = [Dep() for _ in range(4)]
    accs = [K.sb("accs%d" % i, [128, 2, 130], F32) for i in range(2)]
    d_accs = [Dep(), Dep()]
    K.cur = None
    L["_ph_attn"] = ph
    yield
    cnt = {"g": 0, "q": 0}
    load_head(0)
    for h in range(8):
        b = h % 2
        if h + 1 < 8:
            load_head(h + 1)
        groups = []
        for qt in range(NT):
            for g0 in range(0, qt + 1, 4):
                groups.append((qt, g0, min(4, qt + 1 - g0)))

        def emit_S(gidx, qt, g0, nk):
            for c in range(2):
                prt = slice(c * 64, (c + 1) * 64)
                sb_ = 2 + 2 * c + gidx % 2
                for j in range(nk):
                    kt = g0 + j
                    K.op(pe, lambda e: e.matmul(ps[sb_][:, j * 128:(j + 1) * 128], lhsT=KT[b][prt, kt * 128:(kt + 1) * 128],
                                                rhs=QT[b][prt, qt * 128:(qt + 1) * 128], start=True, stop=True),
                         reads=[d_in[b]], writes=[dps[sb_]])

        def emit_rest(gidx, qt, g0, nk):
            for c in range(2):
                sb_ = 2 + 2 * c + gidx % 2
                pi = 2 * c + gidx % 2
                K.op(act, lambda e: e.activation(out=pT4[pi][:, :nk * 128], in_=ps[sb_][:, :nk * 128], func=AF.Exp),
                     reads=[dps[sb_]], writes=[d_pT4[pi]])
            if g0 + nk - 1 == qt:
                for c in range(2):
                    pi = 2 * c + gidx % 2
                    j = nk - 1
                    K.op(dve, lambda e: e.tensor_tensor(out=pT4[pi][:, j * 128:(j + 1) * 128],
                                                        in0=pT4[pi][:, j * 128:(j + 1) * 128], in1=trib[:], op=ALU.mult),
                         reads=[d_pT4[pi], d_tri], writes=[d_pT4[pi]])
            for c in range(2):
                pi = 2 * c + gidx % 2
                pa = 6 + c
                for j in range(nk):
                    kt = g0 + j
                    K.op(pe, lambda e: e.matmul(ps[pa][:, 0:129], lhsT=pT4[pi][:, j * 128:(j + 1) * 128],
                                                rhs=VA[b][:, kt, 0:129], start=(kt == 0), stop=(kt == qt)),
                         reads=[d_pT4[pi], d_in[b]], writes=[dps[pa]])

        def finalize(qt):
            qi = cnt["q"] % 2
            cnt["q"] += 1
            A_ = accs[qi]
            dA = d_accs[qi]
            K.op(dve, lambda e: e.tensor_copy(out=A_[:, 0, 0:129], in_=ps[6][:, 0:129]), reads=[dps[6]], writes=[dA])
            K.op(dve, lambda e: e.tensor_copy(out=A_[:, 1, 0:129], in_=ps[7][:, 0:129]), reads=[dps[7]], writes=[dA])
            f = fin[qi]
            df = d_fin[qi]
            K.op(dve, lambda e: e.reciprocal(out=f[:, 256:258], in_=A_[:, :, 128]), reads=[dA], writes=[df])
            K.op(dve, lambda e: e.tensor_tensor(out=f[:, 257:258], in0=f[:, 257:258], in1=neglam, op=ALU.mult),
                 reads=[df, d_lam], writes=[df])
            K.op(dve, lambda e: e.tensor_scalar(out=f[:, 0:128], in0=A_[:, 0, 0:128], scalar1=f[:, 256:257], scalar2=None,
                                                op0=ALU.mult), reads=[dA, df], writes=[df])
            K.op(dve, lambda e: e.scalar_tensor_tensor(out=f[:, 128:256], in0=A_[:, 1, 0:128], scalar=f[:, 257:258],
                                                       in1=f[:, 0:128], op0=ALU.mult, op1=ALU.add),
                 reads=[dA, df], writes=[df])
            K.op(dve, lambda e: e.tensor_tensor(out=f[:, 0:128], in0=f[:, 128:256], in1=f[:, 128:256], op=ALU.mult),
                 reads=[df], writes=[df])
            K.op(dve, lambda e: e.reduce_sum(out=f[:, 258:259], in_=f[:, 0:128], axis=AX.X), reads=[df], writes=[df])
            K.op(act, lambda e: e.activation(out=f[:, 259:260], in_=f[:, 258:259], func=AF.Ln, scale=1.0 / 128,
                                             bias=eps_t[:, 0:1]), reads=[df, d_eps], writes=[df])
            K.op(act, lambda e: e.activation(out=f[:, 259:260], in_=f[:, 259:260], func=AF.Exp, scale=-0.5),
                 reads=[df], writes=[df])
            K.op(dve, lambda e: e.scalar_tensor_tensor(out=f[:, 0:128], in0=f[:, 128:256], scalar=f[:, 259:260],
                                                       in1=L["subw"][:], op0=ALU.mult, op1=ALU.mult),
                 reads=[df, d_subw], writes=[df])
            K.op(dve, lambda e: e.tensor_tensor(out=fb[qi][:], in0=f[:, 0:128], in1=SAG[b][:, qt, :], op=ALU.mult),
                 reads=[df, d_in[b]], writes=[d_fb[qi]])
            ptv = ps[0][:].bitcast(BF16)
            K.op(pe, lambda e: e.transpose(out=ptv[:, 0:128], in_=fb[qi][:], identity=ident_b[:]),
                 reads=[d_fb[qi], d_ident], writes=[dps[0]])
            K.op(dve, lambda e: e.tensor_copy(out=ATT[b][:, qt * 128:(qt + 1) * 128], in_=ptv[:, 0:128]),
                 reads=[dps[0]], writes=[d_att[b]])

        base = cnt["g"]
        s_done = [False] * (len(groups) + 1)
        for i, (qt, g0, nk) in enumerate(groups):
            if not s_done[i]:
                emit_S(base + i, qt, g0, nk)
                s_done[i] = True
            last_of_qt = (g0 + nk - 1 == qt)
            will_yield = last_of_qt and qt == 10
            if i + 1 < len(groups) and not will_yield:
                emit_S(base + i + 1, *groups[i + 1])
                s_done[i + 1] = True
            emit_rest(base + i, qt, g0, nk)
            if last_of_qt:
                finalize(qt)
                if will_yield:
                    yield
        cnt["g"] += len(groups)
        K.dma(sp, lambda e: e.dma_start(out=s_atT[h * 128:(h + 1) * 128, :], in_=ATT[b][:]),
              reads=[d_att[b]], writes=[d_atT])
        yield


def _phase_rwkv(nc, K, L):
    pe, dve, act, pool, sp = K.pe, K.dve, K.act, K.pool, K.sp
    ps, dps = L["ps"], L["dps"]
    ident_b, ident_f, d_ident = L["ident_b"], L["ident_f"], L["d_ident"]
    eps_t, d_eps = L["eps_t"], L["d_eps"]
    s_prw, d_prw, s_rwT, d_rwT = L["s_prw"], L["d_prw"], L["s_rwT"], L["d_rwT"]
    o_wkvp, d_owkvp = L["o_wkvp"], L["d_owkvp"]
    ph = contextlib.ExitStack()
    K.cur = ph
    H3 = lambda ap: ap.rearrange("p (h c) -> p h c", c=64)

    def TT(eng, out, in0, in1, op, reads, writes):
        K.op(eng, lambda e: e.tensor_tensor(out=out, in0=in0, in1=in1, op=op), reads=reads, writes=writes)

    cst = K.sb("rw_cst", [128, 8, 1024], F32)
    d_cst = Dep()
    for i in range(8):
        K.dma(sp, lambda e: e.dma_start(out=cst[:, i, :], in_=L["rwp_d"][i:i + 1, :].partition_broadcast(128)),
              writes=[d_cst])
    w0b, a0b, kkb, kab, omka, lnxw, lnxb, rkb = (cst[:, i, :] for i in range(8))
    K.op(dve, lambda e: e.tensor_scalar(out=omka, in0=omka, scalar1=-1.0, scalar2=1.0, op0=ALU.mult, op1=ALU.add),
         reads=[d_cst], writes=[d_cst])
    mub = K.sb("rw_mub", [128, RW_IN], F32)
    K.dma(sp, lambda e: e.dma_start(out=mub[:], in_=L["mu_d"][0:1, :].partition_broadcast(128)), writes=[d_cst])
    w2a2 = K.sb("rw_w2a2", [96, 2, 1024], F32)
    K.dma(sp, lambda e: e.dma_start(out=w2a2[:, 0, :], in_=L["w2_d"][:, :]), writes=[d_cst])
    K.dma(sp, lambda e: e.dma_start(out=w2a2[:, 1, :], in_=L["a2_d"][:, :]), writes=[d_cst])
    mask4 = K.sb("rw_mask4", [128, 512], BF16)
    maskT = K.sb("rw_maskT", [128, 128], BF16)
    umat = K.sb("rw_umat", [128, 128], F32)
    ones2 = K.sb("rw_ones2", [128, 2], F32)
    K.op(dve, lambda e: e.memset(ones2[:], 1.0), writes=[d_cst])

    pc = K.sb("rw_pc", [128, RW_IN], F32); d_pc = Dep()
    pp = K.sb("rw_pp", [128, RW_IN], F32); d_pp = Dep(); d_ppb = Dep()
    lw = K.sb("rw_lw", [128, 192], F32); d_lw = Dep()
    lwT = K.sb("rw_lwT", [96, 256], F32); d_lwT = Dep()
    logd = K.sb("rw_logd", [128, 1024], F32); d_logd = Dep()
    aa = K.sb("rw_aa", [128, 1024], F32); d_aa = Dep()
    kk = K.sb("rw_kk", [128, 1024], F32); d_kk = Dep()
    kp = K.sb("rw_kp", [128, 1024], F32); d_kp = Dep()
    bb = K.sb("rw_bb", [128, 1024], F32); d_bb = Dep()
    bon = K.sb("rw_bon", [128, 1024], F32); d_bon = Dep()
    st16 = K.sb("rw_st16", [128, 64], F32); d_st16 = Dep()
    pcv = K.sb("rw_pcv", [128, 16], F32); d_pcv = Dep()
    t1, t2, eneg, epos = (pc[:, i * 1024:(i + 1) * 1024] for i in range(4))
    ys = K.sb("rw_ys", [128, 1024], F32); d_ys = Dep()
    rwo = K.sb("rw_rwo", [128, 1024], BF16); d_rwo = Dep()
    rwF = K.sb("rw_rwF", [128, 8, 128], BF16); d_rwF = Dep()
    ph2 = contextlib.ExitStack()
    K.cur = ph2
    mk_f = K.sb("rw_mkf", [128, 768], F32)
    K.dma(sp, lambda e: e.dma_start(out=mk_f[:, 0:512], in_=L["mask4_d"][:, :]), writes=[d_cst])
    K.dma(sp, lambda e: e.dma_start(out=mk_f[:, 512:640], in_=L["maskT_d"][:, :]), writes=[d_cst])
    K.dma(sp, lambda e: e.dma_start(out=umat[:], in_=L["umat_d"][:, :]), writes=[d_cst])
    K.op(dve, lambda e: e.tensor_copy(out=mask4[:], in_=mk_f[:, 0:512]), reads=[d_cst], writes=[d_cst])
    K.op(dve, lambda e: e.tensor_copy(out=maskT[:], in_=mk_f[:, 512:640]), reads=[d_cst], writes=[d_cst])
    khT = K.sb("rw_khT", [128, 1024], BF16)
    nbhT = K.sb("rw_nbhT", [128, 1024], BF16)
    kktT = K.sb("rw_kktT", [128, 1024], BF16)
    rtT = K.sb("rw_rtT", [128, 1024], BF16)
    vT = K.sb("rw_vT", [128, 1024], BF16)
    d_tm = Dep()
    khF = K.sb("rw_khF", [128, 8, 128], BF16)
    nbhF = K.sb("rw_nbhF", [128, 8, 128], BF16)
    qrF = K.sb("rw_qrF", [128, 8, 2, 128], BF16)
    d_fm = Dep()
    MM = [K.sb("rw_MM%d" % h, [128, 512], BF16) for h in range(16)]
    NN = [K.sb("rw_NN%d" % h, [128, 256], BF16) for h in range(16)]
    XX = [K.sb("rw_XX%d" % h, [128, 128], BF16) for h in range(16)]
    d_MM = [Dep() for _ in range(16)]
    d_NN = [Dep() for _ in range(16)]
    d_XX = [Dep() for _ in range(16)]
    RT = K.sb("rw_RT", [128, 1024], BF16); d_RT = [Dep(), Dep()]
    WT = K.sb("rw_WT", [128, 1024], BF16); d_WT = [Dep(), Dep()]
    ST = K.sb("rw_ST", [128, 8, 64], F32)
    STs = K.sb("rw_STs", [128, 8, 64], F32)
    STb = K.sb("rw_STb", [128, 8, 64], BF16)
    d_ST = Dep(); d_STs = Dep(); d_STb = Dep()
    sto = K.sb("rw_sto", [64, 8, 128], F32); d_sto = Dep()
    K.cur = ph
    K.op(dve, lambda e: e.memset(ST[:], 0.0), writes=[d_ST])
    K.op(dve, lambda e: e.memset(STb[:], 0.0), writes=[d_STb])

    cnt = {"ps": 0, "ev": 0}

    def next_ps():
        i = 2 + cnt["ps"] % 6
        cnt["ps"] += 1
        return i

    def ev_eng():
        cnt["ev"] += 1
        return act if cnt["ev"] % 4 != 0 else dve

    def copy_op(eng, out, in_, reads, writes):
        if eng is act:
            K.op(act, lambda e: e.copy(out=out, in_=in_), reads=reads, writes=writes)
        else:
            K.op(eng, lambda e: e.tensor_copy(out=out, in_=in_), reads=reads, writes=writes)

    r_, k_, v_, g_ = (pp[:, i * 1024:(i + 1) * 1024] for i in range(4))
    def stepA(ci, is_s):
        r0 = 0 if is_s else ci * 128
        if is_s:
            K.op(pool, lambda e: e.memset(pc[:], 0.0), writes=[d_pc])
            K.op(pool, lambda e: e.memset(pp[:], 0.0), writes=[d_pp, d_ppb])
            K.dma(sp, lambda e: e.dma_start(out=pc[:NS_OWN, :], in_=L["s_prws"][:, :]), reads=[L["d_prws"]], writes=[d_pc])
            K.dma(sp, lambda e: e.dma_start(out=pp[1:NS_OWN, :], in_=L["s_prws"][0:NS_OWN - 1, :]), reads=[L["d_prws"]],
                  writes=[d_pp, d_ppb])
            for sq_ in range(16):
                K.dma(sp, lambda e: e.dma_start(out=pp[4 * sq_:4 * sq_ + 1, :], in_=L["shift_d"][sq_:sq_ + 1, :]),
                      writes=[d_pp, d_ppb])
        elif ci == 0:
            K.dma(sp, lambda e: e.dma_start(out=pc[:], in_=s_prw[r0:r0 + 128, :]), reads=[d_prw], writes=[d_pc])
            K.op(pool, lambda e: e.memset(pp[0:1, :], 0.0), writes=[d_pp, d_ppb])
            K.dma(sp, lambda e: e.dma_start(out=pp[1:128, :], in_=s_prw[0:127, :]), reads=[d_prw], writes=[d_pp, d_ppb])
        else:
            K.dma(sp, lambda e: e.dma_start(out=pc[:], in_=s_prw[r0:r0 + 128, :]), reads=[d_prw], writes=[d_pc])
            K.dma(sp, lambda e: e.dma_start(out=pp[:], in_=s_prw[r0 - 1:r0 + 127, :]), reads=[d_prw], writes=[d_pp, d_ppb])
        ca, cb_ = slice(0, 3072), slice(3072, RW_IN)
        for eng_, cs_, dd_ in ((dve, ca, d_pp), (pool, cb_, d_ppb)):
            TT(eng_, pp[:, cs_], pp[:, cs_], pc[:, cs_], ALU.subtract, [dd_, d_pc], [dd_])
            TT(eng_, pp[:, cs_], pp[:, cs_], mub[:, cs_], ALU.mult, [dd_, d_cst], [dd_])
            TT(eng_, pp[:, cs_], pp[:, cs_], pc[:, cs_], ALU.add, [dd_, d_pc], [dd_])
        K.op(act, lambda e: e.activation(out=lw[:, 0:96], in_=pp[:, 4096:4192], func=AF.Tanh), reads=[d_ppb], writes=[d_lw])
        K.op(dve, lambda e: e.tensor_copy(out=lw[:, 96:192], in_=pp[:, 4192:4288]), reads=[d_ppb], writes=[d_lw])
        for j in range(2):
            K.op(pe, lambda e: e.transpose(out=ps[0][:96, j * 128:(j + 1) * 128], in_=lw[:, j * 96:(j + 1) * 96],
                                           identity=ident_f[:]), reads=[d_lw, d_ident], writes=[dps[0]])
        K.op(dve, lambda e: e.tensor_copy(out=lwT[:, :], in_=ps[0][:96, 0:256]), reads=[dps[0]], writes=[d_lwT])
        for which, dst, dd, cb in ((0, logd, d_logd, w0b), (1, aa, d_aa, a0b)):
            for hf in range(2):
                pi = next_ps()
                K.op(pe, lambda e: e.matmul(ps[pi][:, :], lhsT=lwT[:, which * 128:(which + 1) * 128],
                                            rhs=w2a2[:, which, hf * 512:(hf + 1) * 512], start=True, stop=True),
                     reads=[d_lwT, d_cst], writes=[dps[pi]])
                TT(dve, dst[:, hf * 512:(hf + 1) * 512], ps[pi][:, :], cb[:, hf * 512:(hf + 1) * 512], ALU.add,
                   [dps[pi], d_cst], [dd])
            K.op(act, lambda e: e.activation(out=dst[:], in_=dst[:], func=AF.Sigmoid), reads=[dd], writes=[dd])
        K.op(pool, lambda e: e.tensor_scalar(out=logd[:], in0=logd[:], scalar1=-0.6065306597126334, scalar2=None,
                                             op0=ALU.mult), reads=[d_logd], writes=[d_logd])
        TT(pool, kk[:], k_, kkb, ALU.mult, [d_pp, d_cst], [d_kk])
        TT(pool, t1, kk[:], kk[:], ALU.mult, [d_kk], [d_pc])
        K.op(dve, lambda e: e.reduce_sum(out=st16[:, 0:16], in_=H3(t1), axis=AX.X), reads=[d_pc], writes=[d_st16])
        K.op(act, lambda e: e.activation(out=st16[:, 0:16], in_=st16[:, 0:16], func=AF.Sqrt), reads=[d_st16], writes=[d_st16])
        K.op(dve, lambda e: e.tensor_scalar(out=st16[:, 0:16], in0=st16[:, 0:16], scalar1=1e-12, scalar2=None,
                                            op0=ALU.max), reads=[d_st16], writes=[d_st16])
        K.op(dve, lambda e: e.reciprocal(out=st16[:, 0:16], in_=st16[:, 0:16]), reads=[d_st16], writes=[d_st16])
        TT(dve, H3(kk[:]), H3(kk[:]), st16[:, 0:16].unsqueeze(2).to_broadcast([128, 16, 64]), ALU.mult,
           [d_kk, d_st16], [d_kk])
        TT(pool, t1, aa[:], kab, ALU.mult, [d_aa, d_cst], [d_pc])
        TT(pool, t1, t1, omka, ALU.add, [d_pc, d_cst], [d_pc])
        TT(dve, kp[:], k_, t1, ALU.mult, [d_pp, d_pc], [d_kp])
        TT(pool, bb[:], kk[:], aa[:], ALU.mult, [d_kk, d_aa], [d_bb])
        TT(dve, t2, r_, kp[:], ALU.mult, [d_pp, d_kp], [d_pc])
        TT(pool, t2, t2, rkb, ALU.mult, [d_pc, d_cst], [d_pc])
        K.op(dve, lambda e: e.reduce_sum(out=st16[:, 16:32], in_=H3(t2), axis=AX.X), reads=[d_pc], writes=[d_st16])
        TT(dve, H3(bon[:]), H3(v_), st16[:, 16:32].unsqueeze(2).to_broadcast([128, 16, 64]), ALU.mult,
           [d_pp, d_st16], [d_bon])
    def scan(ci):
        r0 = ci * 128
        cps = []
        for hf in range(2):
            pi = next_ps()
            cps.append(pi)
            K.op(pe, lambda e: e.matmul(ps[pi][:, :], lhsT=umat[:], rhs=logd[:, hf * 512:(hf + 1) * 512],
                                        start=True, stop=True), reads=[d_logd, d_cst], writes=[dps[pi]])
        for hf in range(2):
            pi = cps[hf]
            sl = slice(hf * 512, (hf + 1) * 512)
            K.op(act, lambda e: e.activation(out=eneg[:, sl], in_=ps[pi][:, :], func=AF.Exp, scale=-1.0),
                 reads=[dps[pi]], writes=[d_pc])
            K.op(act, lambda e: e.activation(out=epos[:, sl], in_=ps[pi][:, :], func=AF.Exp), reads=[dps[pi]], writes=[d_pc])
            TT(dve, t1[:, sl], ps[pi][:, :], logd[:, sl], ALU.subtract, [dps[pi], d_logd], [d_pc])
        K.op(act, lambda e: e.activation(out=t1, in_=t1, func=AF.Exp), reads=[d_pc], writes=[d_pc])
        pi = next_ps()
        for g in range(8):
            K.op(pe, lambda e: e.matmul(ps[pi][:, g * 2:g * 2 + 2], lhsT=logd[:, g * 128:(g + 1) * 128], rhs=ones2[:],
                                        start=True, stop=True), reads=[d_logd, d_cst], writes=[dps[pi]])
        K.op(act, lambda e: e.activation(out=pcv[:, 0:16], in_=ps[pi][:, 0:16], func=AF.Exp), reads=[dps[pi]], writes=[d_pcv])
        TT(dve, khT[:], kp[:], eneg, ALU.mult, [d_kp, d_pc], [d_tm])
        K.op(dve, lambda e: e.scalar_tensor_tensor(out=nbhT[:], in0=bb[:], scalar=-1.0, in1=eneg, op0=ALU.mult,
                                                   op1=ALU.mult), reads=[d_bb, d_pc], writes=[d_tm])
        TT(pool, kktT[:], kk[:], t1, ALU.mult, [d_kk, d_pc], [d_tm])
        TT(pool, rtT[:], r_, epos, ALU.mult, [d_pp, d_pc], [d_tm])
        K.op(act, lambda e: e.copy(out=vT[:], in_=v_), reads=[d_pp], writes=[d_tm])
        for src, dstv in ((khT, khF[:, :, :]), (nbhT, nbhF[:, :, :]), (kktT, qrF[:, :, 0, :]), (rtT, qrF[:, :, 1, :])):
            tb = cnt["ps"] % 2
            cnt["ps"] += 1
            ptv = ps[tb][:].bitcast(BF16)
            for g in range(8):
                K.op(pe, lambda e: e.transpose(out=ptv[:, g * 128:(g + 1) * 128], in_=src[:, g * 128:(g + 1) * 128],
                                               identity=ident_b[:]), reads=[d_tm, d_ident], writes=[dps[tb]])
            copy_op(ev_eng(), dstv, ptv[:, :].rearrange("p (g t) -> p g t", g=8), [dps[tb]], [d_fm])
        for h in range(16):
            g = h // 2
            prt = slice((h % 2) * 64, (h % 2) * 64 + 64)
            pi = next_ps()
            K.op(pe, lambda e: e.matmul(ps[pi][:, 0:256], lhsT=khF[prt, g, :], rhs=qrF[prt, g, :, :],
                                        start=True, stop=True), reads=[d_fm], writes=[dps[pi]])
            K.op(pe, lambda e: e.matmul(ps[pi][:, 256:512], lhsT=nbhF[prt, g, :], rhs=qrF[prt, g, :, :],
                                        start=True, stop=True), reads=[d_fm], writes=[dps[pi]])
            TT(dve, MM[h][:], ps[pi][:, :], mask4[:], ALU.mult, [dps[pi], d_cst], [d_MM[h]])
            pi = next_ps()
            K.op(pe, lambda e: e.matmul(ps[pi][:, 0:128], lhsT=qrF[prt, g, 0, :], rhs=nbhF[prt, g, :],
                                        start=True, stop=True), reads=[d_fm], writes=[dps[pi]])
            TT(dve, NN[h][:, 128:256], ps[pi][:, 0:128], maskT[:], ALU.mult, [dps[pi], d_cst], [d_NN[h]])
            TT(pool, XX[h][:], ident_b[:], MM[h][:, 256:384], ALU.subtract, [d_ident, d_MM[h]], [d_XX[h]])
        for lvl in range(1, 7):
            for h in range(16):
                nsrc = MM[h][:, 256:384] if lvl == 1 else NN[h][:, 0:128]
                ntsrc = NN[h][:, 128:256]
                rd = [d_MM[h], d_NN[h]]
                pi = next_ps()
                if lvl < 6:
                    K.op(pe, lambda e: e.matmul(ps[pi][:, 0:128], lhsT=ntsrc, rhs=nsrc, start=True, stop=True),
                         reads=rd, writes=[dps[pi]])
                K.op(pe, lambda e: e.matmul(ps[pi][:, 128:256], lhsT=nsrc, rhs=ntsrc, start=True, stop=True),
                     reads=rd, writes=[dps[pi]])
                if lvl < 6:
                    copy_op(ev_eng(), NN[h][:, 0:256], ps[pi][:, 0:256], [dps[pi]], [d_NN[h]])
                else:
                    copy_op(ev_eng(), NN[h][:, 128:256], ps[pi][:, 128:256], [dps[pi]], [d_NN[h]])
            for h in range(16):
                pi = next_ps()
                K.op(pe, lambda e: e.matmul(ps[pi][:, 0:128], lhsT=NN[h][:, 128:256], rhs=XX[h][:], start=True, stop=True),
                     reads=[d_NN[h], d_XX[h]], writes=[dps[pi]])
                TT(dve, XX[h][:], ps[pi][:, 0:128], XX[h][:], ALU.add, [dps[pi], d_XX[h]], [d_XX[h]])
        TT(pool, STs[:], ST[:], pcv[:, 0:16].rearrange("p (g two) -> p g two", two=2)[:, :, 0:1].to_broadcast([128, 8, 64]),
           ALU.mult, [d_ST, d_pcv], [d_STs])
        for half in range(2):
            pi = next_ps()
            for hh in range(8):
                h = half * 8 + hh
                g = h // 2
                prt = slice((h % 2) * 64, (h % 2) * 64 + 64)
                K.op(pe, lambda e: e.matmul(ps[pi][:, hh * 64:(hh + 1) * 64], lhsT=qrF[prt, g, 0, :], rhs=STb[prt, g, :],
                                            start=True, stop=False), reads=[d_fm, d_STb], writes=[dps[pi]])
                K.op(pe, lambda e: e.matmul(ps[pi][:, hh * 64:(hh + 1) * 64], lhsT=MM[h][:, 0:128],
                                            rhs=vT[:, h * 64:(h + 1) * 64], start=False, stop=True),
                     reads=[d_MM[h], d_tm], writes=[dps[pi]])
            copy_op(ev_eng(), RT[:, half * 512:(half + 1) * 512], ps[pi][:, :], [dps[pi]], [d_RT[half]])
        for half in range(2):
            pi = next_ps()
            for hh in range(8):
                h = half * 8 + hh
                K.op(pe, lambda e: e.matmul(ps[pi][:, hh * 64:(hh + 1) * 64], lhsT=XX[h][:], rhs=RT[:, h * 64:(h + 1) * 64],
                                            start=True, stop=True), reads=[d_XX[h], d_RT[half]], writes=[dps[pi]])
            copy_op(ev_eng(), WT[:, half * 512:(half + 1) * 512], ps[pi][:, :], [dps[pi]], [d_WT[half]])
        for half in range(2):
            pi = next_ps()
            for hh in range(8):
                h = half * 8 + hh
                g = h // 2
                prt = slice((h % 2) * 64, (h % 2) * 64 + 64)
                o = ps[pi][:, hh * 64:(hh + 1) * 64]
                K.op(pe, lambda e: e.matmul(o, lhsT=qrF[prt, g, 1, :], rhs=STb[prt, g, :], start=True, stop=False),
                     reads=[d_fm, d_STb], writes=[dps[pi]])
                K.op(pe, lambda e: e.matmul(o, lhsT=MM[h][:, 128:256], rhs=vT[:, h * 64:(h + 1) * 64], start=False, stop=False),
                     reads=[d_MM[h], d_tm], writes=[dps[pi]])
                K.op(pe, lambda e: e.matmul(o, lhsT=MM[h][:, 384:512], rhs=WT[:, h * 64:(h + 1) * 64], start=False, stop=True),
                     reads=[d_MM[h], d_WT[half]], writes=[dps[pi]])
            copy_op(ev_eng(), ys[:, half * 512:(half + 1) * 512], ps[pi][:, :], [dps[pi]], [d_ys])
        for half in range(2):
            pi = next_ps()
            for gg in range(4):
                g = half * 4 + gg
                o = ps[pi][:, gg * 128:(gg + 1) * 128]
                K.op(pe, lambda e: e.matmul(o, lhsT=khT[:, g * 128:(g + 1) * 128], rhs=vT[:, g * 128:(g + 1) * 128],
                                            start=True, stop=False), reads=[d_tm], writes=[dps[pi]])
                K.op(pe, lambda e: e.matmul(o, lhsT=nbhT[:, g * 128:(g + 1) * 128], rhs=WT[:, g * 128:(g + 1) * 128],
                                            start=False, stop=True), reads=[d_tm, d_WT[0], d_WT[1]], writes=[dps[pi]])
            psv = ps[pi][:, :].rearrange("p (g hl v) -> p g hl v", g=4, hl=2)
            for hl in range(2):
                prt = slice(hl * 64, hl * 64 + 64)
                pcb = pcv[prt, 0:16].rearrange("p (g two) -> p g two", two=2)[:, half * 4:half * 4 + 4, 0:1].to_broadcast([64, 4, 64])
                TT(dve, ST[prt, half * 4:half * 4 + 4, :], psv[prt, :, hl, :], pcb, ALU.mult, [dps[pi], d_pcv], [d_ST])
                TT(dve, ST[prt, half * 4:half * 4 + 4, :], ST[prt, half * 4:half * 4 + 4, :],
                   STs[prt, half * 4:half * 4 + 4, :], ALU.add, [d_ST, d_STs], [d_ST])
        K.op(act, lambda e: e.copy(out=STb[:], in_=ST[:]), reads=[d_ST], writes=[d_STb])
    def stepD(ci, is_s):
        r0 = 0 if is_s else ci * 128
        K.op(dve, lambda e: e.reduce_sum(out=st16[:, 32:48], in_=H3(ys[:]), axis=AX.X), reads=[d_ys], writes=[d_st16])
        K.op(dve, lambda e: e.tensor_scalar(out=st16[:, 32:48], in0=st16[:, 32:48], scalar1=1.0 / 64, scalar2=None,
                                            op0=ALU.mult), reads=[d_st16], writes=[d_st16])
        TT(dve, H3(ys[:]), H3(ys[:]), st16[:, 32:48].unsqueeze(2).to_broadcast([128, 16, 64]), ALU.subtract,
           [d_ys, d_st16], [d_ys])
        TT(pool, t2, ys[:], ys[:], ALU.mult, [d_ys], [d_pc])
        K.op(dve, lambda e: e.reduce_sum(out=st16[:, 48:64], in_=H3(t2), axis=AX.X), reads=[d_pc], writes=[d_st16])
        K.op(act, lambda e: e.activation(out=st16[:, 48:64], in_=st16[:, 48:64], func=AF.Sqrt, scale=1.0 / 64,
                                         bias=eps_t[:, 1:2]), reads=[d_st16, d_eps], writes=[d_st16])
        K.op(dve, lambda e: e.reciprocal(out=st16[:, 48:64], in_=st16[:, 48:64]), reads=[d_st16], writes=[d_st16])
        TT(dve, H3(ys[:]), H3(ys[:]), st16[:, 48:64].unsqueeze(2).to_broadcast([128, 16, 64]), ALU.mult,
           [d_ys, d_st16], [d_ys])
        TT(pool, ys[:], ys[:], lnxw, ALU.mult, [d_ys, d_cst], [d_ys])
        TT(pool, ys[:], ys[:], lnxb, ALU.add, [d_ys, d_cst], [d_ys])
        TT(dve, ys[:], ys[:], bon[:], ALU.add, [d_ys, d_bon], [d_ys])
        K.op(act, lambda e: e.activation(out=t2, in_=g_, func=AF.Silu), reads=[d_ppb], writes=[d_pc])
        TT(dve, rwo[:], ys[:], t2, ALU.mult, [d_ys, d_pc], [d_rwo])
        tb = cnt["ps"] % 2
        cnt["ps"] += 1
        ptv = ps[tb][:].bitcast(BF16)
        for g in range(8):
            K.op(pe, lambda e: e.transpose(out=ptv[:, g * 128:(g + 1) * 128], in_=rwo[:, g * 128:(g + 1) * 128],
                                           identity=ident_b[:]), reads=[d_rwo, d_ident], writes=[dps[tb]])
        copy_op(ev_eng(), rwF[:, :, :], ptv[:, :].rearrange("p (g t) -> p g t", g=8), [dps[tb]], [d_rwF])
        if is_s:
            K.dma(sp, lambda e: e.dma_start(out=L["s_rwTs"].rearrange("(g p) t -> p g t", p=128), in_=rwF[:, :, 0:NS_OWN]),
                  reads=[d_rwF], writes=[L["d_rwTs"]])
        else:
            K.dma(sp, lambda e: e.dma_start(out=s_rwT.rearrange("(g p) t -> p g t", p=128)[:, :, r0:r0 + 128],
                                            in_=rwF[:, :, :]), reads=[d_rwF], writes=[d_rwT])

    for ci in range(NT):
        stepA(ci, False)
        scan(ci)
        stepD(ci, False)
    for half in range(2):
        pi = next_ps()
        for gg in range(4):
            g = half * 4 + gg
            K.op(pe, lambda e: e.transpose(out=ps[pi][:64, gg * 128:(gg + 1) * 128], in_=ST[:, g, :], identity=ident_f[:]),
                 reads=[d_ST, d_ident], writes=[dps[pi]])
        K.op(dve, lambda e: e.tensor_copy(out=sto[:, half * 4:half * 4 + 4, :],
                                          in_=ps[pi][:64, :].rearrange("p (g x) -> p g x", g=4)),
             reads=[dps[pi]], writes=[d_sto])
    K.dma(sp, lambda e: e.dma_start(out=o_wkvp.rearrange("(g hl) v c -> v g hl c", hl=2),
                                    in_=sto[:, :, :].rearrange("p g (hl c) -> p g hl c", hl=2)),
          reads=[d_sto], writes=[d_owkvp])
    K.barrier()
    ph2.close()
    ph3 = contextlib.ExitStack()
    K.cur = ph3
    s_rs, s_ysm = L["s_rs"], L["s_ysm"]
    d_rs, d_ysm = Dep(), Dep()
    SS = [K.sb("rs_SS%d" % i, [128, 4096], F32) for i in range(2)]
    TM_ = [K.sb("rs_TM%d" % i, [128, 4096], F32) for i in range(2)]
    VV = [K.sb("rs_VV%d" % i, [128, 2, 6, 64], F32) for i in range(2)]
    YY = [K.sb("rs_YY%d" % i, [128, 4, 64], F32) for i in range(2)]
    SK = [K.sb("rs_SK%d" % i, [128, 64], F32) for i in range(2)]
    d_SS = [Dep(), Dep()]; d_TM = [Dep(), Dep()]; d_VV = [Dep(), Dep()]; d_YY = [Dep(), Dep()]; d_SK = [Dep(), Dep()]
    stepA("s", True)
    K.op(act, lambda e: e.activation(out=t1, in_=logd[:], func=AF.Exp), reads=[d_logd], writes=[d_pc])
    for qi, (src, dd) in enumerate(((kk[:NS_OWN, :], d_kk), (bb[:NS_OWN, :], d_bb), (kp[:NS_OWN, :], d_kp),
                                    (t1[:NS_OWN, :], d_pc), (r_[:NS_OWN, :], d_pp), (v_[:NS_OWN, :], d_pp))):
        K.dma(sp, lambda e: e.dma_start(out=s_rs[:, qi, :], in_=src), reads=[dd], writes=[d_rs])
    wkv_v = L["wkv_d"].rearrange("s h v k -> (s h) (v k)")
    owkv_v = L["o_wkvs"].rearrange("s h v k -> (s h) (v k)")
    for grp in range(2):
        K.dma(sp, lambda e: e.dma_start(out=SS[grp][:], in_=wkv_v[grp * 128:(grp + 1) * 128, :]), writes=[d_SS[grp]])
    for t in range(4):
        for grp in range(2):
            eng = dve if grp == 0 else pool
            for sl in range(8):
                sq_ = grp * 8 + sl
                K.dma(sp, lambda e: e.dma_start(out=VV[grp][sl * 16:(sl + 1) * 16, t % 2, :, :],
                                                in_=s_rs[sq_ * 4 + t, :, :].rearrange("q (h c) -> h q c", c=64)),
                      reads=[d_rs], writes=[d_VV[grp]])
            S3 = SS[grp][:].rearrange("p (v k) -> p v k", k=64)
            T3 = TM_[grp][:].rearrange("p (v k) -> p v k", k=64)
            bk = lambda q_: VV[grp][:, t % 2, q_, :].unsqueeze(1).to_broadcast([128, 64, 64])
            bv = lambda ap: ap.unsqueeze(2).to_broadcast([128, 64, 64])
            dS, dT, dV, dY, dK = d_SS[grp], d_TM[grp], d_VV[grp], d_YY[grp], d_SK[grp]
            TT(eng, T3, S3, bk(0), ALU.mult, [dS, dV], [dT])
            K.op(dve, lambda e: e.reduce_sum(out=SK[grp][:], in_=T3, axis=AX.X), reads=[dT], writes=[dK])
            TT(eng, S3, S3, bk(3), ALU.mult, [dS, dV], [dS])
            TT(eng, T3, bv(SK[grp][:]), bk(1), ALU.mult, [dK, dV], [dT])
            TT(eng, S3, S3, T3, ALU.subtract, [dS, dT], [dS])
            TT(eng, T3, bv(VV[grp][:, t % 2, 5, :]), bk(2), ALU.mult, [dV], [dT])
            TT(eng, S3, S3, T3, ALU.add, [dS, dT], [dS])
            TT(eng, T3, S3, bk(4), ALU.mult, [dS, dV], [dT])
            K.op(dve, lambda e: e.reduce_sum(out=YY[grp][:, t, :], in_=T3, axis=AX.X), reads=[dT], writes=[dY])
    for grp in range(2):
        K.dma(sp, lambda e: e.dma_start(out=owkv_v[grp * 128:(grp + 1) * 128, :], in_=SS[grp][:]), reads=[d_SS[grp]],
              writes=[L["d_owkvs"]])
        for sl in range(8):
            sq_ = grp * 8 + sl
            K.dma(sp, lambda e: e.dma_start(out=s_ysm[sq_ * 4:(sq_ + 1) * 4, :].rearrange("t (h v) -> h t v", v=64),
                                            in_=YY[grp][sl * 16:(sl + 1) * 16, :, :]), reads=[d_YY[grp]], writes=[d_ysm])
    K.op(dve, lambda e: e.memset(ys[:], 0.0), writes=[d_ys])
    K.dma(sp, lambda e: e.dma_start(out=ys[:NS_OWN, :], in_=s_ysm[:, :]), reads=[d_ysm], writes=[d_ys])
    stepD("s", True)
    K.barrier()
    ph3.close()
    ph.close()
    K.cur = None


def _phase_final(nc, K, L):
    pe, dve, act, pool, sp = K.pe, K.dve, K.act, K.pool, K.sp
    ps, dps = L["ps"], L["dps"]
    eps_t, d_eps = L["eps_t"], L["d_eps"]
    ph = contextlib.ExitStack()
    K.cur = ph
    d_cst = Dep()
    nfb = K.sb("fn_nfb", [128, D], F32)
    K.dma(sp, lambda e: e.dma_start(out=nfb[:], in_=L["normf_d"][0:1, :].partition_broadcast(128)), writes=[d_cst])
    wo = K.sb("fn_wo", [128, KC, D], BF16)
    d_wo = Dep()
    wst = [K.sb("fn_wst%d" % i, [128, KC, 128], F32) for i in range(2)]
    d_wst = [Dep(), Dep()]
    wbr = [K.sb("fn_wbr%d" % i, [128, 2, 8, 128], BF16) for i in range(2)]
    d_wbr = [Dep(), Dep()]
    wo_v = L["w_out_d"].rearrange("(kc p) n -> p kc n", p=128)
    for i in range(16):
        b = i % 2
        K.dma(sp, lambda e: e.dma_start(out=wst[b][:, :, :], in_=wo_v[:, :, i * 128:(i + 1) * 128]), writes=[d_wst[b]])
        K.op(pool, lambda e: e.tensor_copy(out=wo[:, :, i * 128:(i + 1) * 128], in_=wst[b][:, :, :]),
             reads=[d_wst[b]], writes=[d_wo])
    wr_v = L["w_brr_d"].rearrange("(kc p) n -> p kc n", p=128)
    wa_v = L["w_bra_d"].rearrange("(kc p) n -> p kc n", p=128)
    inT = [K.sb("fn_inT%d" % i, [128, 2, 8, 512], BF16) for i in range(1)]
    d_inT = [Dep()]
    sgt = [K.sb("fn_sg%d" % i, [128, 2, 512], BF16) for i in range(3)]
    d_sgt = [Dep() for _ in range(3)]
    mT = K.sb("fn_mT", [128, KC, 512], BF16)
    d_mT = Dep()
    m12 = [K.sb("fn_m%d" % i, [128, 2, 512], F32) for i in range(2)]
    d_m12 = [Dep(), Dep()]
    xt = [K.sb("fn_xt%d" % i, [128, D], F32) for i in range(2)]
    d_xt = [Dep(), Dep()]
    sq = K.sb("fn_sq", [128, D], BF16)
    d_sq = Dep()
    stat = [K.sb("fn_stat%d" % i, [128, 2], F32) for i in range(2)]
    d_stat = [Dep(), Dep()]
    cnt = {"w": 0, "sg": 0, "m": 0, "x": 0, "br": 0}

    groups = [("p", tg, 512) for tg in range(4)] + [("s", 0, NS_OWN)]
    import os
    if os.environ.get("SKIP_S"):
        groups = groups[:4]
    for kind, tg, n in groups:
        if kind == "p":
            at_src = L["s_atT"].rearrange("(kc p) t -> p kc t", p=128)[:, :, tg * 512:(tg + 1) * 512]
            rw_src = L["s_rwT"].rearrange("(kc p) t -> p kc t", p=128)[:, :, tg * 512:(tg + 1) * 512]
            rd = [L["d_atT"], L["d_rwT"]]
            sg_src = L["s_sg"].rearrange("(a c p) t -> p a c t", a=2, p=128)[:, :, :, tg * 512:(tg + 1) * 512]
            d_sgsrc = L["d_sg"]
        else:
            at_src = L["s_atTs"].rearrange("(kc p) t -> p kc t", p=128)
            rw_src = L["s_rwTs"].rearrange("(kc p) t -> p kc t", p=128)
            rd = [L["d_atTs"], L["d_rwTs"]]
            sg_src = L["s_sgs"].rearrange("(a c p) t -> p a c t", a=2, p=128)
            d_sgsrc = L["d_sgs"]
        K.dma(sp, lambda e: e.dma_start(out=inT[0][:, 0, :, :n], in_=rw_src), reads=rd, writes=[d_inT[0]])
        K.dma(sp, lambda e: e.dma_start(out=inT[0][:, 1, :, :n], in_=at_src), reads=rd, writes=[d_inT[0]])
        for cc in range(KC):
            wb_i = cnt["w"] % 2
            cnt["w"] += 1
            K.dma(sp, lambda e: e.dma_start(out=wst[wb_i][:, 0:8, :], in_=wr_v[:, :, cc * 128:(cc + 1) * 128]),
                  writes=[d_wst[wb_i]])
            K.dma(sp, lambda e: e.dma_start(out=wst[wb_i][:, 8:16, :], in_=wa_v[:, :, cc * 128:(cc + 1) * 128]),
                  writes=[d_wst[wb_i]])
            K.op(pool, lambda e: e.tensor_copy(out=wbr[wb_i][:, :, :, :].rearrange("p a k n -> p (a k) n"),
                                               in_=wst[wb_i][:, :, :]), reads=[d_wst[wb_i]], writes=[d_wbr[wb_i]])
            si = cnt["sg"] % 3
            cnt["sg"] += 1
            K.dma(sp, lambda e: e.dma_start(out=sgt[si][:, :, :n], in_=sg_src[:, :, cc, :]), reads=[d_sgsrc],
                  writes=[d_sgt[si]])
            banks = [4 + 2 * (cnt["br"] % 2), 5 + 2 * (cnt["br"] % 2)]
            cnt["br"] += 1
            for a in range(2):
                for kc in range(8):
                    K.op(pe, lambda e: e.matmul(ps[banks[a]][:, :n], lhsT=wbr[wb_i][:, a, kc, :], rhs=inT[0][:, a, kc, :n],
                                                start=(kc == 0), stop=(kc == 7)),
                         reads=[d_wbr[wb_i], d_inT[0]], writes=[dps[banks[a]]])
            mi = cnt["m"] % 2
            cnt["m"] += 1
            for a in range(2):
                K.op(dve, lambda e: e.tensor_tensor(out=m12[mi][:, a, :n], in0=ps[banks[a]][:, :n], in1=sgt[si][:, a, :n],
                                                    op=ALU.mult), reads=[dps[banks[a]], d_sgt[si]], writes=[d_m12[mi]])
            K.op(dve, lambda e: e.tensor_tensor(out=mT[:, cc, :n], in0=m12[mi][:, 0, :n], in1=m12[mi][:, 1, :n],
                                                op=ALU.add), reads=[d_m12[mi]], writes=[d_mT])
        ntile = 4 if kind == "p" else 1
        for tt in range(ntile):
            m = 128 if kind == "p" else NS_OWN
            xi = cnt["x"] % 2
            cnt["x"] += 1
            if kind == "p":
                rows = slice(tg * 512 + tt * 128, tg * 512 + (tt + 1) * 128)
                xsrc, ydst, dy = L["xp"][rows, :], L["o_yp"][rows, :], L["d_oyp"]
            else:
                xsrc, ydst, dy = L["xs_own"][:, :], L["o_ys"][:, :], L["d_oys"]
            K.dma(sp, lambda e: e.dma_start(out=xt[xi][:m, :], in_=xsrc), writes=[d_xt[xi]])
            for cg in range(4):
                for kc in range(KC):
                    K.op(pe, lambda e: e.matmul(ps[cg][:m, :], lhsT=mT[:, kc, tt * 128:tt * 128 + m],
                                                rhs=wo[:, kc, cg * 512:(cg + 1) * 512], start=(kc == 0), stop=(kc == KC - 1)),
                         reads=[d_mT, d_wo], writes=[dps[cg]])
                K.op(dve, lambda e: e.tensor_tensor(out=xt[xi][:m, cg * 512:(cg + 1) * 512], in0=ps[cg][:m, :],
                                                    in1=xt[xi][:m, cg * 512:(cg + 1) * 512], op=ALU.add),
                     reads=[dps[cg], d_xt[xi]], writes=[d_xt[xi]])
            K.op(act, lambda e: e.activation(out=sq[:m, :], in_=xt[xi][:m, :], func=AF.Square,
                                             accum_out=stat[xi][:m, 0:1]), reads=[d_xt[xi]], writes=[d_sq, d_stat[xi]])
            K.op(act, lambda e: e.activation(out=stat[xi][:m, 1:2], in_=stat[xi][:m, 0:1], func=AF.Sqrt, scale=1.0 / D,
                                             bias=eps_t[:m, 0:1]), reads=[d_stat[xi], d_eps], writes=[d_stat[xi]])
            K.op(dve, lambda e: e.reciprocal(out=stat[xi][:m, 1:2], in_=stat[xi][:m, 1:2]), reads=[d_stat[xi]],
                 writes=[d_stat[xi]])
            K.op(dve, lambda e: e.scalar_tensor_tensor(out=xt[xi][:m, :], in0=xt[xi][:m, :], scalar=stat[xi][:m, 1:2],
                                                       in1=nfb[:m, :], op0=ALU.mult, op1=ALU.mult),
                 reads=[d_xt[xi], d_stat[xi], d_cst], writes=[d_xt[xi]])
            K.dma(sp, lambda e: e.dma_start(out=ydst, in_=xt[xi][:m, :]), reads=[d_xt[xi]], writes=[dy])
    K.barrier()
    ph.close()
    K.cur = None


def _phase_sattn(nc, K, L):
    pe, dve, act, pool, sp = K.pe, K.dve, K.act, K.pool, K.sp
    ps, dps = L["ps"], L["dps"]
    ident_b, d_ident = L["ident_b"], L["d_ident"]
    eps_t, d_eps = L["eps_t"], L["d_eps"]
    neglam, d_lam, subw, d_subw = L["neglam"], L["d_lam"], L["subw"], L["d_subw"]
    ck, cv = L["ck_d"], L["cv_d"]
    ph = contextlib.ExitStack()
    K.cur = ph
    d_c = Dep()
    pti = K.sb("sa_pti", [128, 256], I32)
    ptf = K.sb("sa_ptf", [128, 256], F32)
    iot = K.sb("sa_iot", [128, 1], F32)
    idx = K.sb("sa_idx", [128, 256], I32)
    K.dma(sp, lambda e: e.dma_start(out=pti[:], in_=L["pt_d"][0:1, :].partition_broadcast(128)), writes=[d_c])
    K.dma(sp, lambda e: e.dma_start(out=iot[:], in_=L["iota_d"][:, :]), writes=[d_c])
    K.op(dve, lambda e: e.tensor_copy(out=ptf[:], in_=pti[:]), reads=[d_c], writes=[d_c])
    K.op(dve, lambda e: e.tensor_scalar(out=ptf[:], in0=ptf[:], scalar1=128.0, scalar2=iot[:, 0:1], op0=ALU.mult,
                                        op1=ALU.add), reads=[d_c], writes=[d_c])
    K.op(dve, lambda e: e.tensor_copy(out=idx[:], in_=ptf[:]), reads=[d_c], writes=[d_c])
    QS = K.sb("sa_QS", [128, 8, 64], BF16)
    QB = K.sb("sa_QB", [128, 8, 16, 8], BF16)
    KN = K.sb("sa_KN", [128, 8, 64], BF16)
    VN = [K.sb("sa_VN%d" % i, [4, 8, 130], BF16) for i in range(2)]
    d_VN = [Dep(), Dep()]
    smk = K.sb("sa_smk", [4, 8], F32)
    smb = K.sb("sa_smb", [4, 8], BF16)
    self_ = K.sb("sa_sel", [8, 2, 16, 64], F32)
    WS = K.sb("sa_WS", [8, 16, 64], BF16)
    ON = K.sb("sa_ON", [8, 16, 8, 128], BF16)
    d_ON = Dep()
    K.dma(sp, lambda e: e.dma_start(out=QS[:], in_=L["s_qTs"].rearrange("(h p) t -> p h t", p=128)), reads=[L["d_qTs"]],
          writes=[d_c])
    K.dma(sp, lambda e: e.dma_start(out=KN[:], in_=L["s_kTs"].rearrange("(h p) t -> p h t", p=128)), reads=[L["d_kTs"]],
          writes=[d_c])
    K.op(dve, lambda e: e.memset(QB[:], 0.0), writes=[d_c])
    for c in range(2):
        prt = slice(c * 64, (c + 1) * 64)
        K.op(dve, lambda e: e.tensor_copy(out=QB[prt, :, :, c * 4:(c + 1) * 4],
                                          in_=QS[prt, :, :].rearrange("p h (s q) -> p h s q", q=4)),
             reads=[d_c], writes=[d_c])
    for i in range(2):
        K.op(dve, lambda e: e.memset(VN[i][:], 1.0), writes=[d_VN[i]])
    K.dma(sp, lambda e: e.dma_start(out=smk[:], in_=L["smask_d"][:, :]), writes=[d_c])
    K.op(dve, lambda e: e.tensor_copy(out=smb[:], in_=smk[:]), reads=[d_c], writes=[d_c])
    K.dma(sp, lambda e: e.dma_start(out=self_[:], in_=L["sel_d"][:, :, :, :]), writes=[d_c])
    K.op(dve, lambda e: e.scalar_tensor_tensor(out=WS[:], in0=self_[:, 1, :, :], scalar=neglam[0:8, :], in1=self_[:, 0, :, :],
                                               op0=ALU.mult, op1=ALU.add), reads=[d_c, d_lam], writes=[d_c])
    KP = [K.sb("sa_KP%d" % i, [128, 1024], BF16) for i in range(6)]
    d_KP = [Dep() for _ in range(6)]
    VG = [K.sb("sa_VG%d" % i, [128, 1024], BF16) for i in range(6)]
    d_VG = [Dep() for _ in range(6)]
    VP = [K.sb("sa_VP%d" % i, [128, 8, 8, 130], BF16) for i in range(2)]
    d_VP = [Dep(), Dep()]
    KTt = [K.sb("sa_KT%d" % i, [128, 8, 128], BF16) for i in range(2)]
    d_KT = [Dep(), Dep()]
    PT = [K.sb("sa_PT%d" % i, [128, 8, 64], BF16) for i in range(2)]
    d_PT = [Dep(), Dep()]
    PN = K.sb("sa_PN", [4, 64], BF16)
    d_PN = Dep()
    rd = K.sb("sa_rd", [8, 8], F32)
    d_rd = Dep()
    for i in range(2):
        K.op(dve, lambda e: e.memset(VP[i][:], 1.0), writes=[d_VP[i]])
    osb = K.sb("sa_osb", [64, 1024], F32); d_osb = Dep()
    osq = K.sb("sa_osq", [64, 1024], F32); d_osq = Dep()
    sst = K.sb("sa_sst", [64, 16], F32); d_sst = Dep()
    sag = K.sb("sa_sag", [64, 1024], BF16); d_sag = Dep()
    ofb = K.sb("sa_ofb", [64, 1024], BF16); d_ofb = Dep()
    atF = K.sb("sa_atF", [128, 8, 64], BF16); d_atF = Dep()
    K.cur = None
    L["_ph_sattn"] = ph
    yield
    cnt = {"kp": 0, "kt": 0, "half": 0}
    acc_banks = [4, 5, 6]
    hb = lambda h: (acc_banks[h // 3], (h % 3) * 129)
    for s_ in range(16):
        K.dma(sp, lambda e: e.dma_start(out=VN[s_ % 2][:, :, 0:128],
                                        in_=L["s_vs"][s_ * 4:(s_ + 1) * 4, :].rearrange("t (h e) -> t h e", h=8)),
              reads=[L["d_vs"]], writes=[d_VN[s_ % 2]])
        for bk_ in acc_banks:
            K.op(dve, lambda e: e.memset(ps[bk_][0:8, :], 0.0), writes=[dps[bk_]])
        for hf in range(2):
            hi = cnt["half"] % 2
            cnt["half"] += 1
            sbank = 2 + hi
            for jj in range(8):
                col = s_ * 16 + hf * 8 + jj
                ki = cnt["kp"] % 6
                cnt["kp"] += 1
                K.dma(pool, lambda e: e.indirect_dma_start(
                    out=KP[ki][:, :], out_offset=None, in_=ck[:, :],
                    in_offset=bass.IndirectOffsetOnAxis(ap=idx[:, col:col + 1], axis=0)), reads=[d_c], writes=[d_KP[ki]])
                K.dma(pool, lambda e: e.indirect_dma_start(
                    out=VG[ki][:, :], out_offset=None, in_=cv[:, :],
                    in_offset=bass.IndirectOffsetOnAxis(ap=idx[:, col:col + 1], axis=0)), reads=[d_c], writes=[d_VG[ki]])
                if True:
                    K.op(dve, lambda e: e.tensor_copy(out=VP[hi][:, jj, :, 0:128],
                                                      in_=VG[ki][:, :].rearrange("p (h e) -> p h e", h=8)),
                         reads=[d_VG[ki]], writes=[d_VP[hi]])
                else:
                    K.op(act, lambda e: e.copy(out=VP[hi][:, jj, :, 0:128],
                                               in_=VG[ki][:, :].rearrange("p (h e) -> p h e", h=8)),
                         reads=[d_VG[ki]], writes=[d_VP[hi]])
                tb = cnt["kt"] % 2
                cnt["kt"] += 1
                ptv = ps[tb][:].bitcast(BF16)
                for h in range(8):
                    K.op(pe, lambda e: e.transpose(out=ptv[:, h * 128:(h + 1) * 128], in_=KP[ki][:, h * 128:(h + 1) * 128],
                                                   identity=ident_b[:]), reads=[d_KP[ki], d_ident], writes=[dps[tb]])
                if False:
                    pass
                else:
                    K.op(dve, lambda e: e.tensor_copy(out=KTt[tb][:, :, :], in_=ptv[:, :].rearrange("p (h t) -> p h t", h=8)),
                         reads=[dps[tb]], writes=[d_KT[tb]])
                for h in range(8):
                    K.op(pe, lambda e: e.matmul(ps[sbank][:, jj * 64 + h * 8:jj * 64 + h * 8 + 8], lhsT=KTt[tb][:, h, :],
                                                rhs=QB[:, h, s_, :], start=True, stop=True),
                         reads=[d_KT[tb], d_c], writes=[dps[sbank]])
            K.op(act, lambda e: e.activation(out=PT[hi][:, :, :], in_=ps[sbank][:, :].rearrange("p (j x) -> p j x", j=8),
                                             func=AF.Exp), reads=[dps[sbank]], writes=[d_PT[hi]])
            for h in range(8):
                bk_, off = hb(h)
                for jj in range(8):
                    K.op(pe, lambda e: e.matmul(ps[bk_][0:8, off:off + 129], lhsT=PT[hi][:, jj, h * 8:(h + 1) * 8],
                                                rhs=VP[hi][:, jj, h, 0:129], start=False, stop=False, skip_group_check=True),
                         reads=[d_PT[hi], d_VP[hi]], writes=[dps[bk_]])
        for h in range(8):
            K.op(pe, lambda e: e.matmul(ps[7][0:4, h * 8:(h + 1) * 8], lhsT=KN[:, h, s_ * 4:(s_ + 1) * 4], rhs=QB[:, h, s_, :],
                                        start=True, stop=True), reads=[d_c], writes=[dps[7]])
        K.op(act, lambda e: e.activation(out=PN[:, :], in_=ps[7][0:4, 0:64], func=AF.Exp), reads=[dps[7]], writes=[d_PN])
        K.op(dve, lambda e: e.tensor_tensor(out=PN[:, :].rearrange("p (h x) -> p h x", h=8),
                                            in0=PN[:, :].rearrange("p (h x) -> p h x", h=8),
                                            in1=smb[:, :].unsqueeze(1).to_broadcast([4, 8, 8]), op=ALU.mult),
             reads=[d_PN, d_c], writes=[d_PN])
        for h in range(8):
            bk_, off = hb(h)
            K.op(pe, lambda e: e.matmul(ps[bk_][0:8, off:off + 129], lhsT=PN[0:4, h * 8:(h + 1) * 8], rhs=VN[s_ % 2][0:4, h, 0:129],
                                        start=False, stop=True, skip_group_check=True),
                 reads=[d_PN, d_VN[s_ % 2]], writes=[dps[bk_]])
        for bi_, bk_ in enumerate(acc_banks):
            nh = 3 if bi_ < 2 else 2
            v3 = ps[bk_][0:8, 0:nh * 129].rearrange("p (h e) -> p h e", e=129)
            K.op(dve, lambda e: e.reciprocal(out=rd[:, bi_ * 3:bi_ * 3 + nh].unsqueeze(2), in_=v3[:, :, 128:129]),
                 reads=[dps[bk_]], writes=[d_rd])
            K.op(dve, lambda e: e.tensor_tensor(out=ON[:, s_, bi_ * 3:bi_ * 3 + nh, :], in0=v3[:, :, 0:128],
                                                in1=rd[:, bi_ * 3:bi_ * 3 + nh].unsqueeze(2).to_broadcast([8, nh, 128]),
                                                op=ALU.mult), reads=[dps[bk_], d_rd], writes=[d_ON])
        yield
    K.dma(sp, lambda e: e.dma_start(out=sag[:], in_=L["s_ags"][:, :]), reads=[L["d_ags"]], writes=[d_sag])
    for hc in range(2):
        for s_ in range(16):
            K.op(pe, lambda e: e.matmul(ps[2 + hc][0:64, :], lhsT=WS[0:8, s_, :],
                                        rhs=ON[0:8, s_, hc * 4:(hc + 1) * 4, :], start=(s_ == 0), stop=(s_ == 15)),
                 reads=[d_ON, d_c], writes=[dps[2 + hc]])
        K.op(dve, lambda e: e.tensor_copy(out=osb[:, hc * 512:(hc + 1) * 512], in_=ps[2 + hc][0:64, :]),
             reads=[dps[2 + hc]], writes=[d_osb])
    H8 = lambda ap: ap.rearrange("p (h e) -> p h e", h=8)
    K.op(dve, lambda e: e.tensor_tensor(out=osq[:], in0=osb[:], in1=osb[:], op=ALU.mult), reads=[d_osb], writes=[d_osq])
    K.op(dve, lambda e: e.reduce_sum(out=sst[:, 0:8], in_=H8(osq[:]), axis=AX.X), reads=[d_osq], writes=[d_sst])
    K.op(act, lambda e: e.activation(out=sst[:, 0:8], in_=sst[:, 0:8], func=AF.Sqrt, scale=1.0 / 128, bias=eps_t[:64, 0:1]),
         reads=[d_sst, d_eps], writes=[d_sst])
    K.op(dve, lambda e: e.reciprocal(out=sst[:, 0:8], in_=sst[:, 0:8]), reads=[d_sst], writes=[d_sst])
    K.op(dve, lambda e: e.tensor_tensor(out=H8(osb[:]), in0=H8(osb[:]), in1=sst[:, 0:8].unsqueeze(2).to_broadcast([64, 8, 128]),
                                        op=ALU.mult), reads=[d_osb, d_sst], writes=[d_osb])
    K.op(dve, lambda e: e.tensor_tensor(out=H8(osb[:]), in0=H8(osb[:]), in1=subw[:64, :].unsqueeze(1).to_broadcast([64, 8, 128]),
                                        op=ALU.mult), reads=[d_osb, d_subw], writes=[d_osb])
    K.op(dve, lambda e: e.tensor_tensor(out=ofb[:], in0=osb[:], in1=sag[:], op=ALU.mult), reads=[d_osb, d_sag], writes=[d_ofb])
    ptv = ps[0][:].bitcast(BF16)
    for h in range(8):
        K.op(pe, lambda e: e.transpose(out=ptv[:, h * 64:(h + 1) * 64], in_=ofb[:, h * 128:(h + 1) * 128],
                                       identity=ident_b[:64, :64]), reads=[d_ofb, d_ident], writes=[dps[0]])
    K.op(dve, lambda e: e.tensor_copy(out=atF[:, :, :], in_=ptv[:, 0:512].rearrange("p (h t) -> p h t", h=8)),
         reads=[dps[0]], writes=[d_atF])
    K.dma(sp, lambda e: e.dma_start(out=L["s_atTs"].rearrange("(h p) t -> p h t", p=128), in_=atF[:, :, :]),
          reads=[d_atF], writes=[L["d_atTs"]])


_SU = np.triu(np.ones((128, 128), np.float32), 1)
_IU = np.triu(np.ones((128, 128), np.float32), 0)
_MASK4 = np.ascontiguousarray(np.concatenate([_SU, _IU, -_SU, _IU], axis=1))
_MASKT = np.ascontiguousarray(-_SU.T)
_UMAT = _IU

_SMASK = np.zeros((4, 8), np.float32)
_SEL = np.zeros((8, 2, 16, 64), np.float32)
for _c in range(2):
    for _q in range(4):
        for _t in range(4):
            if _t <= _q:
                _SMASK[_t, _c * 4 + _q] = 1.0
        for _s in range(16):
            _SEL[_c * 4 + _q, _c, _s, _s * 4 + _q] = 1.0

_NC_CACHE = {}


def _prep_inputs(inp, cores):
    ident = np.eye(128, dtype=np.float32)
    xs_all = np.ascontiguousarray(inp["x_sample"].reshape(NS_ALL, D))
    w_in = np.ascontiguousarray(inp["w_in"][0])
    ck_flat = inp["cache_k"].reshape(2560 * 128, 1024)
    cv_flat = inp["cache_v"].reshape(2560 * 128, 1024)
    maps = []
    for c in cores:
        m = {
            "xp": np.ascontiguousarray(inp["x_prompt"][c]),
            "xs_own": np.ascontiguousarray(xs_all[c * 64:(c + 1) * 64]),
            "w_in": w_in,
            "norm_in": np.ascontiguousarray(inp["norm_in"]),
            "ident": ident,
            "trimask": np.triu(np.ones((128, 128), np.float32)),
            "lam4": np.concatenate([inp["lambda_q1"][0], inp["lambda_k1"][0], inp["lambda_q2"][0],
                                    inp["lambda_k2"][0]]).reshape(1, 256).astype(np.float32),
            "subln": np.ascontiguousarray(inp["subln_w"]).reshape(1, 128),
            "mu": np.ascontiguousarray(inp["mu_shift"]).reshape(1, RW_IN),
            "shift_own": np.ascontiguousarray(inp["state_shift"][0, c * 16:(c + 1) * 16]),
            "wkv_own": np.ascontiguousarray(inp["state_wkv"][0, c * 16:(c + 1) * 16]),
            "rwp": np.stack([inp["w0"][0], inp["a0"][0], inp["k_k"][0], inp["k_a"][0], inp["k_a"][0],
                             inp["lnx_w"][0], inp["lnx_b"][0], inp["r_k"][0].reshape(1024)]).astype(np.float32),
            "w2": np.ascontiguousarray(inp["w2"][0]),
            "a2": np.ascontiguousarray(inp["a2"][0]),
            "normf": np.ascontiguousarray(inp["norm_f"]).reshape(1, D),
            "w_out": np.ascontiguousarray(inp["w_out"][0]),
            "w_brr": np.ascontiguousarray(inp["w_br_rwkv"][0]),
            "w_bra": np.ascontiguousarray(inp["w_br_attn"][0]),
            "mask4": _MASK4,
            "ck": ck_flat,
            "cv": cv_flat,
            "pt_own": np.ascontiguousarray(inp["page_table"][c * 16:(c + 1) * 16]).reshape(1, 256).astype(np.int32),
            "iota": np.arange(128, dtype=np.float32).reshape(128, 1),
            "smask": _SMASK,
            "sel": _SEL,
            "maskT": _MASKT,
            "umat": _UMAT,
        }
        maps.append(m)
    return maps


def kernel(**inp):
    cores = list(range(NCORES))
    if "nc" not in _NC_CACHE:
        _NC_CACHE["nc"] = build()
    nc = _NC_CACHE["nc"]
    maps = _prep_inputs(inp, cores)
    res = run_bass_kernel_spmd(nc, maps, core_ids=cores).results
    f = np.float32
    y_p = np.stack([res[c]["o_yp"] for c in cores]).astype(f)
    y_s = np.concatenate([res[c]["o_ys"] for c in cores]).reshape(128, 4, D).astype(f)
    kp = np.stack([res[c]["o_kp"] for c in cores]).reshape(1, 8, T, 8, 2, 64).astype(f)
    vp = np.stack([res[c]["o_vp"] for c in cores]).reshape(1, 8, T, 8, 128).astype(f)
    ks = np.concatenate([res[c]["o_ks"] for c in cores]).reshape(1, 128, 4, 8, 2, 64).astype(f)
    vs = np.concatenate([res[c]["o_vs"] for c in cores]).reshape(1, 128, 4, 8, 128).astype(f)
    wp = np.stack([res[c]["o_wkvp"] for c in cores]).reshape(1, 8, 16, 64, 64).astype(f)
    ws = np.concatenate([res[c]["o_wkvs"] for c in cores]).reshape(1, 128, 16, 64, 64).astype(f)
    shp = np.stack([res[c]["o_shp"][0] for c in cores]).reshape(1, 8, RW_IN).astype(f)
    shs = np.concatenate([res[c]["o_shs"] for c in cores]).reshape(1, 128, RW_IN).astype(f)
    return (y_p, y_s, kp, vp, ks, vs, wp, ws, shp, shs)
```
